# Optimizing a Trainium2 kernel written in Bass

```python
import math
import jax, jax.numpy as jnp
from jax import lax
import numpy as np

D_MODEL = 2048
BATCH = 1
SEQ = 8192
DEPTH = 1

PLE_DIM = 256
EPS = 1e-6
SSM_WIDTH = D_MODEL // 2
SSM_GROUP = 16
SSM_GROUPS = SSM_WIDTH // SSM_GROUP
SSM_STATE = 64
DT_MIN = 1e-3
DT_MAX = 1e-1
N_HEADS = 8
N_KV_HEADS = 2
HEAD_DIM = 128
ATTN_WIDTH = N_HEADS * HEAD_DIM
Q_LORA_RANK = 512
IDX_HEADS = 16
IDX_DIM = 64
TOPK_MAX = 256
Q_BLOCK = 128
IN_SIZES = (SSM_WIDTH, SSM_WIDTH, Q_LORA_RANK, N_KV_HEADS * HEAD_DIM, N_KV_HEADS * HEAD_DIM,
            ATTN_WIDTH, IDX_DIM, IDX_HEADS, D_MODEL, D_MODEL)
IN_WIDTH = sum(IN_SIZES)

kernel_name = "hybrid_s5_dsa_gated_block"


def rms_norm(x, g):
    x32 = x.astype(jnp.float32)
    y = x32 * lax.rsqrt(jnp.mean(x32 * x32, axis=-1, keepdims=True) + EPS)
    return y.astype(x.dtype) * g


def split_points():
    pts, acc = [], 0
    for s in IN_SIZES[:-1]:
        acc += s
        pts.append(acc)
    return pts


def s5_branch(u, a_re, a_im, log_dt, b_re, b_im, c_re, c_im, d_skip, w_glu):
    bsz, seq, _ = u.shape
    f32 = jnp.float32
    uf = u.astype(f32).reshape(bsz, seq, SSM_GROUPS, SSM_GROUP)
    dt = jnp.exp(log_dt.astype(f32))[:, None]
    ar, ai = a_re.astype(f32), a_im.astype(f32)
    mag = jnp.exp(dt * ar)
    abar_re = mag * jnp.cos(dt * ai)
    abar_im = mag * jnp.sin(dt * ai)
    den = ar * ar + ai * ai
    nr = abar_re - 1.0
    f_re = (nr * ar + abar_im * ai) / den
    f_im = (abar_im * ar - nr * ai) / den
    br, bi = b_re.astype(f32), b_im.astype(f32)
    bb_re = f_re[..., None] * br - f_im[..., None] * bi
    bb_im = f_re[..., None] * bi + f_im[..., None] * br
    bu_re = jnp.einsum('bsgc,gnc->bsgn', uf, bb_re)
    bu_im = jnp.einsum('bsgc,gnc->bsgn', uf, bb_im)
    at_re = jnp.broadcast_to(abar_re, bu_re.shape)
    at_im = jnp.broadcast_to(abar_im, bu_im.shape)

    def combine(e1, e2):
        a1r, a1i, x1r, x1i = e1
        a2r, a2i, x2r, x2i = e2
        return (a1r * a2r - a1i * a2i,
                a1r * a2i + a1i * a2r,
                a2r * x1r - a2i * x1i + x2r,
                a2r * x1i + a2i * x1r + x2i)

    _, _, h_re, h_im = lax.associative_scan(combine, (at_re, at_im, bu_re, bu_im), axis=1)
    y = (jnp.einsum('bsgn,gcn->bsgc', h_re, c_re.astype(f32))
         - jnp.einsum('bsgn,gcn->bsgc', h_im, c_im.astype(f32))
         + d_skip.astype(f32).reshape(SSM_GROUPS, SSM_GROUP) * uf)
    y = jax.nn.gelu(y.reshape(bsz, seq, SSM_WIDTH))
    a, b = jnp.split(y @ w_glu.astype(f32), 2, axis=-1)
    return (a * jax.nn.sigmoid(b)).astype(u.dtype)


def dsa_branch(q, k, v, q_idx, k_idx, w_idx):
    bsz, seq = q.shape[0], q.shape[1]
    n_keys = k.shape[1]
    topk = min(TOPK_MAX, n_keys // 4)
    n_blocks = seq // Q_BLOCK
    rep = N_HEADS // N_KV_HEADS
    key_pos = jnp.arange(n_keys)
    gather = jax.vmap(lambda arr, ii: arr[ii])

    def block(j):
        start = j * Q_BLOCK
        qb = lax.dynamic_slice_in_dim(q, start, Q_BLOCK, axis=1)
        qib = lax.dynamic_slice_in_dim(q_idx, start, Q_BLOCK, axis=1)
        wb = lax.dynamic_slice_in_dim(w_idx, start, Q_BLOCK, axis=1)
        t = start + jnp.arange(Q_BLOCK)
        raw = jax.nn.relu(jnp.einsum('bqhd,bsd->bqhs', qib, k_idx))
        score = jnp.einsum('bqhs,bqh->bqs', raw, wb).astype(jnp.float32)
        causal = key_pos[None, :] <= t[:, None]
        score = jnp.where(causal[None], score, -jnp.inf)
        vals, idx = lax.top_k(score, topk)
        valid = jnp.isfinite(vals)
        ks = gather(k, idx)
        vs = gather(v, idx)
        qg = qb.reshape(bsz, Q_BLOCK, N_KV_HEADS, rep, HEAD_DIM)
        logits = jnp.einsum('bqgrd,bqngd->bqgrn', qg, ks).astype(jnp.float32) * (HEAD_DIM ** -0.5)
        logits = jnp.where(valid[:, :, None, None, :], logits, -jnp.inf)
        probs = jax.nn.softmax(logits, axis=-1).astype(v.dtype)
        o = jnp.einsum('bqgrn,bqngd->bqgrd', probs, vs)
        return o.reshape(bsz, Q_BLOCK, ATTN_WIDTH)

    out = lax.map(block, jnp.arange(n_blocks))
    return out.transpose(1, 0, 2, 3).reshape(bsz, seq, ATTN_WIDTH)


def setup_inputs(seed: int = 0) -> dict:
    key = jax.random.key(seed)
    ks = jax.random.split(key, 32)
    f32 = jnp.float32

    def nrm(k, shape, scale):
        return jax.random.normal(k, shape, f32) * scale

    def gain(k, shape):
        return 1.0 + 0.02 * jax.random.normal(k, shape, f32)

    L = DEPTH
    n = jnp.arange(SSM_STATE, dtype=f32)
    a_re = -0.5 + 0.01 * jax.random.normal(ks[8], (L, SSM_GROUPS, SSM_STATE), f32)
    a_im = math.pi * n + 0.01 * jax.random.normal(ks[9], (L, SSM_GROUPS, SSM_STATE), f32)
    log_dt = jax.random.uniform(ks[10], (L, SSM_GROUPS), f32, math.log(DT_MIN), math.log(DT_MAX))
    b_scale = (2.0 * SSM_GROUP) ** -0.5
    c_scale = (2.0 * SSM_STATE) ** -0.5
    return {
        "x": jax.random.normal(ks[0], (BATCH, SEQ, D_MODEL), f32),
        "p": jax.random.normal(ks[1], (DEPTH, BATCH, SEQ, PLE_DIM), f32),
        "g_mix": gain(ks[2], (L, D_MODEL)),
        "w_in": nrm(ks[3], (L, D_MODEL, IN_WIDTH), D_MODEL ** -0.5),
        "g_q": gain(ks[4], (L, Q_LORA_RANK)),
        "w_uq": nrm(ks[5], (L, Q_LORA_RANK, ATTN_WIDTH), Q_LORA_RANK ** -0.5),
        "w_uq_idx": nrm(ks[6], (L, Q_LORA_RANK, IDX_HEADS * IDX_DIM), Q_LORA_RANK ** -0.5),
        "g_kidx": gain(ks[7], (L, IDX_DIM)),
        "a_re": a_re,
        "a_im": a_im,
        "log_dt": log_dt,
        "b_re": nrm(ks[11], (L, SSM_GROUPS, SSM_STATE, SSM_GROUP), b_scale),
        "b_im": nrm(ks[12], (L, SSM_GROUPS, SSM_STATE, SSM_GROUP), b_scale),
        "c_re": nrm(ks[13], (L, SSM_GROUPS, SSM_GROUP, SSM_STATE), c_scale),
        "c_im": nrm(ks[14], (L, SSM_GROUPS, SSM_GROUP, SSM_STATE), c_scale),
        "d_skip": nrm(ks[15], (L, SSM_WIDTH), 1.0),
        "w_glu": nrm(ks[16], (L, SSM_WIDTH, 2 * SSM_WIDTH), SSM_WIDTH ** -0.5),
        "w_ssm_out": nrm(ks[17], (L, SSM_WIDTH, D_MODEL), SSM_WIDTH ** -0.5),
        "w_attn_out": nrm(ks[18], (L, ATTN_WIDTH, D_MODEL), ATTN_WIDTH ** -0.5),
        "w_o": nrm(ks[19], (L, D_MODEL, D_MODEL), D_MODEL ** -0.5),
        "g_ple": gain(ks[20], (L, D_MODEL)),
        "w_ple_gate": nrm(ks[21], (L, D_MODEL, D_MODEL), D_MODEL ** -0.5),
        "w_ple": nrm(ks[22], (L, PLE_DIM, D_MODEL), PLE_DIM ** -0.5),
        "g_ple_post": gain(ks[23], (L, D_MODEL)),
        "g_final": gain(ks[24], (D_MODEL,)),
    }


def reference(x, p, g_mix, w_in, g_q, w_uq, w_uq_idx, g_kidx, a_re, a_im, log_dt, b_re, b_im,
              c_re, c_im, d_skip, w_glu, w_ssm_out, w_attn_out, w_o, g_ple, w_ple_gate, w_ple,
              g_ple_post, g_final):
    bsz, seq, _ = x.shape
    pts = split_points()
    for i in range(DEPTH):
        h = rms_norm(x, g_mix[i])
        proj = h @ w_in[i]
        (u, z_ssm, c_q, k, v, z_attn, k_idx, w_idx, gate_ssm, gate_attn) = jnp.split(proj, pts, axis=-1)

        y_ssm = s5_branch(u, a_re[i], a_im[i], log_dt[i], b_re[i], b_im[i], c_re[i], c_im[i],
                          d_skip[i], w_glu[i]) * jax.nn.silu(z_ssm)
        o_ssm = y_ssm @ w_ssm_out[i]

        cq = rms_norm(c_q, g_q[i])
        q = (cq @ w_uq[i]).reshape(bsz, seq, N_HEADS, HEAD_DIM)
        q_idx = (cq @ w_uq_idx[i]).reshape(bsz, seq, IDX_HEADS, IDX_DIM)
        k_idx = rms_norm(k_idx, g_kidx[i])
        w_idx = w_idx * ((IDX_HEADS ** -0.5) * (IDX_DIM ** -0.5))
        k = k.reshape(bsz, seq, N_KV_HEADS, HEAD_DIM)
        v = v.reshape(bsz, seq, N_KV_HEADS, HEAD_DIM)
        y_attn = dsa_branch(q, k, v, q_idx, k_idx, w_idx) * jax.nn.silu(z_attn)
        o_attn = y_attn @ w_attn_out[i]

        merged = jax.nn.sigmoid(gate_ssm) * o_ssm + jax.nn.sigmoid(gate_attn) * o_attn
        x = x + merged @ w_o[i]

        e = rms_norm(p[i] @ w_ple[i], g_ple_post[i])
        gate = jax.nn.sigmoid(rms_norm(x, g_ple[i]) @ w_ple_gate[i])
        x = x + gate * e
    return rms_norm(x, g_final)
```

```python
import math
import numpy as np
from contextlib import ExitStack
import concourse.bass as bass
import concourse.mybir as mybir
from concourse.bass_utils import run_bass_kernel_spmd

F32 = mybir.dt.float32
BF16 = mybir.dt.bfloat16
AF = mybir.ActivationFunctionType
ALU = mybir.AluOpType
AX = mybir.AxisListType

NCORE = 8
TPC = 8
NT = 1024
D = 2048
INW = 8272
EPS = 1e-6
TOPK = 256
NBIS = 18
PI = math.pi


class Sched:
    def __init__(self, nc, es, ndma=28):
        self.nc = nc
        self.engs = {'pe': nc.tensor, 'act': nc.scalar, 'dve': nc.vector, 'pool': nc.gpsimd, 'sp': nc.sync}
        self.semh = []
        self.esem = {}
        for e in ['pe', 'act', 'dve', 'pool']:
            self.esem[e] = len(self.semh)
            self.semh.append(es.enter_context(nc.semaphore("s_" + e)))
        self.cnt = {e: 0 for e in self.engs}
        self.seen = {e: {} for e in self.engs}
        self.last_w = {}
        self.readers = {}
        self.dsem = []
        for i in range(ndma):
            self.dsem.append(len(self.semh))
            self.semh.append(es.enter_context(nc.semaphore("s_dma%d" % i)))
        self.duse = [0] * ndma
        self.di = 0

    def _deps(self, reads, writes):
        deps = []
        for k in reads:
            lw = self.last_w.get(k)
            if lw:
                deps.append(lw)
            if k[:2] in ("pb", "pt") and len(k) == 3:
                deps.extend(self.readers.get(k, {}).items())
        for k in writes:
            lw = self.last_w.get(k)
            if lw:
                deps.append(lw)
            deps.extend(self.readers.get(k, {}).items())
        return deps

    def _wait(self, e, deps):
        best = {}
        for s, v in deps:
            if v > best.get(s, 0):
                best[s] = v
        for s, v in best.items():
            if e == 'pe' and s == self.esem['pe']:
                continue
            if self.seen[e].get(s, 0) >= v:
                continue
            self.engs[e].wait_ge(self.semh[s], v)
            self.seen[e][s] = v

    def _upd(self, tok, reads, writes):
        for k in reads:
            d = self.readers.setdefault(k, {})
            if tok[1] > d.get(tok[0], 0):
                d[tok[0]] = tok[1]
        for k in writes:
            self.last_w[k] = tok
            self.readers[k] = {}

    def op(self, e, fn, reads=(), writes=()):
        self._wait(e, self._deps(reads, writes))
        inst = fn(self.engs[e])
        self.cnt[e] += 1
        inst.then_inc(self.semh[self.esem[e]], 1)
        tok = (self.esem[e], self.cnt[e])
        self._upd(tok, reads, writes)
        return tok

    def dma(self, q, out, in_, reads=(), writes=(), **kw):
        i = self.di
        self.di = (i + 1) % len(self.dsem)
        deps = self._deps(reads, writes)
        if self.duse[i]:
            deps.append((self.dsem[i], 16 * self.duse[i]))
        self._wait(q, deps)
        inst = self.engs[q].dma_start(out=out, in_=in_, **kw)
        self.duse[i] += 1
        inst.then_inc(self.semh[self.dsem[i]], 16)
        tok = (self.dsem[i], 16 * self.duse[i])
        self._upd(tok, reads, writes)
        return tok

    def wait_keys(self, e, keys):
        deps = []
        for k in keys:
            lw = self.last_w.get(k)
            if lw:
                deps.append(lw)
            deps.extend(self.readers.get(k, {}).items())
        self._wait(e, deps)

    def barrier(self):
        deps = [(self.esem[e], self.cnt[e]) for e in ['pe', 'act', 'dve', 'pool'] if self.cnt[e]]
        deps += [(self.dsem[i], 16 * self.duse[i]) for i in range(len(self.dsem)) if self.duse[i]]
        for e in self.engs:
            self._wait(e, deps)


class Arena:
    def __init__(self, nbytes):
        self.free = [(0, nbytes)]

    def alloc(self, n):
        n = (n + 63) // 64 * 64
        for i, (o, sz) in enumerate(self.free):
            if sz >= n:
                if sz == n:
                    self.free.pop(i)
                else:
                    self.free[i] = (o + n, sz - n)
                return o, n
        raise MemoryError("arena out of memory: need %d, free list %s" % (n, self.free))

    def release(self, o, n):
        self.free.append((o, n))
        self.free.sort()
        out = []
        for (a, b) in self.free:
            if out and out[-1][0] + out[-1][1] == a:
                out[-1] = (out[-1][0], out[-1][1] + b)
            else:
                out.append((a, b))
        self.free = out


ARENA_KB = 204


class _Stop(Exception):
    pass


def build_nc(dbg=None, stop=None):
    dbg = dbg or ()

    def chk(tag):
        if stop == tag:
            raise _Stop()
    nc = bass.Bass("TRN2", target_bir_lowering=False)

    def din(name, shape):
        return nc.dram_tensor(name, list(shape), F32, kind="ExternalInput").ap()

    x_d = din("x", [NT, D]); p_d = din("p", [NT, 256])
    g_mix_d = din("g_mix", [1, D]); w_in_d = din("w_in", [D, INW])
    g_q_d = din("g_q", [1, 512]); w_uq_d = din("w_uq", [512, 1024]); w_uqi_d = din("w_uq_idx", [512, 1024])
    g_kidx_d = din("g_kidx", [1, 64])
    a_re_d = din("a_re", [32, 128]); a_im_d = din("a_im", [32, 128]); log_dt_d = din("log_dt", [32, 2])
    b_re_d = din("b_re", [4096, 16]); b_im_d = din("b_im", [4096, 16])
    c_re_d = din("c_re", [64, 16, 64]); c_im_d = din("c_im", [64, 16, 64])
    d_skip_d = din("d_skip", [1, 1024])
    w_glu_d = din("w_glu", [1024, 2048]); w_so_d = din("w_ssm_out", [1024, 2048]); w_ao_d = din("w_attn_out", [1024, 2048])
    w_o_d = din("w_o", [D, D]); g_ple_d = din("g_ple", [1, D]); w_pg_d = din("w_ple_gate", [D, D])
    w_ple_d = din("w_ple", [256, D]); g_pp_d = din("g_ple_post", [1, D]); g_fin_d = din("g_final", [1, D])
    qpos_d = din("qpos", [128, TPC]); oh_d = din("onehot", [128, NCORE])
    y_d = nc.dram_tensor("y", [NT, D], F32, kind="ExternalOutput").ap()

    def dint(name, shape, dt):
        return nc.dram_tensor(name, list(shape), dt, kind="Internal").ap()

    kT_loc = dint("kT_loc", [256, NT], BF16); kT_gat = dint("kT_gat", [NCORE * 256, NT], BF16)
    v_loc = dint("v_loc", [NT, 256], BF16); v_gat = dint("v_gat", [NCORE * NT, 256], BF16)
    ki_loc = dint("ki_loc", [64, NT], BF16); ki_gat = dint("ki_gat", [NCORE * 64, NT], BF16)
    e_loc = dint("e_loc", [128, TPC * 64], F32); e_gat = dint("e_gat", [NCORE * 128, TPC * 64], F32)

    dbg_out = {}

    def dbg_tensor(name, shape, dt=F32):
        dbg_out[name] = nc.dram_tensor("dbg_" + name, list(shape), dt, kind="ExternalOutput").ap()
        return dbg_out[name]

    with ExitStack() as es:
        S = Sched(nc, es)
        cc_sem = es.enter_context(nc.semaphore("cc_sem"))

        arena_t = es.enter_context(nc.sbuf_tensor("arena", [128, ARENA_KB * 256], F32))
        arena = Arena(ARENA_KB * 1024)

        def sb(st, name, shape, dt):
            esz = 4 if dt == F32 else 2
            nel = 1
            for d_ in shape[1:]:
                nel *= d_
            off, n = arena.alloc(nel * esz)
            st.callback(arena.release, off, n)
            v = arena_t[:, off // 4:(off + n) // 4]
            if dt != F32:
                v = v.bitcast(dt)
            v = v[0:shape[0], 0:nel]
            if len(shape) == 3:
                v = v.rearrange("p (a b) -> p a b", a=shape[1])
            elif len(shape) == 4:
                v = v.rearrange("p (a b c) -> p a b c", a=shape[1], b=shape[2])
            elif len(shape) == 5:
                v = v.rearrange("p (a b c d) -> p a b c d", a=shape[1], b=shape[2], c=shape[3])
            return v

        pb = [es.enter_context(nc.psum_tensor("pb%d" % i, [128, 512], F32)) for i in range(6)]
        pt = [es.enter_context(nc.psum_tensor("pt%d" % i, [128, 1024], BF16)) for i in range(2)]
        PB = ["pb%d" % i for i in range(6)]
        PT = ["pt0", "pt1"]

        identf = sb(es, "identf", [128, 128], F32)
        ident = sb(es, "ident", [128, 128], BF16)
        ones_b = sb(es, "ones_b", [128, 128], BF16)
        qpos = sb(es, "qpos_s", [128, TPC], F32)
        oh = sb(es, "oh_s", [128, NCORE], F32)
        S.op('pool', lambda e: e.memset(identf[:], 0.0), writes=['identf'])
        S.op('pool', lambda e: e.affine_select(out=identf[:], in_=identf[:], pattern=[[-1, 128]],
                                                compare_op=ALU.not_equal, fill=1.0, base=0, channel_multiplier=1),
             reads=['identf'], writes=['identf'])
        S.op('dve', lambda e: e.tensor_copy(out=ident[:], in_=identf[:]), reads=['identf'], writes=['ident'])
        S.op('dve', lambda e: e.memset(ones_b[:], 1.0), writes=['ones_b'])
        S.dma('sp', qpos[:], qpos_d, writes=['qpos'])
        S.dma('sp', oh[:], oh_d, writes=['oh'])

        def load_pp(st, name, vec_d, n):
            t = sb(st, name, [128, n], F32)
            S.dma('sp', t[:], vec_d.rearrange("o (c p) -> p (o c)", p=128), writes=[name],
                  allow_slow_non_contiguous=True)
            return t

        gm = load_pp(es, "gm", g_mix_d, 16)
        gpl = load_pp(es, "gpl", g_ple_d, 16)
        gq = load_pp(es, "gq", g_q_d, 4)
        dsk = load_pp(es, "dsk", d_skip_d, 8)

        pers = es.enter_context(ExitStack())
        ust = es.enter_context(ExitStack())
        szs = sb(pers, "szs", [128, 8, NT], BF16)
        sza = sb(pers, "sza", [128, 8, NT], BF16)
        cqn = sb(pers, "cqn", [128, 4, NT], BF16)
        wq = sb(pers, "wq", [128, TPC, 16], F32)

        cast_i = [0]

        stg = [sb(es, "stg%d" % i, [128, 1024], F32) for i in range(2)]

        def load_w_bf16(dst_ap, src_ap, key):
            P, A, B = dst_ap.shape
            bs_ = min(B, 1024)
            step = max(1, 1024 // bs_)
            for b0 in range(0, B, bs_):
                for a0 in range(0, A, step):
                    a1 = min(A, a0 + step)
                    i = cast_i[0] % 2
                    cast_i[0] += 1
                    sv = stg[i][0:P, 0:(a1 - a0) * bs_].rearrange("p (a b) -> p a b", b=bs_)
                    S.dma('sp', sv, src_ap[:, a0:a1, b0:b0 + bs_], writes=["stg%d" % i])
                    if cast_i[0] % 3 == 0:
                        S.op('act', lambda e: e.activation(out=dst_ap[:, a0:a1, b0:b0 + bs_], in_=sv, func=AF.Copy), reads=["stg%d" % i], writes=[key])
                    else:
                        S.op('pool', lambda e: e.tensor_copy(out=dst_ap[:, a0:a1, b0:b0 + bs_], in_=sv), reads=["stg%d" % i], writes=[key])

        evac_i = [0]

        def copy_evac(out_ap, in_ap, reads, writes, scale=None):
            evac_i[0] += 1
            if evac_i[0] % 2 == 0 or scale is not None:
                if scale is None:
                    S.op('act', lambda e: e.activation(out=out_ap, in_=in_ap, func=AF.Copy), reads=reads, writes=writes)
                else:
                    S.op('act', lambda e: e.activation(out=out_ap, in_=in_ap, func=AF.Copy, scale=scale), reads=reads, writes=writes)
            else:
                S.op('dve', lambda e: e.tensor_copy(out=out_ap, in_=in_ap), reads=reads, writes=writes)

        def rstd_cols(col_ap, key, inv_n):
            S.op('dve', lambda e: e.tensor_scalar(out=col_ap, in0=col_ap, scalar1=inv_n, scalar2=EPS, op0=ALU.mult, op1=ALU.add),
                 reads=[key], writes=[key])
            S.op('act', lambda e: e.activation(out=col_ap, in_=col_ap, func=AF.Sqrt), reads=[key], writes=[key])
            S.op('dve', lambda e: e.reciprocal(out=col_ap, in_=col_ap), reads=[key], writes=[key])

        def norm_transpose(st, tag, get_tile, gain, dstT, dkey, gkey):
            xn = sb(st, tag + "_xn", [128, D], BF16)
            junk = sb(st, tag + "_junk", [128, D], BF16)
            ssq = sb(st, tag + "_ssq", [128, TPC], F32)
            for m in range(TPC):
                xs, xkey = get_tile(m)
                S.op('act', lambda e: e.activation(out=junk[:], in_=xs, func=AF.Square, accum_out=ssq[:, m:m + 1]),
                     reads=[xkey], writes=[tag + "_junk", tag + "_ssq"])
                rstd_cols(ssq[:, m:m + 1], tag + "_ssq", 1.0 / D)
                S.op('dve', lambda e: e.tensor_scalar(out=xn[:], in0=xs, scalar1=ssq[:, m:m + 1], scalar2=None, op0=ALU.mult),
                     reads=[xkey, tag + "_ssq"], writes=[tag + "_xn"])
                for hb in range(2):
                    for c in range(8):
                        dc = hb * 8 + c
                        S.op('pe', lambda e: e.transpose(out=pt[hb][:, c * 128:(c + 1) * 128], in_=xn[:, dc * 128:(dc + 1) * 128], identity=ident[:]),
                             reads=[tag + "_xn", 'ident'], writes=[PT[hb]])
                    for c in range(8):
                        dc = hb * 8 + c
                        S.op('act', lambda e: e.activation(out=dstT[:, dc, m * 128:(m + 1) * 128], in_=pt[hb][:, c * 128:(c + 1) * 128],
                                                           func=AF.Copy, scale=gain[:, dc:dc + 1]),
                             reads=[PT[hb], gkey], writes=[dkey])

        def x_tile_loader(st, tag, src_d, width):
            bufs = [sb(st, "%s_x%d" % (tag, i), [128, width], F32) for i in range(2)]

            def get(m):
                b = m % 2
                key = "%s_x%d" % (tag, b)
                S.dma('sp', bufs[b][:], src_d[m * 128:(m + 1) * 128, :], writes=[key])
                return bufs[b][:], key
            return get

        def proj_fm(st, tag, w_d, col0, ncols_list, KC, actT, akey, evac):
            wst = [sb(st, "%s_w%d" % (tag, i), [128, KC, 128], BF16) for i in range(2)]
            c0 = col0
            for i, ncol in enumerate(ncols_list):
                b = i % 2
                wkey = "%s_w%d" % (tag, b)
                load_w_bf16(wst[b][:, :, 0:ncol], w_d[:, c0:c0 + ncol].rearrange("(kc p) c -> p kc c", p=128), wkey)
                for half in range(2):
                    pi = (2 * i + half) % 4
                    for kc in range(KC):
                        S.op('pe', lambda e: e.matmul(pb[pi][0:ncol, :], lhsT=wst[b][:, kc, 0:ncol], rhs=actT[:, kc, half * 512:(half + 1) * 512],
                                                      start=(kc == 0), stop=(kc == KC - 1)),
                             reads=[wkey, akey], writes=[PB[pi]])
                    evac(i, half, pb[pi], PB[pi], ncol)
                c0 += ncol

        try:
            with ExitStack() as ph:
                hT = sb(ph, "hT", [128, 16, NT], BF16)
                uT = sb(ust, "uT", [128, 8, NT], BF16)
                kTl = sb(ph, "kTl", [128, 2, NT], BF16)
                cqf = sb(ph, "cqf", [128, 4, NT], F32)
                with ExitStack() as pa:
                    norm_transpose(pa, "A", x_tile_loader(pa, "A", x_d, D), gm, hT, "hT", "gm")
                    S.barrier()
                chk("A")

                def evac_main(i, half, ps, pkey, ncol):
                    sl = slice(half * 512, (half + 1) * 512)
                    if i < 8:
                        copy_evac(uT[:, i, sl], ps[:, :], [pkey], ["uT"])
                    elif i < 16:
                        S.op('act', lambda e: e.activation(out=szs[:, i - 8, sl], in_=ps[:, :], func=AF.Silu), reads=[pkey], writes=["szs"])
                    elif i < 20:
                        copy_evac(cqf[:, i - 16, sl], ps[:, :], [pkey], ["cqf"])
                    else:
                        copy_evac(kTl[:, i - 20, sl], ps[:, :], [pkey], ["kTl"])

                proj_fm(ph, "B1", w_in_d, 0, [128] * 22, 16, hT, "hT", evac_main)

                def evac_za(i, half, ps, pkey, ncol):
                    sl = slice(half * 512, (half + 1) * 512)
                    S.op('act', lambda e: e.activation(out=sza[:, i, sl], in_=ps[:, :], func=AF.Silu), reads=[pkey], writes=["sza"])

                proj_fm(ph, "B2", w_in_d, 3072, [128] * 8, 16, hT, "hT", evac_za)
                chk("B2")
                S.dma('sp', kT_loc.rearrange("(c p) t -> p c t", p=128), kTl[:], reads=["kTl"], writes=["kT_loc"])

                wv = sb(ph, "wv", [128, 16, 256], BF16)
                wkw = sb(ph, "wkw", [128, 16, 80], BF16)
                load_w_bf16(wv[:], w_in_d[:, 2816:3072].rearrange("(kc p) c -> p kc c", p=128), "wv")
                load_w_bf16(wkw[:], w_in_d[:, 4096:4176].rearrange("(kc p) c -> p kc c", p=128), "wkw")
                vtok = sb(ph, "vtok", [128, TPC, 256], BF16)
                kif = sb(ph, "kif", [128, 64], F32)
                kib = sb(ph, "kib", [128, 64], BF16)
                kjunk = sb(ph, "kjunk", [128, 64], BF16)
                kss = sb(ph, "kss", [128, TPC], F32)
                kiT = sb(ph, "kiT", [64, NT], BF16)
                gki = sb(ph, "gki", [64, 1], F32)
                S.dma('sp', gki[:], g_kidx_d.rearrange("o (c p) -> p (o c)", p=64), writes=["gki"], allow_slow_non_contiguous=True)
                for m in range(TPC):
                    for kc in range(16):
                        S.op('pe', lambda e: e.matmul(pb[4][:, 0:256], lhsT=hT[:, kc, m * 128:(m + 1) * 128], rhs=wv[:, kc, :],
                                                      start=(kc == 0), stop=(kc == 15)), reads=["hT", "wv"], writes=[PB[4]])
                    for kc in range(16):
                        S.op('pe', lambda e: e.matmul(pb[5][:, 0:80], lhsT=hT[:, kc, m * 128:(m + 1) * 128], rhs=wkw[:, kc, :],
                                                      start=(kc == 0), stop=(kc == 15)), reads=["hT", "wkw"], writes=[PB[5]])
                    copy_evac(vtok[:, m, :], pb[4][:, 0:256], [PB[4]], ["vtok"])
                    S.op('dve', lambda e: e.tensor_copy(out=kif[:], in_=pb[5][:, 0:64]), reads=[PB[5]], writes=["kif"])
                    S.op('dve', lambda e: e.tensor_scalar(out=wq[:, m, :], in0=pb[5][:, 64:80], scalar1=1.0 / 32.0, scalar2=None, op0=ALU.mult),
                         reads=[PB[5]], writes=["wq"])
                    S.op('act', lambda e: e.activation(out=kjunk[:], in_=kif[:], func=AF.Square, accum_out=kss[:, m:m + 1]),
                         reads=["kif"], writes=["kjunk", "kss"])
                    rstd_cols(kss[:, m:m + 1], "kss", 1.0 / 64)
                    S.op('dve', lambda e: e.tensor_scalar(out=kib[:], in0=kif[:], scalar1=kss[:, m:m + 1], scalar2=None, op0=ALU.mult),
                         reads=["kif", "kss"], writes=["kib"])
                    S.op('pe', lambda e: e.transpose(out=pt[0][0:64, 0:128], in_=kib[:, :], identity=ident[:]),
                         reads=["kib", "ident"], writes=[PT[0]])
                    S.op('act', lambda e: e.activation(out=kiT[:, m * 128:(m + 1) * 128], in_=pt[0][0:64, 0:128], func=AF.Copy, scale=gki[:, 0:1]),
                         reads=[PT[0], "gki"], writes=["kiT"])
                S.dma('sp', v_loc.rearrange("(m p) d -> p m d", p=128), vtok[:], reads=["vtok"], writes=["v_loc"])
                S.dma('sp', ki_loc, kiT[:], reads=["kiT"], writes=["ki_loc"])
                chk("B3")

                S.wait_keys('pool', ["kT_loc", "v_loc", "ki_loc"])
                ncc = [0]
                for (a, b_) in [(kT_loc, kT_gat), (v_loc, v_gat), (ki_loc, ki_gat)]:
                    nc.gpsimd.collective_compute("AllGather", ALU.bypass, replica_groups=[list(range(NCORE))],
                                                 ins=[a], outs=[b_]).then_inc(cc_sem, 1)
                    ncc[0] += 1

                sq = sb(ph, "sq", [128, 4, NT], BF16)
                rsb = sb(ph, "rsb", [128, NT], F32)
                S.op('act', lambda e: e.activation(out=sq[:], in_=cqf[:], func=AF.Square), reads=["cqf"], writes=["sq"])
                for half in range(2):
                    sl = slice(half * 512, (half + 1) * 512)
                    for c in range(4):
                        S.op('pe', lambda e: e.matmul(pb[half][:, :], lhsT=ones_b[:], rhs=sq[:, c, sl], start=(c == 0), stop=(c == 3)),
                             reads=["ones_b", "sq"], writes=[PB[half]])
                    S.op('dve', lambda e: e.tensor_scalar(out=rsb[:, sl], in0=pb[half][:, :], scalar1=1.0 / 512, scalar2=EPS, op0=ALU.mult, op1=ALU.add),
                         reads=[PB[half]], writes=["rsb"])
                S.op('act', lambda e: e.activation(out=rsb[:], in_=rsb[:], func=AF.Sqrt), reads=["rsb"], writes=["rsb"])
                S.op('dve', lambda e: e.reciprocal(out=rsb[:], in_=rsb[:]), reads=["rsb"], writes=["rsb"])
                for c in range(4):
                    S.op('dve', lambda e: e.scalar_tensor_tensor(out=cqn[:, c, :], in0=cqf[:, c, :], scalar=gq[:, c:c + 1], in1=rsb[:],
                                                                op0=ALU.mult, op1=ALU.mult), reads=["cqf", "gq", "rsb"], writes=["cqn"])
                if "uT" in dbg:
                    S.dma('sp', dbg_tensor("uT", [128, 8 * NT], BF16), uT[:].rearrange("p a b -> p (a b)"), reads=["uT"], writes=["dbg_uT"])
                if "cqn" in dbg:
                    S.dma('sp', dbg_tensor("cqn", [128, 4 * NT], BF16), cqn[:].rearrange("p a b -> p (a b)"), reads=["cqn"], writes=["dbg_cqn"])
                S.barrier()
                chk("B5")

            with ExitStack() as ph:
                ptab = ph.enter_context(ExitStack())
                Ct = sb(ptab, "Ct", [128, 32, 128], F32)
                St = sb(ptab, "St", [128, 32, 128], F32)
                W2r = sb(ph, "W2r", [128, 32, 128], BF16)
                W2i = sb(ph, "W2i", [128, 32, 128], BF16)
                CLr = sb(ph, "CLr", [128, 32, 128], BF16)
                CLi = sb(ph, "CLi", [128, 32, 128], BF16)
                LTBr = sb(ph, "LTBr", [128, 32, 128], BF16)
                LTBi = sb(ph, "LTBi", [128, 32, 128], BF16)
                rr = sb(ph, "rr", [128, 32], F32)
                l128r = sb(ph, "l128r", [128, 32], F32)
                l128i = sb(ph, "l128i", [128, 32], F32)
                ymain = sb(ph, "ymain", [128, 8, NT], BF16)
                Eloc = sb(ph, "Eloc", [128, TPC, 32, 2], F32)

                with ExitStack() as pp:
                    nat = sb(pp, "nat", [32, 3, 128], F32)
                    S.dma('sp', nat[:, 0, :], a_re_d, writes=["nat"])
                    S.dma('sp', nat[:, 1, :], a_im_d, writes=["nat"])
                    ld2 = sb(pp, "ld2", [32, 2], F32)
                    S.dma('sp', ld2[:], log_dt_d, writes=["ld2"])
                    S.op('dve', lambda e: e.tensor_copy(out=nat[:, 2, :].rearrange("p (a b) -> p a b", a=2),
                                                        in_=ld2[:, :].unsqueeze(2).to_broadcast([32, 2, 64])),
                         reads=["ld2", "nat"], writes=["nat"])
                    prm = sb(pp, "prm", [128, 3, 32], F32)
                    for q in range(3):
                        S.op('pe', lambda e: e.transpose(out=pb[0][:, q * 32:(q + 1) * 32], in_=nat[:, q, :], identity=identf[0:32, 0:32]),
                             reads=["nat", "identf"], writes=[PB[0]])
                    S.op('dve', lambda e: e.tensor_copy(out=prm[:].rearrange("p a b -> p (a b)"), in_=pb[0][:, 0:96]), reads=[PB[0]], writes=["prm"])
                    ar = prm[:, 0, :]
                    ai = prm[:, 1, :]
                    w = sb(pp, "wk", [128, 16, 32], F32)
                    WK = "wk"

                    def V(i):
                        return w[:, i, :]

                    def tt(o, a, b_, op):
                        S.op('dve', lambda e: e.tensor_tensor(out=o, in0=a, in1=b_, op=op), reads=[WK, "prm"], writes=[WK])

                    def ts(o, a, s1, op0, s2=None, op1=None):
                        if op1 is None:
                            S.op('dve', lambda e: e.tensor_scalar(out=o, in0=a, scalar1=s1, scalar2=None, op0=op0), reads=[WK, "prm"], writes=[WK])
                        else:
                            S.op('dve', lambda e: e.tensor_scalar(out=o, in0=a, scalar1=s1, scalar2=s2, op0=op0, op1=op1), reads=[WK, "prm"], writes=[WK])

                    def act(o, a, f, scale=1.0):
                        S.op('act', lambda e: e.activation(out=o, in_=a, func=f, scale=scale), reads=[WK, "prm"], writes=[WK])

                    def reduce_pi(xv, tmp):
                        for _ in range(5):
                            ts(tmp, xv, PI, ALU.is_gt, -2.0 * PI, ALU.mult)
                            tt(xv, xv, tmp, ALU.add)
                        for _ in range(2):
                            ts(tmp, xv, -PI, ALU.is_lt, 2.0 * PI, ALU.mult)
                            tt(xv, xv, tmp, ALU.add)
                        ts(xv, xv, PI, ALU.min, -PI, ALU.max)

                    dt_ = V(0)
                    act(dt_, prm[:, 2, :], AF.Exp)
                    tt(V(1), dt_, ar, ALU.mult)
                    act(V(1), V(1), AF.Exp)
                    tt(V(2), dt_, ai, ALU.mult)
                    ts(V(3), V(2), PI / 2, ALU.add)
                    reduce_pi(V(2), V(4))
                    reduce_pi(V(3), V(4))
                    act(V(2), V(2), AF.Sin)
                    act(V(3), V(3), AF.Sin)
                    S.op('dve', lambda e: e.tensor_copy(out=rr[:], in_=V(1)), reads=[WK], writes=["rr"])
                    tt(V(5), V(1), V(3), ALU.mult)
                    tt(V(6), V(1), V(2), ALU.mult)
                    tt(V(7), ar, ar, ALU.mult)
                    tt(V(8), ai, ai, ALU.mult)
                    tt(V(7), V(7), V(8), ALU.add)
                    S.op('dve', lambda e: e.reciprocal(out=V(7), in_=V(7)), reads=[WK], writes=[WK])
                    ts(V(8), V(5), -1.0, ALU.add)
                    tt(V(9), V(8), ar, ALU.mult)
                    tt(V(10), V(6), ai, ALU.mult)
                    tt(V(9), V(9), V(10), ALU.add)
                    tt(V(9), V(9), V(7), ALU.mult)
                    tt(V(10), V(6), ar, ALU.mult)
                    tt(V(11), V(8), ai, ALU.mult)
                    tt(V(10), V(10), V(11), ALU.subtract)
                    tt(V(10), V(10), V(7), ALU.mult)
                    ptb = pp.enter_context(ExitStack())
                    Rp = sb(ptb, "Rp", [128, 32, 128], F32)
                    tmpa = sb(ptb, "tmpa", [128, 32, 64], F32)
                    tmpb = sb(ptb, "tmpb", [128, 32, 64], F32)
                    TK = ["Ct", "St", "Rp", "tmpa", "tmpb", WK]

                    def tb(o, a, b_, op):
                        S.op('dve', lambda e: e.tensor_tensor(out=o, in0=a, in1=b_, op=op), reads=TK, writes=TK)

                    S.op('dve', lambda e: e.tensor_copy(out=Ct[:, :, 0], in_=V(3)), reads=[WK], writes=["Ct"])
                    S.op('dve', lambda e: e.tensor_copy(out=St[:, :, 0], in_=V(2)), reads=[WK], writes=["St"])
                    S.op('dve', lambda e: e.tensor_copy(out=Rp[:, :, 0], in_=V(1)), reads=[WK], writes=["Rp"])
                    L = 1
                    while L < 128:
                        cb = Ct[:, :, L - 1:L].to_broadcast([128, 32, L])
                        sbc = St[:, :, L - 1:L].to_broadcast([128, 32, L])
                        rb = Rp[:, :, L - 1:L].to_broadcast([128, 32, L])
                        tb(tmpa[:, :, 0:L], Ct[:, :, 0:L], cb, ALU.mult)
                        tb(tmpb[:, :, 0:L], St[:, :, 0:L], sbc, ALU.mult)
                        tb(Ct[:, :, L:2 * L], tmpa[:, :, 0:L], tmpb[:, :, 0:L], ALU.subtract)
                        tb(tmpa[:, :, 0:L], Ct[:, :, 0:L], sbc, ALU.mult)
                        tb(tmpb[:, :, 0:L], St[:, :, 0:L], cb, ALU.mult)
                        tb(St[:, :, L:2 * L], tmpa[:, :, 0:L], tmpb[:, :, 0:L], ALU.add)
                        tb(Rp[:, :, L:2 * L], Rp[:, :, 0:L], rb, ALU.mult)
                        L *= 2
                    tb(W2r[:], Rp[:], Ct[:], ALU.mult)
                    tb(W2i[:], Rp[:], St[:], ALU.mult)
                    S.op('dve', lambda e: e.tensor_tensor(out=l128r[:], in0=Rp[:, :, 127], in1=Ct[:, :, 127], op=ALU.mult), reads=TK, writes=["l128"])
                    S.op('dve', lambda e: e.tensor_tensor(out=l128i[:], in0=Rp[:, :, 127], in1=St[:, :, 127], op=ALU.mult), reads=TK, writes=["l128"])
                    S.barrier()
                    ptb.close()
                    bnr = sb(pp, "bnr", [128, 32, 16], F32)
                    bni = sb(pp, "bni", [128, 32, 16], F32)
                    S.dma('sp', bnr[:], b_re_d.rearrange("(j p) c -> p j c", p=128), writes=["bnr"])
                    S.dma('sp', bni[:], b_im_d.rearrange("(j p) c -> p j c", p=128), writes=["bni"])
                    t1 = sb(pp, "t1", [128, 32, 16], F32)
                    t2 = sb(pp, "t2", [128, 32, 16], F32)
                    X4r = sb(pp, "X4r", [128, 32, 32], BF16)
                    X4i = sb(pp, "X4i", [128, 32, 32], BF16)
                    BK = ["bnr", "bni", "t1", "t2", "X4r", "X4i", WK]

                    def tbb(o, a, b_, op):
                        S.op('dve', lambda e: e.tensor_tensor(out=o, in0=a, in1=b_, op=op), reads=BK, writes=BK)

                    frb = V(9).unsqueeze(2).to_broadcast([128, 32, 16])
                    fib = V(10).unsqueeze(2).to_broadcast([128, 32, 16])
                    S.op('pool', lambda e: e.memset(X4r[:], 0.0), writes=["X4r"])
                    S.op('pool', lambda e: e.memset(X4i[:], 0.0), writes=["X4i"])
                    tbb(t1[:], bnr[:], frb, ALU.mult)
                    tbb(t2[:], bni[:], fib, ALU.mult)
                    tbb(t1[:], t1[:], t2[:], ALU.subtract)
                    for (lo, c0) in [(0, 0), (64, 16)]:
                        S.op('dve', lambda e: e.tensor_copy(out=X4r[lo:lo + 64, :, c0:c0 + 16], in_=t1[lo:lo + 64, :, :]), reads=BK, writes=BK)
                    tbb(t1[:], bni[:], frb, ALU.mult)
                    tbb(t2[:], bnr[:], fib, ALU.mult)
                    tbb(t1[:], t1[:], t2[:], ALU.add)
                    for (lo, c0) in [(0, 0), (64, 16)]:
                        S.op('dve', lambda e: e.tensor_copy(out=X4i[lo:lo + 64, :, c0:c0 + 16], in_=t1[lo:lo + 64, :, :]), reads=BK, writes=BK)
                    rowm = sb(pp, "rowm", [128, 4], F32)
                    for q in range(4):
                        S.op('dve', lambda e: e.tensor_reduce(out=rowm[:, q:q + 1], in_=identf[:, 32 * q:32 * q + 32], axis=AX.X, op=ALU.add),
                             reads=["identf"], writes=["rowm"])
                    for (X4, LTB, lk) in [(X4r, LTBr, "LTBr"), (X4i, LTBi, "LTBi")]:
                        for k in range(8):
                            S.op('pe', lambda e: e.transpose(out=pt[0][:, k * 128:(k + 1) * 128],
                                                             in_=X4[:, 4 * k:4 * k + 4, :].rearrange("p a b -> p (a b)"), identity=ident[:]),
                                 reads=BK + ["ident"], writes=[PT[0]])
                        for k in range(8):
                            for q in range(4):
                                S.op('dve', lambda e: e.tensor_scalar(out=LTB[:, 4 * k + q, :], in0=pt[0][:, k * 128:(k + 1) * 128], scalar1=rowm[:, q:q + 1], scalar2=None, op0=ALU.mult),
                                     reads=[PT[0], "rowm"], writes=[lk])
                    Xc = sb(pp, "Xc", [32, 32, 128], F32)
                    for (c_d, CL, ck, sgn) in [(c_re_d, CLr, "CLr", 1.0), (c_im_d, CLi, "CLi", -1.0)]:
                        S.op('pool', lambda e: e.memset(Xc[:], 0.0), reads=["Xc"], writes=["Xc"])
                        S.op('pool', lambda e: e.memset(CL[:], 0.0), writes=[ck])
                        cv = c_d.rearrange("(j two) c n -> two c j n", two=2)
                        S.dma('sp', Xc[0:16, :, 0:64], cv[0], reads=[], writes=["Xc"])
                        S.dma('sp', Xc[16:32, :, 64:128], cv[1], reads=[], writes=["Xc"])
                        for hb in range(2):
                            for jj in range(16):
                                j = hb * 16 + jj
                                S.op('pe', lambda e: e.transpose(out=pb[hb][:, jj * 32:(jj + 1) * 32], in_=Xc[:, j, :], identity=identf[0:32, 0:32]),
                                     reads=["Xc", "identf"], writes=[PB[hb]])
                            for q in range(4):
                                src = pb[hb][:, 0:512].rearrange("p (k q c) -> p k q c", q=4, c=32)[:, :, q, :]
                                dst = CL[:, hb * 16:(hb + 1) * 16, :].rearrange("p (k q) c -> p k q c", q=4)[:, :, q, 32 * q:32 * q + 32]
                                S.op('act', lambda e: e.activation(out=dst, in_=src, func=AF.Copy, scale=sgn), reads=[PB[hb]], writes=[ck])
                    S.barrier()
                    chk("C1")

                with ExitStack() as pm:
                    Rm = [sb(pm, "Rm%d" % i, [128, NT], F32) for i in range(1)]
                    bre = sb(pm, "bre", [128, NT], F32)
                    bim = sb(pm, "bim", [128, NT], F32)
                    ta = sb(pm, "ta", [128, NT], F32)
                    tb_ = sb(pm, "tb_", [128, NT], F32)
                    gre = sb(pm, "gre", [128, NT], F32)
                    gim = sb(pm, "gim", [128, NT], F32)
                    hre = [sb(pm, "hre%d" % i, [128, NT], BF16) for i in range(1)]
                    him = [sb(pm, "him%d" % i, [128, NT], BF16) for i in range(1)]

                    def bc(tab, j, n):
                        return tab[:, j:j + 1, :].to_broadcast([128, n, 128])

                    def v3(ap_, n):
                        return ap_.rearrange("p (a b) -> p a b", b=128)

                    for k in range(8):
                        for q in range(4):
                            j = 4 * k + q
                            jb = 0
                            rk = "Rm%d" % jb
                            S.op('pool', lambda e: e.tensor_copy(out=Rm[jb][:], in_=rr[:, j:j + 1].to_broadcast([128, NT])), reads=["rr"], writes=[rk])
                            S.op('pool', lambda e: e.memset(v3(Rm[jb][:], 8)[:, :, 0:1], 0.0), writes=[rk])
                            for half in range(2):
                                sl = slice(half * 512, (half + 1) * 512)
                                S.op('pe', lambda e: e.matmul(pb[0][:, :], lhsT=LTBr[:, j, :], rhs=uT[:, k, sl], start=True, stop=True),
                                     reads=["LTBr", "uT"], writes=[PB[0]])
                                S.op('pe', lambda e: e.matmul(pb[1][:, :], lhsT=LTBi[:, j, :], rhs=uT[:, k, sl], start=True, stop=True),
                                     reads=["LTBi", "uT"], writes=[PB[1]])
                                cB = bc(Ct, j, 4)
                                sB = bc(St, j, 4)
                                S.op('dve', lambda e: e.tensor_tensor(out=v3(bre[:, sl], 4), in0=v3(pb[0][:, :], 4), in1=cB, op=ALU.mult), reads=[PB[0], "Ct"], writes=["bre"])
                                S.op('dve', lambda e: e.tensor_tensor(out=v3(ta[:, sl], 4), in0=v3(pb[1][:, :], 4), in1=sB, op=ALU.mult), reads=[PB[1], "St"], writes=["ta"])
                                S.op('dve', lambda e: e.tensor_tensor(out=v3(bim[:, sl], 4), in0=v3(pb[1][:, :], 4), in1=cB, op=ALU.mult), reads=[PB[1], "Ct"], writes=["bim"])
                                S.op('dve', lambda e: e.tensor_tensor(out=v3(tb_[:, sl], 4), in0=v3(pb[0][:, :], 4), in1=sB, op=ALU.mult), reads=[PB[0], "St"], writes=["tb_"])
                            S.op('pool', lambda e: e.tensor_tensor(out=bre[:], in0=bre[:], in1=ta[:], op=ALU.add), reads=["bre", "ta"], writes=["bre"])
                            S.op('pool', lambda e: e.tensor_tensor(out=bim[:], in0=bim[:], in1=tb_[:], op=ALU.subtract), reads=["bim", "tb_"], writes=["bim"])
                            rmb = Rm[jb][:, :]
                            S.op('dve', lambda e: e.tensor_tensor_scan(out=gre[:], data0=rmb, data1=bre[:], initial=0.0, op0=ALU.mult, op1=ALU.add),
                                 reads=[rk, "bre"], writes=["gre"])
                            S.op('dve', lambda e: e.tensor_tensor_scan(out=gim[:], data0=rmb, data1=bim[:], initial=0.0, op0=ALU.mult, op1=ALU.add),
                                 reads=[rk, "bim"], writes=["gim"])
                            c8 = bc(Ct, j, TPC)
                            s8 = bc(St, j, TPC)
                            S.op('pool', lambda e: e.tensor_tensor(out=v3(ta[:], 8), in0=v3(gre[:], 8), in1=c8, op=ALU.mult), reads=["gre", "Ct"], writes=["ta"])
                            S.op('dve', lambda e: e.tensor_tensor(out=v3(tb_[:], 8), in0=v3(gim[:], 8), in1=s8, op=ALU.mult), reads=["gim", "St"], writes=["tb_"])
                            S.op('pool', lambda e: e.tensor_tensor(out=ta[:], in0=ta[:], in1=tb_[:], op=ALU.subtract), reads=["ta", "tb_"], writes=["ta"])
                            S.op('act', lambda e: e.activation(out=hre[jb][:], in_=ta[:], func=AF.Copy), reads=["ta"], writes=["hre%d" % jb])
                            S.op('act', lambda e: e.activation(out=Eloc[:, :, j, 0], in_=v3(ta[:], 8)[:, :, 127], func=AF.Copy), reads=["ta"], writes=["Eloc"])
                            S.op('pool', lambda e: e.tensor_tensor(out=v3(bre[:], 8), in0=v3(gim[:], 8), in1=c8, op=ALU.mult), reads=["gim", "Ct"], writes=["bre"])
                            S.op('dve', lambda e: e.tensor_tensor(out=v3(bim[:], 8), in0=v3(gre[:], 8), in1=s8, op=ALU.mult), reads=["gre", "St"], writes=["bim"])
                            S.op('pool', lambda e: e.tensor_tensor(out=bre[:], in0=bre[:], in1=bim[:], op=ALU.add), reads=["bre", "bim"], writes=["bre"])
                            S.op('act', lambda e: e.activation(out=him[jb][:], in_=bre[:], func=AF.Copy), reads=["bre"], writes=["him%d" % jb])
                            S.op('act', lambda e: e.activation(out=Eloc[:, :, j, 1], in_=v3(bre[:], 8)[:, :, 127], func=AF.Copy), reads=["bre"], writes=["Eloc"])
                            for half in range(2):
                                sl = slice(half * 512, (half + 1) * 512)
                                S.op('pe', lambda e: e.matmul(pb[2 + half][:, :], lhsT=CLr[:, j, :], rhs=hre[jb][:, sl], start=(q == 0), stop=False),
                                     reads=["CLr", "hre%d" % jb], writes=[PB[2 + half]])
                                S.op('pe', lambda e: e.matmul(pb[2 + half][:, :], lhsT=CLi[:, j, :], rhs=him[jb][:, sl], start=False, stop=(q == 3)),
                                     reads=["CLi", "him%d" % jb], writes=[PB[2 + half]])
                        for half in range(2):
                            sl = slice(half * 512, (half + 1) * 512)
                            copy_evac(ymain[:, k, sl], pb[2 + half][:, :], [PB[2 + half]], ["ymain"])
                    S.dma('sp', e_loc, Eloc[:].rearrange("p m j c -> p (m j c)"), reads=["Eloc"], writes=["e_loc"])
                    S.wait_keys('pool', ["e_loc"])
                    nc.gpsimd.collective_compute("AllGather", ALU.bypass, replica_groups=[list(range(NCORE))],
                                                 ins=[e_loc], outs=[e_gat]).then_inc(cc_sem, 1)
                    ncc[0] += 1
                    if "ymain" in dbg:
                        S.dma('sp', dbg_tensor("ymain", [128, 8 * NT], BF16), ymain[:].rearrange("p a b -> p (a b)"), reads=["ymain"], writes=["dbg_ymain"])
                    S.barrier()
                    chk("C2")

                ptab.close()
                with ExitStack() as pc:
                    nc.sync.wait_ge(cc_sem, ncc[0])
                    nc.gpsimd.wait_ge(cc_sem, ncc[0])
                    cre = sb(pc, "cre", [128, TPC, 32], F32)
                    cim = sb(pc, "cim", [128, TPC, 32], F32)
                    pcc = pc.enter_context(ExitStack())
                    Eall = sb(pcc, "Eall", [128, NCORE, TPC, 32, 2], F32)
                    S.dma('sp', Eall[:].rearrange("p r m j c -> p r (m j c)"), e_gat.rearrange("(r p) f -> p r f", p=128), writes=["Eall"])
                    sre = sb(pcc, "sre", [128, 32], F32)
                    sim = sb(pcc, "sim", [128, 32], F32)
                    u1 = sb(pcc, "u1", [128, 32], F32)
                    u2 = sb(pcc, "u2", [128, 32], F32)
                    CK = ["sre", "sim", "cre", "cim", "u1", "u2"]

                    def cop(fn, extra=()):
                        S.op('dve', fn, reads=CK + list(extra), writes=CK)

                    cop(lambda e: e.memset(sre[:], 0.0))
                    cop(lambda e: e.memset(sim[:], 0.0))
                    cop(lambda e: e.memset(cre[:], 0.0))
                    cop(lambda e: e.memset(cim[:], 0.0))
                    for G in range(NCORE * TPC):
                        r, m = G % NCORE, G // NCORE
                        if G > 0:
                            cop(lambda e: e.scalar_tensor_tensor(out=cre[:, m, :], in0=sre[:], scalar=oh[:, r:r + 1], in1=cre[:, m, :], op0=ALU.mult, op1=ALU.add), ["oh"])
                            cop(lambda e: e.scalar_tensor_tensor(out=cim[:, m, :], in0=sim[:], scalar=oh[:, r:r + 1], in1=cim[:, m, :], op0=ALU.mult, op1=ALU.add), ["oh"])
                        if G == NCORE * TPC - 1:
                            break
                        cop(lambda e: e.tensor_tensor(out=u1[:], in0=sre[:], in1=l128r[:], op=ALU.mult), ["l128"])
                        cop(lambda e: e.tensor_tensor(out=u2[:], in0=sim[:], in1=l128i[:], op=ALU.mult), ["l128"])
                        cop(lambda e: e.tensor_tensor(out=u1[:], in0=u1[:], in1=u2[:], op=ALU.subtract))
                        cop(lambda e: e.tensor_tensor(out=u2[:], in0=sre[:], in1=l128i[:], op=ALU.mult), ["l128"])
                        cop(lambda e: e.tensor_tensor(out=sim[:], in0=sim[:], in1=l128r[:], op=ALU.mult), ["l128"])
                        cop(lambda e: e.tensor_tensor(out=sim[:], in0=sim[:], in1=u2[:], op=ALU.add))
                        cop(lambda e: e.tensor_tensor(out=sim[:], in0=sim[:], in1=Eall[:, r, m, :, 1], op=ALU.add), ["Eall"])
                        cop(lambda e: e.tensor_tensor(out=sre[:], in0=u1[:], in1=Eall[:, r, m, :, 0], op=ALU.add), ["Eall"])
                    S.barrier()
                    pcc.close()
                    Am = [sb(pc, "Am%d" % i, [128, 32, 128], BF16) for i in range(1)]
                    Bm = [sb(pc, "Bm%d" % i, [128, 32, 128], BF16) for i in range(1)]
                    t3 = sb(pc, "t3", [128, 8, 32], F32)
                    t4 = sb(pc, "t4", [128, 8, 32], F32)
                    for i in range(1):
                        S.op('pool', lambda e: e.memset(Am[i][:], 0.0), writes=["Am%d" % i])
                        S.op('pool', lambda e: e.memset(Bm[i][:], 0.0), writes=["Bm%d" % i])
                    yg = sb(pc, "yg", [128, 8, NT], F32)
                    for m in range(TPC):
                        mb = 0
                        ak, bk = "Am%d" % mb, "Bm%d" % mb
                        for q in range(4):
                            def blk(T):
                                return T[:].rearrange("p (k q) c -> p k q c", q=4)[:, :, q, 32 * q:32 * q + 32]

                            def cb_(cvec):
                                return cvec[:, m, :].rearrange("p (k q) -> p k q", q=4)[:, :, q:q + 1].to_broadcast([128, 8, 32])
                            KK = ["t3", "t4", ak, bk]
                            S.op('dve', lambda e: e.tensor_tensor(out=t3[:], in0=blk(CLr), in1=cb_(cre), op=ALU.mult), reads=CK + KK + ["CLr"], writes=KK)
                            S.op('dve', lambda e: e.tensor_tensor(out=t4[:], in0=blk(CLi), in1=cb_(cim), op=ALU.mult), reads=CK + KK + ["CLi"], writes=KK)
                            S.op('dve', lambda e: e.tensor_tensor(out=blk(Am[mb]), in0=t3[:], in1=t4[:], op=ALU.add), reads=KK, writes=KK)
                            S.op('dve', lambda e: e.tensor_tensor(out=t3[:], in0=blk(CLi), in1=cb_(cre), op=ALU.mult), reads=CK + KK + ["CLi"], writes=KK)
                            S.op('dve', lambda e: e.tensor_tensor(out=t4[:], in0=blk(CLr), in1=cb_(cim), op=ALU.mult), reads=CK + KK + ["CLr"], writes=KK)
                            S.op('dve', lambda e: e.tensor_tensor(out=blk(Bm[mb]), in0=t3[:], in1=t4[:], op=ALU.subtract), reads=KK, writes=KK)
                        for k in range(8):
                            pi = k % 4
                            for q in range(4):
                                j = 4 * k + q
                                S.op('pe', lambda e: e.matmul(pb[pi][:, 0:128], lhsT=Am[mb][:, j, :], rhs=W2r[:, j, :], start=(q == 0), stop=False),
                                     reads=[ak, "W2r"], writes=[PB[pi]])
                                S.op('pe', lambda e: e.matmul(pb[pi][:, 0:128], lhsT=Bm[mb][:, j, :], rhs=W2i[:, j, :], start=False, stop=(q == 3)),
                                     reads=[bk, "W2i"], writes=[PB[pi]])
                            tsl = slice(m * 128, (m + 1) * 128)
                            S.op('dve', lambda e: e.tensor_tensor(out=yg[:, k, tsl], in0=pb[pi][:, 0:128], in1=ymain[:, k, tsl], op=ALU.add),
                                 reads=[PB[pi], "ymain"], writes=["yg"])
                    for k in range(8):
                        S.op('dve', lambda e: e.scalar_tensor_tensor(out=yg[:, k, :], in0=uT[:, k, :], scalar=dsk[:, k:k + 1], in1=yg[:, k, :], op0=ALU.mult, op1=ALU.add),
                             reads=["uT", "dsk", "yg"], writes=["yg"])
                    if "ypre" in dbg:
                        S.dma('sp', dbg_tensor("ypre", [128, 8 * NT], F32), yg[:].rearrange("p a b -> p (a b)"), reads=["yg"], writes=["dbg_ypre"])
                    gt = sb(pc, "gt", [128, NT], F32)
                    ygb = ymain
                    for k in range(8):
                        S.op('act', lambda e: e.activation(out=gt[:], in_=yg[:, k, :], func=AF.Square), reads=["yg"], writes=["gt"])
                        S.op('dve', lambda e: e.tensor_scalar(out=gt[:], in0=gt[:], scalar1=0.044715, scalar2=1.0, op0=ALU.mult, op1=ALU.add), reads=["gt"], writes=["gt"])
                        S.op('dve', lambda e: e.tensor_tensor(out=gt[:], in0=gt[:], in1=yg[:, k, :], op=ALU.mult), reads=["gt", "yg"], writes=["gt"])
                        S.op('act', lambda e: e.activation(out=gt[:], in_=gt[:], func=AF.Sigmoid, scale=1.5957691216057308), reads=["gt"], writes=["gt"])
                        S.op('dve', lambda e: e.tensor_tensor(out=ygb[:, k, :], in0=gt[:], in1=yg[:, k, :], op=ALU.mult), reads=["gt", "yg"], writes=["ymain"])
                    S.barrier()
                    chk("C3")

                with ExitStack() as pg:
                    wa = [sb(pg, "wa%d" % i, [128, 8, 128], BF16) for i in range(2)]
                    wb = [sb(pg, "wb%d" % i, [128, 8, 128], BF16) for i in range(2)]
                    sg = sb(pg, "sg", [128, 512], F32)
                    for fc in range(8):
                        b = fc % 2
                        load_w_bf16(wa[b][:], w_glu_d[:, fc * 128:(fc + 1) * 128].rearrange("(kc p) c -> p kc c", p=128), "wa%d" % b)
                        load_w_bf16(wb[b][:], w_glu_d[:, 1024 + fc * 128:1024 + (fc + 1) * 128].rearrange("(kc p) c -> p kc c", p=128), "wb%d" % b)
                        for half in range(2):
                            sl = slice(half * 512, (half + 1) * 512)
                            pa_, pb_ = 2 * half, 2 * half + 1
                            for kc in range(8):
                                S.op('pe', lambda e: e.matmul(pb[pa_][:, :], lhsT=wa[b][:, kc, :], rhs=ymain[:, kc, sl], start=(kc == 0), stop=(kc == 7)),
                                     reads=["wa%d" % b, "ymain"], writes=[PB[pa_]])
                            for kc in range(8):
                                S.op('pe', lambda e: e.matmul(pb[pb_][:, :], lhsT=wb[b][:, kc, :], rhs=ymain[:, kc, sl], start=(kc == 0), stop=(kc == 7)),
                                     reads=["wb%d" % b, "ymain"], writes=[PB[pb_]])
                            S.op('act', lambda e: e.activation(out=sg[:], in_=pb[pb_][:, :], func=AF.Sigmoid), reads=[PB[pb_]], writes=["sg"])
                            S.op('dve', lambda e: e.tensor_tensor(out=sg[:], in0=pb[pa_][:, :], in1=sg[:], op=ALU.mult), reads=[PB[pa_], "sg"], writes=["sg"])
                            S.op('dve', lambda e: e.tensor_tensor(out=szs[:, fc, sl], in0=sg[:], in1=szs[:, fc, sl], op=ALU.mult), reads=["sg", "szs"], writes=["szs"])
                    if "yssm" in dbg:
                        S.dma('sp', dbg_tensor("yssm", [128, 8 * NT], BF16), szs[:].rearrange("p a b -> p (a b)"), reads=["szs"], writes=["dbg_yssm"])
                    S.barrier()
                    chk("C4")

            ust.close()
            with ExitStack() as ph:
                NK = NCORE * NT
                wuq = sb(ph, "wuq", [128, 4, 1024], BF16)
                wuqi = sb(ph, "wuqi", [128, 4, 1024], BF16)
                load_w_bf16(wuq[:], w_uq_d.rearrange("(kc p) c -> p kc c", p=128), "wuq")
                load_w_bf16(wuqi[:], w_uqi_d.rearrange("(kc p) c -> p kc c", p=128), "wuqi")
                kiA = sb(ph, "kiA", [128, NK], BF16)
                KTg = sb(ph, "KTg", [128, NK], BF16)
                Vg = sb(ph, "Vg", [128, 64, 130], BF16)
                Sc = sb(ph, "Sc", [128, NK], F32)
                Mb = sb(ph, "Mb", [128, NK], BF16)
                MT = sb(ph, "MT", [128, 64, 128], BF16)
                qTt = sb(ph, "qTt", [128, 8, 128], BF16)
                qiT = sb(ph, "qiT", [128, 8, 128], BF16)
                Dg = sb(ph, "Dg", [128, 16, 128], BF16)
                cbias = sb(ph, "cbias", [128, 1024], F32)
                rl = [sb(ph, "rl%d" % i, [128, 512], BF16) for i in range(3)]
                Eb = [sb(ph, "Eb%d" % i, [128, 512], BF16) for i in range(2)]
                Pb = [sb(ph, "Pb%d" % i, [128, 512], BF16) for i in range(2)]
                bs = sb(ph, "bs", [128, 16], F32)
                bcnt = sb(ph, "bcnt", [128, 1], F32)
                bcn2 = sb(ph, "bcn2", [128, 1], F32)
                bnm = sb(ph, "bnm", [128, 1], F32)
                lomin = sb(ph, "lomin", [128, 16], F32)
                osb = sb(ph, "osb", [128, 128], BF16)
                rec = sb(ph, "rec", [128, 1], F32)
                nc.sync.wait_ge(cc_sem, ncc[0])
                for half in range(2):
                    for r in range(NCORE):
                        S.dma('sp', kiA[half * 64:(half + 1) * 64, :].rearrange("p (m r t) -> p m r t", r=NCORE, t=128)[:, :, r, :],
                              ki_gat[r * 64:(r + 1) * 64, :].rearrange("p (m t) -> p m t", t=128), writes=["kiA"])
                S.op('pool', lambda e: e.memset(Vg[:, :, 128:130], 1.0), writes=["Vg"])

                for m in DTILES:
                    nkb = 8 * (m + 1)
                    nk = nkb * 128
                    tsl = slice(m * 128, (m + 1) * 128)
                    chk("D0")
                    for (wmat, wk_, dst, dk) in [(wuq, "wuq", qTt, "qTt"), (wuqi, "wuqi", qiT, "qiT")]:
                        for hb in range(2):
                            for c in range(4):
                                fc = hb * 4 + c
                                for kc in range(4):
                                    S.op('pe', lambda e: e.matmul(pb[hb][:, c * 128:(c + 1) * 128], lhsT=wmat[:, kc, fc * 128:(fc + 1) * 128], rhs=cqn[:, kc, tsl],
                                                                  start=(kc == 0), stop=(kc == 3)), reads=[wk_, "cqn"], writes=[PB[hb]])
                            copy_evac(dst[:, hb * 4:(hb + 1) * 4, :].rearrange("p a b -> p (a b)"), pb[hb][:, :], [PB[hb]], [dk])
                    for h in range(16):
                        S.op('dve', lambda e: e.tensor_scalar(out=Dg[:, h, :], in0=ident[:], scalar1=wq[:, m, h:h + 1], scalar2=None, op0=ALU.mult),
                             reads=["ident", "wq"], writes=["Dg"])
                    S.op('pool', lambda e: e.iota(cbias[:], pattern=[[1, 1024]], base=(nkb - 8) * 128, channel_multiplier=0, allow_small_or_imprecise_dtypes=True),
                         writes=["cbias"])
                    S.op('dve', lambda e: e.tensor_scalar(out=cbias[:], in0=cbias[:], scalar1=qpos[:, m:m + 1], scalar2=None, op0=ALU.is_gt),
                         reads=["cbias", "qpos"], writes=["cbias"])
                    S.op('dve', lambda e: e.tensor_scalar(out=cbias[:], in0=cbias[:], scalar1=-1e30, scalar2=None, op0=ALU.mult), reads=["cbias"], writes=["cbias"])
                    nch = nk // 512
                    for ch in range(nch):
                        ksl = slice(ch * 512, (ch + 1) * 512)
                        sp = 2 + ch % 2
                        for h in range(16):
                            rp = h % 2
                            po = (h % 2) * 64
                            rb_ = h % 3
                            S.op('pe', lambda e: e.matmul(pb[rp][:, :], lhsT=qiT[po:po + 64, h // 2, :], rhs=kiA[po:po + 64, ksl], start=True, stop=True),
                                 reads=["qiT", "kiA"], writes=[PB[rp]])
                            S.op('act', lambda e: e.activation(out=rl[rb_][:], in_=pb[rp][:, :], func=AF.Relu), reads=[PB[rp]], writes=["rl%d" % rb_])
                            S.op('pe', lambda e: e.matmul(pb[sp][:, :], lhsT=Dg[:, h, :], rhs=rl[rb_][:], start=(h == 0), stop=(h == 15)),
                                 reads=["Dg", "rl%d" % rb_], writes=[PB[sp]])
                        S.op('dve', lambda e: e.tensor_reduce(out=lomin[:, ch:ch + 1], in_=pb[sp][:, :], axis=AX.X, op=ALU.min), reads=[PB[sp]], writes=["lomin"])
                        if ch >= nch - 2:
                            off = (ch - (nch - 2)) * 512
                            S.op('dve', lambda e: e.tensor_tensor(out=Sc[:, ksl], in0=pb[sp][:, :], in1=cbias[:, off:off + 512], op=ALU.add),
                                 reads=[PB[sp], "cbias"], writes=["Sc"])
                        else:
                            S.op('act', lambda e: e.activation(out=Sc[:, ksl], in_=pb[sp][:, :], func=AF.Copy), reads=[PB[sp]], writes=["Sc"])
                    chk("D2")
                    BS = "bs"
                    LO, HI, MID, PR, T1, T2 = [bs[:, i:i + 1] for i in range(6)]
                    CNT = bcnt[:, 0:1]
                    CN2 = bcn2[:, 0:1]
                    NM = bnm[:, 0:1]

                    def bop(fn, extra=(), wr=()):
                        S.op('dve', fn, reads=[BS] + list(extra), writes=[BS] + list(wr))

                    bop(lambda e: e.tensor_reduce(out=LO, in_=lomin[:, 0:nch], axis=AX.X, op=ALU.min), ["lomin"])
                    bop(lambda e: e.tensor_reduce(out=HI, in_=Sc[:, 0:nk], axis=AX.X, op=ALU.max), ["Sc"])
                    ndve = (nk * 7 // 16) // 128 * 128
                    for it in range(NBIS):
                        bop(lambda e: e.tensor_tensor(out=MID, in0=LO, in1=HI, op=ALU.add))
                        bop(lambda e: e.tensor_scalar(out=MID, in0=MID, scalar1=0.5, scalar2=None, op0=ALU.mult))
                        bop(lambda e: e.tensor_scalar(out=NM, in0=MID, scalar1=-1.0, scalar2=None, op0=ALU.mult), ["bnm"], ["bnm"])
                        S.op('act', lambda e: e.activation(out=Mb[:, ndve:nk], in_=Sc[:, ndve:nk], func=AF.Sign, bias=NM, scale=1.0, accum_out=CN2),
                             reads=["Sc", "bnm"], writes=["Mb_hi", "bcn2"])
                        S.op('dve', lambda e: e.tensor_scalar(out=Mb[:, 0:ndve], in0=Sc[:, 0:ndve], scalar1=MID, scalar2=0.0, op0=ALU.is_ge, op1=ALU.add, accum_out=CNT),
                             reads=["Sc", BS], writes=["Mb_lo", "bcnt"])
                        bop(lambda e: e.tensor_scalar(out=T1, in0=CN2, scalar1=float(nk - ndve), scalar2=0.5, op0=ALU.add, op1=ALU.mult), ["bcn2"])
                        bop(lambda e: e.tensor_tensor(out=T1, in0=T1, in1=CNT, op=ALU.add), ["bcnt"])
                        bop(lambda e: e.tensor_scalar(out=PR, in0=T1, scalar1=float(TOPK) - 0.25, scalar2=None, op0=ALU.is_ge))
                        bop(lambda e: e.tensor_tensor(out=T1, in0=MID, in1=LO, op=ALU.subtract))
                        bop(lambda e: e.tensor_tensor(out=T2, in0=HI, in1=MID, op=ALU.subtract))
                        bop(lambda e: e.scalar_tensor_tensor(out=LO, in0=T1, scalar=PR, in1=LO, op0=ALU.mult, op1=ALU.add))
                        bop(lambda e: e.scalar_tensor_tensor(out=HI, in0=T2, scalar=PR, in1=MID, op0=ALU.mult, op1=ALU.add))
                    S.op('dve', lambda e: e.tensor_scalar(out=Mb[:, 0:nk], in0=Sc[:, 0:nk], scalar1=LO, scalar2=None, op0=ALU.is_ge),
                         reads=["Sc", BS], writes=["Mb_lo", "Mb_hi"])
                    if "mask" in dbg and m == dbg_m:
                        S.dma('sp', dbg_tensor("mask", [128, NK], BF16)[:, 0:nk], Mb[:, 0:nk], reads=["Mb_lo", "Mb_hi"], writes=["dbg_mask"])
                        S.dma('sp', dbg_tensor("score", [128, NK], F32)[:, 0:nk], Sc[:, 0:nk], reads=["Sc"], writes=["dbg_score"])
                    chk("D3")
                    for g8 in range(nkb // 8):
                        pti = g8 % 2
                        for c in range(8):
                            kb = g8 * 8 + c
                            S.op('pe', lambda e: e.transpose(out=pt[pti][:, c * 128:(c + 1) * 128], in_=Mb[:, kb * 128:(kb + 1) * 128], identity=ident[:]),
                                 reads=["Mb_lo", "Mb_hi", "ident"], writes=[PT[pti]])
                        copy_evac(MT[:, g8 * 8:(g8 + 1) * 8, :].rearrange("p a b -> p (a b)"), pt[pti][:, :], [PT[pti]], ["MT"])
                    chk("D4")
                    for g in range(2):
                        for r in range(NCORE):
                            S.dma('sp', KTg[:, 0:nk].rearrange("p (m r t) -> p m r t", r=NCORE, t=128)[:, :, r, :],
                                  kT_gat[r * 256 + g * 128:r * 256 + (g + 1) * 128, 0:(m + 1) * 128].rearrange("p (m t) -> p m t", t=128),
                                  writes=["KTg"])
                            S.dma('sp', Vg[:, 0:nkb, 0:128].rearrange("p (m r) d -> p m r d", r=NCORE)[:, :, r, :],
                                  v_gat[r * NT:r * NT + (m + 1) * 128, g * 128:(g + 1) * 128].rearrange("(m p) d -> p m d", p=128),
                                  writes=["Vg"])
                        for kb in range(nkb):
                            lp = 4 + kb % 2
                            eb = kb % 2
                            S.op('pe', lambda e: e.matmul(pb[lp][:, :], lhsT=KTg[:, kb * 128:(kb + 1) * 128], rhs=qTt[:, 4 * g:4 * g + 4, :].rearrange("p a b -> p (a b)"),
                                                          start=True, stop=True), reads=["KTg", "qTt"], writes=[PB[lp]])
                            S.op('act', lambda e: e.activation(out=Eb[eb][:], in_=pb[lp][:, :], func=AF.Exp, scale=128.0 ** -0.5), reads=[PB[lp]], writes=["Eb%d" % eb])
                            me = 'dve' if kb % 2 == 0 else 'pool'
                            S.op(me, lambda e: e.tensor_tensor(out=Pb[eb][:].rearrange("p (a b) -> p a b", b=128), in0=Eb[eb][:].rearrange("p (a b) -> p a b", b=128),
                                                               in1=MT[:, kb:kb + 1, :].to_broadcast([128, 4, 128]), op=ALU.mult),
                                 reads=["Eb%d" % eb, "MT"], writes=["Pb%d" % eb])
                            for hh in range(4):
                                S.op('pe', lambda e: e.matmul(pb[hh][:, 0:129], lhsT=Pb[eb][:, hh * 128:(hh + 1) * 128], rhs=Vg[:, kb, 0:129],
                                                              start=(kb == 0), stop=(kb == nkb - 1)), reads=["Pb%d" % eb, "Vg"], writes=[PB[hh]])
                        for hh in range(4):
                            hd = 4 * g + hh
                            S.op('dve', lambda e: e.reciprocal(out=rec[:], in_=pb[hh][:, 128:129]), reads=[PB[hh]], writes=["rec"])
                            S.op('dve', lambda e: e.tensor_scalar(out=osb[:], in0=pb[hh][:, 0:128], scalar1=rec[:, 0:1], scalar2=None, op0=ALU.mult),
                                 reads=[PB[hh], "rec"], writes=["osb"])
                            S.op('pe', lambda e: e.transpose(out=pt[0][:, 0:128], in_=osb[:], identity=ident[:]), reads=["osb", "ident"], writes=[PT[0]])
                            S.op('dve', lambda e: e.tensor_tensor(out=sza[:, hd, tsl], in0=pt[0][:, 0:128], in1=sza[:, hd, tsl], op=ALU.mult),
                                 reads=[PT[0], "sza"], writes=["sza"])
                if "yattn" in dbg:
                    S.dma('sp', dbg_tensor("yattn", [128, 8 * NT], BF16), sza[:].rearrange("p a b -> p (a b)"), reads=["sza"], writes=["dbg_yattn"])
                S.barrier()
                chk("D")

            with ExitStack() as ph:
                mts = ph.enter_context(ExitStack())
                mT = sb(mts, "mT", [128, 16, NT], BF16)
                with ExitStack() as p1:
                    hT = sb(p1, "hT2", [128, 16, NT], BF16)
                    with ExitStack() as pa:
                        norm_transpose(pa, "A2", x_tile_loader(pa, "A2", x_d, D), gm, hT, "hT2", "gm")
                        S.barrier()
                    wso = [sb(p1, "wso%d" % i, [128, 8, 128], BF16) for i in range(2)]
                    wao = [sb(p1, "wao%d" % i, [128, 8, 128], BF16) for i in range(2)]
                    wgs = [sb(p1, "wgs%d" % i, [128, 16, 128], BF16) for i in range(2)]
                    wga = [sb(p1, "wga%d" % i, [128, 16, 128], BF16) for i in range(2)]
                    s1 = sb(p1, "s1", [128, 512], F32)
                    s2 = sb(p1, "s2", [128, 512], F32)
                    for fc in range(16):
                        b = fc % 2
                        cs = slice(fc * 128, (fc + 1) * 128)
                        load_w_bf16(wso[b][:], w_so_d[:, cs].rearrange("(kc p) c -> p kc c", p=128), "wso%d" % b)
                        load_w_bf16(wao[b][:], w_ao_d[:, cs].rearrange("(kc p) c -> p kc c", p=128), "wao%d" % b)
                        load_w_bf16(wgs[b][:], w_in_d[:, 4176 + fc * 128:4176 + (fc + 1) * 128].rearrange("(kc p) c -> p kc c", p=128), "wgs%d" % b)
                        load_w_bf16(wga[b][:], w_in_d[:, 6224 + fc * 128:6224 + (fc + 1) * 128].rearrange("(kc p) c -> p kc c", p=128), "wga%d" % b)
                        for half in range(2):
                            sl = slice(half * 512, (half + 1) * 512)
                            for kc in range(8):
                                S.op('pe', lambda e: e.matmul(pb[0][:, :], lhsT=wso[b][:, kc, :], rhs=szs[:, kc, sl], start=(kc == 0), stop=(kc == 7)),
                                     reads=["wso%d" % b, "szs"], writes=[PB[0]])
                            for kc in range(8):
                                S.op('pe', lambda e: e.matmul(pb[1][:, :], lhsT=wao[b][:, kc, :], rhs=sza[:, kc, sl], start=(kc == 0), stop=(kc == 7)),
                                     reads=["wao%d" % b, "sza"], writes=[PB[1]])
                            for kc in range(16):
                                S.op('pe', lambda e: e.matmul(pb[2][:, :], lhsT=wgs[b][:, kc, :], rhs=hT[:, kc, sl], start=(kc == 0), stop=(kc == 15)),
                                     reads=["wgs%d" % b, "hT2"], writes=[PB[2]])
                            for kc in range(16):
                                S.op('pe', lambda e: e.matmul(pb[3][:, :], lhsT=wga[b][:, kc, :], rhs=hT[:, kc, sl], start=(kc == 0), stop=(kc == 15)),
                                     reads=["wga%d" % b, "hT2"], writes=[PB[3]])
                            S.op('act', lambda e: e.activation(out=s1[:], in_=pb[2][:, :], func=AF.Sigmoid), reads=[PB[2]], writes=["s1"])
                            S.op('act', lambda e: e.activation(out=s2[:], in_=pb[3][:, :], func=AF.Sigmoid), reads=[PB[3]], writes=["s2"])
                            S.op('dve', lambda e: e.tensor_tensor(out=s1[:], in0=pb[0][:, :], in1=s1[:], op=ALU.mult), reads=[PB[0], "s1"], writes=["s1"])
                            S.op('dve', lambda e: e.tensor_tensor(out=s2[:], in0=pb[1][:, :], in1=s2[:], op=ALU.mult), reads=[PB[1], "s2"], writes=["s2"])
                            S.op('pool', lambda e: e.tensor_tensor(out=mT[:, fc, sl], in0=s1[:], in1=s2[:], op=ALU.add), reads=["s1", "s2"], writes=["mT"])
                    if "merged" in dbg:
                        S.dma('sp', dbg_tensor("merged", [128, 16 * NT], BF16), mT[:].rearrange("p a b -> p (a b)"), reads=["mT"], writes=["dbg_merged"])
                    S.barrier()
                pers.close()

                x2 = sb(ph, "x2", [128, TPC, D], F32)
                with ExitStack() as p2:
                    wos = [sb(p2, "wos%d" % i, [128, 16, 512], BF16) for i in range(2)]
                    xq = [sb(p2, "xq%d" % i, [128, 512], F32) for i in range(4)]
                    for fb in range(4):
                        b = fb % 2
                        fs = slice(fb * 512, (fb + 1) * 512)
                        load_w_bf16(wos[b][:], w_o_d[:, fs].rearrange("(kc p) c -> p kc c", p=128), "wos%d" % b)
                        for m in range(TPC):
                            pi = m % 4
                            for kc in range(16):
                                S.op('pe', lambda e: e.matmul(pb[pi][:, :], lhsT=mT[:, kc, m * 128:(m + 1) * 128], rhs=wos[b][:, kc, :], start=(kc == 0), stop=(kc == 15)),
                                     reads=["mT", "wos%d" % b], writes=[PB[pi]])
                            S.dma('sp', xq[pi][:], x_d[m * 128:(m + 1) * 128, fs], writes=["xq%d" % pi])
                            S.op('dve', lambda e: e.tensor_tensor(out=x2[:, m, fs], in0=pb[pi][:, :], in1=xq[pi][:], op=ALU.add),
                                 reads=[PB[pi], "xq%d" % pi], writes=["x2"])
                    if "x2" in dbg:
                        S.dma('sp', dbg_tensor("x2", [128, TPC * D], F32), x2[:].rearrange("p a b -> p (a b)"), reads=["x2"], writes=["dbg_x2"])
                    S.barrier()

                mts.close()
                gate = sb(ph, "gate", [128, TPC, D], BF16)
                with ExitStack() as p3:
                    xnT = sb(p3, "xnT", [128, 16, NT], BF16)
                    with ExitStack() as pa:
                        def get_x2(m):
                            return x2[:, m, :], "x2"
                        norm_transpose(pa, "A3", get_x2, gpl, xnT, "xnT", "gpl")
                        S.barrier()
                    wps = [sb(p3, "wps%d" % i, [128, 16, 512], BF16) for i in range(2)]
                    for fb in range(4):
                        b = fb % 2
                        fs = slice(fb * 512, (fb + 1) * 512)
                        load_w_bf16(wps[b][:], w_pg_d[:, fs].rearrange("(kc p) c -> p kc c", p=128), "wps%d" % b)
                        for m in range(TPC):
                            pi = m % 4
                            for kc in range(16):
                                S.op('pe', lambda e: e.matmul(pb[pi][:, :], lhsT=xnT[:, kc, m * 128:(m + 1) * 128], rhs=wps[b][:, kc, :], start=(kc == 0), stop=(kc == 15)),
                                     reads=["xnT", "wps%d" % b], writes=[PB[pi]])
                            S.op('act', lambda e: e.activation(out=gate[:, m, fs], in_=pb[pi][:, :], func=AF.Sigmoid), reads=[PB[pi]], writes=["gate"])
                    S.barrier()

                with ExitStack() as p4:
                    wpl = sb(p4, "wpl", [128, 2, D], BF16)
                    load_w_bf16(wpl[:], w_ple_d.rearrange("(kc p) c -> p kc c", p=128), "wpl")
                    gpp = sb(p4, "gpp", [128, D], F32)
                    gfin = sb(p4, "gfin", [128, D], F32)
                    grow = sb(p4, "grow", [1, 2 * D], F32)
                    onesf = sb(p4, "onesf", [1, 128], F32)
                    S.op('dve', lambda e: e.memset(onesf[:], 1.0), writes=["onesf"])
                    S.dma('sp', grow[0:1, 0:D], g_pp_d, writes=["grow"])
                    S.dma('sp', grow[0:1, D:2 * D], g_fin_d, writes=["grow"])
                    for gi, (gdst, gk) in enumerate([(gpp, "gpp"), (gfin, "gfin")]):
                        for fb in range(4):
                            S.op('pe', lambda e: e.matmul(pb[fb][:, :], lhsT=onesf[0:1, :], rhs=grow[0:1, gi * D + fb * 512:gi * D + (fb + 1) * 512], start=True, stop=True),
                                 reads=["onesf", "grow"], writes=[PB[fb]])
                            S.op('dve', lambda e: e.tensor_copy(out=gdst[:, fb * 512:(fb + 1) * 512], in_=pb[fb][:, :]), reads=[PB[fb]], writes=[gk])
                    pf = [sb(p4, "pf%d" % i, [128, 256], F32) for i in range(2)]
                    pbf = sb(p4, "pbf", [128, 256], BF16)
                    pT = sb(p4, "pT", [128, 2, 128], BF16)
                    er = sb(p4, "er", [128, D], F32)
                    ej = sb(p4, "ej", [128, D], BF16)
                    ob = [sb(p4, "ob%d" % i, [128, D], F32) for i in range(2)]
                    st4 = sb(p4, "st4", [128, 2 * TPC], F32)
                    for m in range(TPC):
                        b = m % 2
                        S.dma('sp', pf[b][:], p_d[m * 128:(m + 1) * 128, :], writes=["pf%d" % b])
                        S.op('dve', lambda e: e.tensor_copy(out=pbf[:], in_=pf[b][:]), reads=["pf%d" % b], writes=["pbf"])
                        for c in range(2):
                            S.op('pe', lambda e: e.transpose(out=pt[0][:, c * 128:(c + 1) * 128], in_=pbf[:, c * 128:(c + 1) * 128], identity=ident[:]),
                                 reads=["pbf", "ident"], writes=[PT[0]])
                        S.op('act', lambda e: e.activation(out=pT[:].rearrange("p a b -> p (a b)"), in_=pt[0][:, 0:256], func=AF.Copy), reads=[PT[0]], writes=["pT"])
                        for fb in range(4):
                            for kc in range(2):
                                S.op('pe', lambda e: e.matmul(pb[fb][:, :], lhsT=pT[:, kc, :], rhs=wpl[:, kc, fb * 512:(fb + 1) * 512], start=(kc == 0), stop=(kc == 1)),
                                     reads=["pT", "wpl"], writes=[PB[fb]])
                        for fb in range(4):
                            copy_evac(er[:, fb * 512:(fb + 1) * 512], pb[fb][:, :], [PB[fb]], ["er"])
                        S.op('act', lambda e: e.activation(out=ej[:], in_=er[:], func=AF.Square, accum_out=st4[:, 2 * m:2 * m + 1]), reads=["er"], writes=["ej", "st4"])
                        rstd_cols(st4[:, 2 * m:2 * m + 1], "st4", 1.0 / D)
                        S.op('dve', lambda e: e.scalar_tensor_tensor(out=er[:], in0=er[:], scalar=st4[:, 2 * m:2 * m + 1], in1=gpp[:], op0=ALU.mult, op1=ALU.mult),
                             reads=["er", "st4", "gpp"], writes=["er"])
                        S.op('pool', lambda e: e.tensor_tensor(out=er[:], in0=er[:], in1=gate[:, m, :], op=ALU.mult), reads=["er", "gate"], writes=["er"])
                        S.op('dve', lambda e: e.tensor_tensor(out=er[:], in0=er[:], in1=x2[:, m, :], op=ALU.add), reads=["er", "x2"], writes=["er"])
                        S.op('act', lambda e: e.activation(out=ej[:], in_=er[:], func=AF.Square, accum_out=st4[:, 2 * m + 1:2 * m + 2]), reads=["er"], writes=["ej", "st4"])
                        rstd_cols(st4[:, 2 * m + 1:2 * m + 2], "st4", 1.0 / D)
                        S.op('dve', lambda e: e.scalar_tensor_tensor(out=ob[b][:], in0=er[:], scalar=st4[:, 2 * m + 1:2 * m + 2], in1=gfin[:], op0=ALU.mult, op1=ALU.mult),
                             reads=["er", "st4", "gfin"], writes=["ob%d" % b])
                        S.dma('sp', y_d[m * 128:(m + 1) * 128, :], ob[b][:], reads=["ob%d" % b], writes=["y"])
                    S.barrier()

        except _Stop:
            pass
        S.barrier()
    return nc, dbg_out


dbg_m = 7
import os as _os
DTILES = [int(v) for v in _os.environ.get("DTILES", "0,1,2,3,4,5,6,7").split(",") if v != ""]
_CACHE = {}


def _shard_inputs(inputs):
    f = lambda a: np.ascontiguousarray(np.asarray(a, dtype=np.float32))
    x = f(inputs["x"])[0].reshape(NCORE * TPC, 128, D)
    p = f(inputs["p"])[0, 0].reshape(NCORE * TPC, 128, 256)
    common = {
        "g_mix": f(inputs["g_mix"]).reshape(1, D), "w_in": f(inputs["w_in"])[0],
        "g_q": f(inputs["g_q"]).reshape(1, 512), "w_uq": f(inputs["w_uq"])[0], "w_uq_idx": f(inputs["w_uq_idx"])[0],
        "g_kidx": f(inputs["g_kidx"]).reshape(1, 64),
        "a_re": f(inputs["a_re"])[0].reshape(32, 128), "a_im": f(inputs["a_im"])[0].reshape(32, 128),
        "log_dt": f(inputs["log_dt"])[0].reshape(32, 2),
        "b_re": f(inputs["b_re"])[0].reshape(4096, 16), "b_im": f(inputs["b_im"])[0].reshape(4096, 16),
        "c_re": f(inputs["c_re"])[0], "c_im": f(inputs["c_im"])[0],
        "d_skip": f(inputs["d_skip"]).reshape(1, 1024),
        "w_glu": f(inputs["w_glu"])[0], "w_ssm_out": f(inputs["w_ssm_out"])[0], "w_attn_out": f(inputs["w_attn_out"])[0],
        "w_o": f(inputs["w_o"])[0], "g_ple": f(inputs["g_ple"]).reshape(1, D), "w_ple_gate": f(inputs["w_ple_gate"])[0],
        "w_ple": f(inputs["w_ple"])[0], "g_ple_post": f(inputs["g_ple_post"]).reshape(1, D), "g_final": f(inputs["g_final"]).reshape(1, D),
    }
    maps = []
    for i in range(NCORE):
        mp = dict(common)
        mp["x"] = np.ascontiguousarray(x[i::NCORE].reshape(NT, D))
        mp["p"] = np.ascontiguousarray(p[i::NCORE].reshape(NT, 256))
        qp = np.zeros((128, TPC), np.float32)
        for m in range(TPC):
            qp[:, m] = (NCORE * m + i) * 128 + np.arange(128)
        mp["qpos"] = qp
        ohm = np.zeros((128, NCORE), np.float32)
        ohm[:, i] = 1.0
        mp["onehot"] = ohm
        maps.append(mp)
    return maps


def _assemble(res, name="y", width=D):
    out = np.zeros((NCORE * TPC, 128, width), np.float32)
    for i in range(NCORE):
        out[i::NCORE] = np.asarray(res.results[i][name]).reshape(TPC, 128, width)
    return out.reshape(1, NCORE * NT, width)


def kernel(**inputs):
    if "nc" not in _CACHE:
        _CACHE["nc"] = build_nc()[0]
    nc = _CACHE["nc"]
    maps = _shard_inputs(inputs)
    res = run_bass_kernel_spmd(nc, maps, core_ids=list(range(NCORE)))
    return _assemble(res)
```

```python
import math
import numpy as np
from contextlib import ExitStack
import concourse.bass as bass
import concourse.mybir as mybir
from concourse.bass_utils import run_bass_kernel_spmd

F32 = mybir.dt.float32
BF16 = mybir.dt.bfloat16
AF = mybir.ActivationFunctionType
ALU = mybir.AluOpType
AX = mybir.AxisListType

NCORE = 8
TPC = 8
NT = 1024
D = 2048
INW = 8272
EPS = 1e-6
TOPK = 256
NBIS = 18
PI = math.pi


class Sched:
    def __init__(self, nc, es, ndma=28, needed=None):
        self.nc = nc
        self.needed = needed
        self.rec = set()
        self.sigcnt = {}
        self.sigval = {}
        self.engs = {'pe': nc.tensor, 'act': nc.scalar, 'dve': nc.vector, 'pool': nc.gpsimd, 'sp': nc.sync}
        self.semh = []
        self.esem = {}
        for e in ['pe', 'act', 'dve', 'pool']:
            self.esem[e] = len(self.semh)
            self.semh.append(es.enter_context(nc.semaphore("s_" + e)))
        self.cnt = {e: 0 for e in self.engs}
        self.seen = {e: {} for e in self.engs}
        self.last_w = {}
        self.readers = {}
        self.dsem = []
        for i in range(ndma):
            self.dsem.append(len(self.semh))
            self.semh.append(es.enter_context(nc.semaphore("s_dma%d" % i)))
        self.duse = [0] * ndma
        self.di = 0

    def _deps(self, reads, writes):
        deps = []
        for k in reads:
            lw = self.last_w.get(k)
            if lw:
                deps.append(lw)
            if k[:2] in ("pb", "pt") and len(k) == 3:
                deps.extend(self.readers.get(k, {}).items())
        for k in writes:
            lw = self.last_w.get(k)
            if lw:
                deps.append(lw)
            deps.extend(self.readers.get(k, {}).items())
        return deps

    def _wait(self, e, deps):
        best = {}
        for s, v in deps:
            if v > best.get(s, 0):
                best[s] = v
        for s, v in best.items():
            if e == 'pe' and s == self.esem['pe']:
                continue
            if self.seen[e].get(s, 0) >= v:
                continue
            if s in self.esem.values():
                self.rec.add((s, v))
                self.engs[e].wait_ge(self.semh[s], self.sigval[(s, v)])
            else:
                self.engs[e].wait_ge(self.semh[s], v)
            self.seen[e][s] = v

    def _upd(self, tok, reads, writes):
        for k in reads:
            d = self.readers.setdefault(k, {})
            if tok[1] > d.get(tok[0], 0):
                d[tok[0]] = tok[1]
        for k in writes:
            self.last_w[k] = tok
            self.readers[k] = {}

    def op(self, e, fn, reads=(), writes=()):
        self._wait(e, self._deps(reads, writes))
        inst = fn(self.engs[e])
        self.cnt[e] += 1
        tok = (self.esem[e], self.cnt[e])
        if self.needed is None or tok in self.needed:
            self.sigcnt[e] = self.sigcnt.get(e, 0) + 1
            inst.then_inc(self.semh[self.esem[e]], 1)
            self.sigval[tok] = self.sigcnt[e]
        self._upd(tok, reads, writes)
        return tok

    def dma(self, q, out, in_, reads=(), writes=(), **kw):
        i = self.di
        self.di = (i + 1) % len(self.dsem)
        deps = self._deps(reads, writes)
        if self.duse[i]:
            deps.append((self.dsem[i], 16 * self.duse[i]))
        self._wait(q, deps)
        inst = self.engs[q].dma_start(out=out, in_=in_, **kw)
        self.duse[i] += 1
        inst.then_inc(self.semh[self.dsem[i]], 16)
        tok = (self.dsem[i], 16 * self.duse[i])
        self._upd(tok, reads, writes)
        return tok

    def wait_keys(self, e, keys):
        deps = []
        for k in keys:
            lw = self.last_w.get(k)
            if lw:
                deps.append(lw)
            deps.extend(self.readers.get(k, {}).items())
        self._wait(e, deps)

    def barrier(self):
        deps = [(self.esem[e], self.cnt[e]) for e in ['pe', 'act', 'dve', 'pool'] if self.cnt[e]]
        deps += [(self.dsem[i], 16 * self.duse[i]) for i in range(len(self.dsem)) if self.duse[i]]
        for e in self.engs:
            self._wait(e, deps)


class Arena:
    def __init__(self, nbytes):
        self.free = [(0, nbytes)]

    def alloc(self, n):
        n = (n + 63) // 64 * 64
        for i, (o, sz) in enumerate(self.free):
            if sz >= n:
                if sz == n:
                    self.free.pop(i)
                else:
                    self.free[i] = (o + n, sz - n)
                return o, n
        raise MemoryError("arena out of memory: need %d, free list %s" % (n, self.free))

    def release(self, o, n):
        self.free.append((o, n))
        self.free.sort()
        out = []
        for (a, b) in self.free:
            if out and out[-1][0] + out[-1][1] == a:
                out[-1] = (out[-1][0], out[-1][1] + b)
            else:
                out.append((a, b))
        self.free = out


ARENA_KB = 204


class _Stop(Exception):
    pass


def build_nc(dbg=None, stop=None):
    _, _, S1 = _build_nc(dbg, stop, None)
    nc, dbg_out, S2 = _build_nc(dbg, stop, S1.rec)
    return nc, dbg_out


def _build_nc(dbg=None, stop=None, needed=None):
    dbg = dbg or ()

    def chk(tag):
        if stop == tag:
            raise _Stop()
    nc = bass.Bass("TRN2", target_bir_lowering=False)

    def din(name, shape):
        return nc.dram_tensor(name, list(shape), F32, kind="ExternalInput").ap()

    x_d = din("x", [NT, D]); p_d = din("p", [NT, 256])
    g_mix_d = din("g_mix", [1, D]); w_in_d = din("w_in", [D, INW])
    g_q_d = din("g_q", [1, 512]); w_uq_d = din("w_uq", [512, 1024]); w_uqi_d = din("w_uq_idx", [512, 1024])
    g_kidx_d = din("g_kidx", [1, 64])
    a_re_d = din("a_re", [32, 128]); a_im_d = din("a_im", [32, 128]); log_dt_d = din("log_dt", [32, 2])
    b_re_d = din("b_re", [4096, 16]); b_im_d = din("b_im", [4096, 16])
    c_re_d = din("c_re", [64, 16, 64]); c_im_d = din("c_im", [64, 16, 64])
    d_skip_d = din("d_skip", [1, 1024])
    w_glu_d = din("w_glu", [1024, 2048]); w_so_d = din("w_ssm_out", [1024, 2048]); w_ao_d = din("w_attn_out", [1024, 2048])
    w_o_d = din("w_o", [D, D]); g_ple_d = din("g_ple", [1, D]); w_pg_d = din("w_ple_gate", [D, D])
    w_ple_d = din("w_ple", [256, D]); g_pp_d = din("g_ple_post", [1, D]); g_fin_d = din("g_final", [1, D])
    qpos_d = din("qpos", [128, TPC]); oh_d = din("onehot", [128, NCORE])
    y_d = nc.dram_tensor("y", [NT, D], F32, kind="ExternalOutput").ap()

    def dint(name, shape, dt):
        return nc.dram_tensor(name, list(shape), dt, kind="Internal").ap()

    kT_loc = dint("kT_loc", [256, NT], BF16); kT_gat = dint("kT_gat", [NCORE * 256, NT], BF16)
    v_loc = dint("v_loc", [NT, 256], BF16); v_gat = dint("v_gat", [NCORE * NT, 256], BF16)
    ki_loc = dint("ki_loc", [64, NT], BF16); ki_gat = dint("ki_gat", [NCORE * 64, NT], BF16)
    e_loc = dint("e_loc", [128, TPC * 64], F32); e_gat = dint("e_gat", [NCORE * 128, TPC * 64], F32)

    dbg_out = {}

    def dbg_tensor(name, shape, dt=F32):
        dbg_out[name] = nc.dram_tensor("dbg_" + name, list(shape), dt, kind="ExternalOutput").ap()
        return dbg_out[name]

    with ExitStack() as es:
        S = Sched(nc, es, needed=needed)
        cc_sem = es.enter_context(nc.semaphore("cc_sem"))

        arena_t = es.enter_context(nc.sbuf_tensor("arena", [128, ARENA_KB * 256], F32))
        arena = Arena(ARENA_KB * 1024)

        def sb(st, name, shape, dt):
            esz = 4 if dt == F32 else 2
            nel = 1
            for d_ in shape[1:]:
                nel *= d_
            off, n = arena.alloc(nel * esz)
            st.callback(arena.release, off, n)
            v = arena_t[:, off // 4:(off + n) // 4]
            if dt != F32:
                v = v.bitcast(dt)
            v = v[0:shape[0], 0:nel]
            if len(shape) == 3:
                v = v.rearrange("p (a b) -> p a b", a=shape[1])
            elif len(shape) == 4:
                v = v.rearrange("p (a b c) -> p a b c", a=shape[1], b=shape[2])
            elif len(shape) == 5:
                v = v.rearrange("p (a b c d) -> p a b c d", a=shape[1], b=shape[2], c=shape[3])
            return v

        pb = [es.enter_context(nc.psum_tensor("pb%d" % i, [128, 512], F32)) for i in range(6)]
        pt = [es.enter_context(nc.psum_tensor("pt%d" % i, [128, 1024], BF16)) for i in range(2)]
        PB = ["pb%d" % i for i in range(6)]
        PT = ["pt0", "pt1"]

        identf = sb(es, "identf", [128, 128], F32)
        ident = sb(es, "ident", [128, 128], BF16)
        ones_b = sb(es, "ones_b", [128, 128], BF16)
        qpos = sb(es, "qpos_s", [128, TPC], F32)
        oh = sb(es, "oh_s", [128, NCORE], F32)
        S.op('pool', lambda e: e.memset(identf[:], 0.0), writes=['identf'])
        S.op('pool', lambda e: e.affine_select(out=identf[:], in_=identf[:], pattern=[[-1, 128]],
                                                compare_op=ALU.not_equal, fill=1.0, base=0, channel_multiplier=1),
             reads=['identf'], writes=['identf'])
        S.op('dve', lambda e: e.tensor_copy(out=ident[:], in_=identf[:]), reads=['identf'], writes=['ident'])
        S.op('dve', lambda e: e.memset(ones_b[:], 1.0), writes=['ones_b'])
        S.dma('sp', qpos[:], qpos_d, writes=['qpos'])
        S.dma('sp', oh[:], oh_d, writes=['oh'])

        def load_pp(st, name, vec_d, n):
            t = sb(st, name, [128, n], F32)
            S.dma('sp', t[:], vec_d.rearrange("o (c p) -> p (o c)", p=128), writes=[name],
                  allow_slow_non_contiguous=True)
            return t

        gm = load_pp(es, "gm", g_mix_d, 16)
        gpl = load_pp(es, "gpl", g_ple_d, 16)
        gq = load_pp(es, "gq", g_q_d, 4)
        dsk = load_pp(es, "dsk", d_skip_d, 8)

        pers = es.enter_context(ExitStack())
        ust = es.enter_context(ExitStack())
        szs = sb(pers, "szs", [128, 8, NT], BF16)
        sza = sb(pers, "sza", [128, 8, NT], BF16)
        cqn = sb(pers, "cqn", [128, 4, NT], BF16)
        wq = sb(pers, "wq", [128, TPC, 16], F32)

        cast_i = [0]

        stg = [sb(es, "stg%d" % i, [128, 1024], F32) for i in range(2)]

        def load_w_bf16(dst_ap, src_ap, key):
            P, A, B = dst_ap.shape
            bs_ = min(B, 1024)
            step = max(1, 1024 // bs_)
            for b0 in range(0, B, bs_):
                for a0 in range(0, A, step):
                    a1 = min(A, a0 + step)
                    i = cast_i[0] % 2
                    cast_i[0] += 1
                    sv = stg[i][0:P, 0:(a1 - a0) * bs_].rearrange("p (a b) -> p a b", b=bs_)
                    S.dma('sp', sv, src_ap[:, a0:a1, b0:b0 + bs_], writes=["stg%d" % i])
                    if cast_i[0] % 3 == 0:
                        S.op('act', lambda e: e.activation(out=dst_ap[:, a0:a1, b0:b0 + bs_], in_=sv, func=AF.Copy), reads=["stg%d" % i], writes=[key])
                    else:
                        S.op('pool', lambda e: e.tensor_copy(out=dst_ap[:, a0:a1, b0:b0 + bs_], in_=sv), reads=["stg%d" % i], writes=[key])

        evac_i = [0]

        def copy_evac(out_ap, in_ap, reads, writes, scale=None):
            evac_i[0] += 1
            if evac_i[0] % 2 == 0 or scale is not None:
                if scale is None:
                    S.op('act', lambda e: e.activation(out=out_ap, in_=in_ap, func=AF.Copy), reads=reads, writes=writes)
                else:
                    S.op('act', lambda e: e.activation(out=out_ap, in_=in_ap, func=AF.Copy, scale=scale), reads=reads, writes=writes)
            else:
                S.op('dve', lambda e: e.tensor_copy(out=out_ap, in_=in_ap), reads=reads, writes=writes)

        def rstd_cols(col_ap, key, inv_n):
            S.op('dve', lambda e: e.tensor_scalar(out=col_ap, in0=col_ap, scalar1=inv_n, scalar2=EPS, op0=ALU.mult, op1=ALU.add),
                 reads=[key], writes=[key])
            S.op('act', lambda e: e.activation(out=col_ap, in_=col_ap, func=AF.Sqrt), reads=[key], writes=[key])
            S.op('dve', lambda e: e.reciprocal(out=col_ap, in_=col_ap), reads=[key], writes=[key])

        def norm_transpose(st, tag, get_tile, gain, dstT, dkey, gkey):
            xn = sb(st, tag + "_xn", [128, D], BF16)
            junk = sb(st, tag + "_junk", [128, D], BF16)
            ssq = sb(st, tag + "_ssq", [128, TPC], F32)
            for m in range(TPC):
                xs, xkey = get_tile(m)
                S.op('act', lambda e: e.activation(out=junk[:], in_=xs, func=AF.Square, accum_out=ssq[:, m:m + 1]),
                     reads=[xkey], writes=[tag + "_junk", tag + "_ssq"])
                rstd_cols(ssq[:, m:m + 1], tag + "_ssq", 1.0 / D)
                S.op('dve', lambda e: e.tensor_scalar(out=xn[:], in0=xs, scalar1=ssq[:, m:m + 1], scalar2=None, op0=ALU.mult),
                     reads=[xkey, tag + "_ssq"], writes=[tag + "_xn"])
                for hb in range(2):
                    for c in range(8):
                        dc = hb * 8 + c
                        S.op('pe', lambda e: e.transpose(out=pt[hb][:, c * 128:(c + 1) * 128], in_=xn[:, dc * 128:(dc + 1) * 128], identity=ident[:]),
                             reads=[tag + "_xn", 'ident'], writes=[PT[hb]])
                    for c in range(8):
                        dc = hb * 8 + c
                        S.op('act', lambda e: e.activation(out=dstT[:, dc, m * 128:(m + 1) * 128], in_=pt[hb][:, c * 128:(c + 1) * 128],
                                                           func=AF.Copy, scale=gain[:, dc:dc + 1]),
                             reads=[PT[hb], gkey], writes=[dkey])

        def x_tile_loader(st, tag, src_d, width):
            bufs = [sb(st, "%s_x%d" % (tag, i), [128, width], F32) for i in range(2)]

            def get(m):
                b = m % 2
                key = "%s_x%d" % (tag, b)
                S.dma('sp', bufs[b][:], src_d[m * 128:(m + 1) * 128, :], writes=[key])
                return bufs[b][:], key
            return get

        def proj_fm(st, tag, w_d, col0, ncols_list, KC, actT, akey, evac):
            wst = [sb(st, "%s_w%d" % (tag, i), [128, KC, 128], BF16) for i in range(2)]
            c0 = col0
            for i, ncol in enumerate(ncols_list):
                b = i % 2
                wkey = "%s_w%d" % (tag, b)
                load_w_bf16(wst[b][:, :, 0:ncol], w_d[:, c0:c0 + ncol].rearrange("(kc p) c -> p kc c", p=128), wkey)
                for half in range(2):
                    pi = (2 * i + half) % 4
                    for kc in range(KC):
                        S.op('pe', lambda e: e.matmul(pb[pi][0:ncol, :], lhsT=wst[b][:, kc, 0:ncol], rhs=actT[:, kc, half * 512:(half + 1) * 512],
                                                      start=(kc == 0), stop=(kc == KC - 1)),
                             reads=[wkey, akey], writes=[PB[pi]])
                    evac(i, half, pb[pi], PB[pi], ncol)
                c0 += ncol

        try:
            with ExitStack() as ph:
                hT = sb(ph, "hT", [128, 16, NT], BF16)
                uT = sb(ust, "uT", [128, 8, NT], BF16)
                kTl = sb(ph, "kTl", [128, 2, NT], BF16)
                cqf = sb(ph, "cqf", [128, 4, NT], F32)
                with ExitStack() as pa:
                    norm_transpose(pa, "A", x_tile_loader(pa, "A", x_d, D), gm, hT, "hT", "gm")
                    S.barrier()
                chk("A")

                def evac_main(i, half, ps, pkey, ncol):
                    sl = slice(half * 512, (half + 1) * 512)
                    if i < 8:
                        copy_evac(uT[:, i, sl], ps[:, :], [pkey], ["uT"])
                    elif i < 16:
                        S.op('act', lambda e: e.activation(out=szs[:, i - 8, sl], in_=ps[:, :], func=AF.Silu), reads=[pkey], writes=["szs"])
                    elif i < 20:
                        copy_evac(cqf[:, i - 16, sl], ps[:, :], [pkey], ["cqf"])
                    else:
                        copy_evac(kTl[:, i - 20, sl], ps[:, :], [pkey], ["kTl"])

                proj_fm(ph, "B1", w_in_d, 0, [128] * 22, 16, hT, "hT", evac_main)

                def evac_za(i, half, ps, pkey, ncol):
                    sl = slice(half * 512, (half + 1) * 512)
                    S.op('act', lambda e: e.activation(out=sza[:, i, sl], in_=ps[:, :], func=AF.Silu), reads=[pkey], writes=["sza"])

                proj_fm(ph, "B2", w_in_d, 3072, [128] * 8, 16, hT, "hT", evac_za)
                chk("B2")
                S.dma('sp', kT_loc.rearrange("(c p) t -> p c t", p=128), kTl[:], reads=["kTl"], writes=["kT_loc"])

                wv = sb(ph, "wv", [128, 16, 256], BF16)
                wkw = sb(ph, "wkw", [128, 16, 80], BF16)
                load_w_bf16(wv[:], w_in_d[:, 2816:3072].rearrange("(kc p) c -> p kc c", p=128), "wv")
                load_w_bf16(wkw[:], w_in_d[:, 4096:4176].rearrange("(kc p) c -> p kc c", p=128), "wkw")
                vtok = sb(ph, "vtok", [128, TPC, 256], BF16)
                kif = sb(ph, "kif", [128, 64], F32)
                kib = sb(ph, "kib", [128, 64], BF16)
                kjunk = sb(ph, "kjunk", [128, 64], BF16)
                kss = sb(ph, "kss", [128, TPC], F32)
                kiT = sb(ph, "kiT", [64, NT], BF16)
                gki = sb(ph, "gki", [64, 1], F32)
                S.dma('sp', gki[:], g_kidx_d.rearrange("o (c p) -> p (o c)", p=64), writes=["gki"], allow_slow_non_contiguous=True)
                for m in range(TPC):
                    for kc in range(16):
                        S.op('pe', lambda e: e.matmul(pb[4][:, 0:256], lhsT=hT[:, kc, m * 128:(m + 1) * 128], rhs=wv[:, kc, :],
                                                      start=(kc == 0), stop=(kc == 15)), reads=["hT", "wv"], writes=[PB[4]])
                    for kc in range(16):
                        S.op('pe', lambda e: e.matmul(pb[5][:, 0:80], lhsT=hT[:, kc, m * 128:(m + 1) * 128], rhs=wkw[:, kc, :],
                                                      start=(kc == 0), stop=(kc == 15)), reads=["hT", "wkw"], writes=[PB[5]])
                    copy_evac(vtok[:, m, :], pb[4][:, 0:256], [PB[4]], ["vtok"])
                    S.op('dve', lambda e: e.tensor_copy(out=kif[:], in_=pb[5][:, 0:64]), reads=[PB[5]], writes=["kif"])
                    S.op('dve', lambda e: e.tensor_scalar(out=wq[:, m, :], in0=pb[5][:, 64:80], scalar1=1.0 / 32.0, scalar2=None, op0=ALU.mult),
                         reads=[PB[5]], writes=["wq"])
                    S.op('act', lambda e: e.activation(out=kjunk[:], in_=kif[:], func=AF.Square, accum_out=kss[:, m:m + 1]),
                         reads=["kif"], writes=["kjunk", "kss"])
                    rstd_cols(kss[:, m:m + 1], "kss", 1.0 / 64)
                    S.op('dve', lambda e: e.tensor_scalar(out=kib[:], in0=kif[:], scalar1=kss[:, m:m + 1], scalar2=None, op0=ALU.mult),
                         reads=["kif", "kss"], writes=["kib"])
                    S.op('pe', lambda e: e.transpose(out=pt[0][0:64, 0:128], in_=kib[:, :], identity=ident[:]),
                         reads=["kib", "ident"], writes=[PT[0]])
                    S.op('act', lambda e: e.activation(out=kiT[:, m * 128:(m + 1) * 128], in_=pt[0][0:64, 0:128], func=AF.Copy, scale=gki[:, 0:1]),
                         reads=[PT[0], "gki"], writes=["kiT"])
                S.dma('sp', v_loc.rearrange("(m p) d -> p m d", p=128), vtok[:], reads=["vtok"], writes=["v_loc"])
                S.dma('sp', ki_loc, kiT[:], reads=["kiT"], writes=["ki_loc"])
                chk("B3")

                S.wait_keys('pool', ["kT_loc", "v_loc", "ki_loc"])
                ncc = [0]
                for (a, b_) in [(kT_loc, kT_gat), (v_loc, v_gat), (ki_loc, ki_gat)]:
                    nc.gpsimd.collective_compute("AllGather", ALU.bypass, replica_groups=[list(range(NCORE))],
                                                 ins=[a], outs=[b_]).then_inc(cc_sem, 1)
                    ncc[0] += 1

                sq = sb(ph, "sq", [128, 4, NT], BF16)
                rsb = sb(ph, "rsb", [128, NT], F32)
                S.op('act', lambda e: e.activation(out=sq[:], in_=cqf[:], func=AF.Square), reads=["cqf"], writes=["sq"])
                for half in range(2):
                    sl = slice(half * 512, (half + 1) * 512)
                    for c in range(4):
                        S.op('pe', lambda e: e.matmul(pb[half][:, :], lhsT=ones_b[:], rhs=sq[:, c, sl], start=(c == 0), stop=(c == 3)),
                             reads=["ones_b", "sq"], writes=[PB[half]])
                    S.op('dve', lambda e: e.tensor_scalar(out=rsb[:, sl], in0=pb[half][:, :], scalar1=1.0 / 512, scalar2=EPS, op0=ALU.mult, op1=ALU.add),
                         reads=[PB[half]], writes=["rsb"])
                S.op('act', lambda e: e.activation(out=rsb[:], in_=rsb[:], func=AF.Sqrt), reads=["rsb"], writes=["rsb"])
                S.op('dve', lambda e: e.reciprocal(out=rsb[:], in_=rsb[:]), reads=["rsb"], writes=["rsb"])
                for c in range(4):
                    S.op('dve', lambda e: e.scalar_tensor_tensor(out=cqn[:, c, :], in0=cqf[:, c, :], scalar=gq[:, c:c + 1], in1=rsb[:],
                                                                op0=ALU.mult, op1=ALU.mult), reads=["cqf", "gq", "rsb"], writes=["cqn"])
                if "uT" in dbg:
                    S.dma('sp', dbg_tensor("uT", [128, 8 * NT], BF16), uT[:].rearrange("p a b -> p (a b)"), reads=["uT"], writes=["dbg_uT"])
                if "cqn" in dbg:
                    S.dma('sp', dbg_tensor("cqn", [128, 4 * NT], BF16), cqn[:].rearrange("p a b -> p (a b)"), reads=["cqn"], writes=["dbg_cqn"])
                S.barrier()
                chk("B5")

            with ExitStack() as ph:
                ptab = ph.enter_context(ExitStack())
                Ct = sb(ptab, "Ct", [128, 32, 128], F32)
                St = sb(ptab, "St", [128, 32, 128], F32)
                W2r = sb(ph, "W2r", [128, 32, 128], BF16)
                W2i = sb(ph, "W2i", [128, 32, 128], BF16)
                CLr = sb(ph, "CLr", [128, 32, 128], BF16)
                CLi = sb(ph, "CLi", [128, 32, 128], BF16)
                LTBr = sb(ph, "LTBr", [128, 32, 128], BF16)
                LTBi = sb(ph, "LTBi", [128, 32, 128], BF16)
                rr = sb(ph, "rr", [128, 32], F32)
                l128r = sb(ph, "l128r", [128, 32], F32)
                l128i = sb(ph, "l128i", [128, 32], F32)
                ymain = sb(ph, "ymain", [128, 8, NT], BF16)
                Eloc = sb(ph, "Eloc", [128, TPC, 32, 2], F32)

                with ExitStack() as pp:
                    nat = sb(pp, "nat", [32, 3, 128], F32)
                    S.dma('sp', nat[:, 0, :], a_re_d, writes=["nat"])
                    S.dma('sp', nat[:, 1, :], a_im_d, writes=["nat"])
                    ld2 = sb(pp, "ld2", [32, 2], F32)
                    S.dma('sp', ld2[:], log_dt_d, writes=["ld2"])
                    S.op('dve', lambda e: e.tensor_copy(out=nat[:, 2, :].rearrange("p (a b) -> p a b", a=2),
                                                        in_=ld2[:, :].unsqueeze(2).to_broadcast([32, 2, 64])),
                         reads=["ld2", "nat"], writes=["nat"])
                    prm = sb(pp, "prm", [128, 3, 32], F32)
                    for q in range(3):
                        S.op('pe', lambda e: e.transpose(out=pb[0][:, q * 32:(q + 1) * 32], in_=nat[:, q, :], identity=identf[0:32, 0:32]),
                             reads=["nat", "identf"], writes=[PB[0]])
                    S.op('dve', lambda e: e.tensor_copy(out=prm[:].rearrange("p a b -> p (a b)"), in_=pb[0][:, 0:96]), reads=[PB[0]], writes=["prm"])
                    ar = prm[:, 0, :]
                    ai = prm[:, 1, :]
                    w = sb(pp, "wk", [128, 16, 32], F32)
                    WK = "wk"

                    def V(i):
                        return w[:, i, :]

                    def tt(o, a, b_, op):
                        S.op('dve', lambda e: e.tensor_tensor(out=o, in0=a, in1=b_, op=op), reads=[WK, "prm"], writes=[WK])

                    def ts(o, a, s1, op0, s2=None, op1=None):
                        if op1 is None:
                            S.op('dve', lambda e: e.tensor_scalar(out=o, in0=a, scalar1=s1, scalar2=None, op0=op0), reads=[WK, "prm"], writes=[WK])
                        else:
                            S.op('dve', lambda e: e.tensor_scalar(out=o, in0=a, scalar1=s1, scalar2=s2, op0=op0, op1=op1), reads=[WK, "prm"], writes=[WK])

                    def act(o, a, f, scale=1.0):
                        S.op('act', lambda e: e.activation(out=o, in_=a, func=f, scale=scale), reads=[WK, "prm"], writes=[WK])

                    def reduce_pi(xv, tmp):
                        for _ in range(5):
                            ts(tmp, xv, PI, ALU.is_gt, -2.0 * PI, ALU.mult)
                            tt(xv, xv, tmp, ALU.add)
                        for _ in range(2):
                            ts(tmp, xv, -PI, ALU.is_lt, 2.0 * PI, ALU.mult)
                            tt(xv, xv, tmp, ALU.add)
                        ts(xv, xv, PI, ALU.min, -PI, ALU.max)

                    dt_ = V(0)
                    act(dt_, prm[:, 2, :], AF.Exp)
                    tt(V(1), dt_, ar, ALU.mult)
                    act(V(1), V(1), AF.Exp)
                    tt(V(2), dt_, ai, ALU.mult)
                    ts(V(3), V(2), PI / 2, ALU.add)
                    reduce_pi(V(2), V(4))
                    reduce_pi(V(3), V(4))
                    act(V(2), V(2), AF.Sin)
                    act(V(3), V(3), AF.Sin)
                    S.op('dve', lambda e: e.tensor_copy(out=rr[:], in_=V(1)), reads=[WK], writes=["rr"])
                    tt(V(5), V(1), V(3), ALU.mult)
                    tt(V(6), V(1), V(2), ALU.mult)
                    tt(V(7), ar, ar, ALU.mult)
                    tt(V(8), ai, ai, ALU.mult)
                    tt(V(7), V(7), V(8), ALU.add)
                    S.op('dve', lambda e: e.reciprocal(out=V(7), in_=V(7)), reads=[WK], writes=[WK])
                    ts(V(8), V(5), -1.0, ALU.add)
                    tt(V(9), V(8), ar, ALU.mult)
                    tt(V(10), V(6), ai, ALU.mult)
                    tt(V(9), V(9), V(10), ALU.add)
                    tt(V(9), V(9), V(7), ALU.mult)
                    tt(V(10), V(6), ar, ALU.mult)
                    tt(V(11), V(8), ai, ALU.mult)
                    tt(V(10), V(10), V(11), ALU.subtract)
                    tt(V(10), V(10), V(7), ALU.mult)
                    ptb = pp.enter_context(ExitStack())
                    Rp = sb(ptb, "Rp", [128, 32, 128], F32)
                    tmpa = sb(ptb, "tmpa", [128, 32, 64], F32)
                    tmpb = sb(ptb, "tmpb", [128, 32, 64], F32)
                    TK = ["Ct", "St", "Rp", "tmpa", "tmpb", WK]

                    def tb(o, a, b_, op):
                        S.op('dve', lambda e: e.tensor_tensor(out=o, in0=a, in1=b_, op=op), reads=TK, writes=TK)

                    S.op('dve', lambda e: e.tensor_copy(out=Ct[:, :, 0], in_=V(3)), reads=[WK], writes=["Ct"])
                    S.op('dve', lambda e: e.tensor_copy(out=St[:, :, 0], in_=V(2)), reads=[WK], writes=["St"])
                    S.op('dve', lambda e: e.tensor_copy(out=Rp[:, :, 0], in_=V(1)), reads=[WK], writes=["Rp"])
                    L = 1
                    while L < 128:
                        cb = Ct[:, :, L - 1:L].to_broadcast([128, 32, L])
                        sbc = St[:, :, L - 1:L].to_broadcast([128, 32, L])
                        rb = Rp[:, :, L - 1:L].to_broadcast([128, 32, L])
                        tb(tmpa[:, :, 0:L], Ct[:, :, 0:L], cb, ALU.mult)
                        tb(tmpb[:, :, 0:L], St[:, :, 0:L], sbc, ALU.mult)
                        tb(Ct[:, :, L:2 * L], tmpa[:, :, 0:L], tmpb[:, :, 0:L], ALU.subtract)
                        tb(tmpa[:, :, 0:L], Ct[:, :, 0:L], sbc, ALU.mult)
                        tb(tmpb[:, :, 0:L], St[:, :, 0:L], cb, ALU.mult)
                        tb(St[:, :, L:2 * L], tmpa[:, :, 0:L], tmpb[:, :, 0:L], ALU.add)
                        tb(Rp[:, :, L:2 * L], Rp[:, :, 0:L], rb, ALU.mult)
                        L *= 2
                    tb(W2r[:], Rp[:], Ct[:], ALU.mult)
                    tb(W2i[:], Rp[:], St[:], ALU.mult)
                    S.op('dve', lambda e: e.tensor_tensor(out=l128r[:], in0=Rp[:, :, 127], in1=Ct[:, :, 127], op=ALU.mult), reads=TK, writes=["l128"])
                    S.op('dve', lambda e: e.tensor_tensor(out=l128i[:], in0=Rp[:, :, 127], in1=St[:, :, 127], op=ALU.mult), reads=TK, writes=["l128"])
                    S.barrier()
                    ptb.close()
                    bnr = sb(pp, "bnr", [128, 32, 16], F32)
                    bni = sb(pp, "bni", [128, 32, 16], F32)
                    S.dma('sp', bnr[:], b_re_d.rearrange("(j p) c -> p j c", p=128), writes=["bnr"])
                    S.dma('sp', bni[:], b_im_d.rearrange("(j p) c -> p j c", p=128), writes=["bni"])
                    t1 = sb(pp, "t1", [128, 32, 16], F32)
                    t2 = sb(pp, "t2", [128, 32, 16], F32)
                    X4r = sb(pp, "X4r", [128, 32, 32], BF16)
                    X4i = sb(pp, "X4i", [128, 32, 32], BF16)
                    BK = ["bnr", "bni", "t1", "t2", "X4r", "X4i", WK]

                    def tbb(o, a, b_, op):
                        S.op('dve', lambda e: e.tensor_tensor(out=o, in0=a, in1=b_, op=op), reads=BK, writes=BK)

                    frb = V(9).unsqueeze(2).to_broadcast([128, 32, 16])
                    fib = V(10).unsqueeze(2).to_broadcast([128, 32, 16])
                    S.op('pool', lambda e: e.memset(X4r[:], 0.0), writes=["X4r"])
                    S.op('pool', lambda e: e.memset(X4i[:], 0.0), writes=["X4i"])
                    tbb(t1[:], bnr[:], frb, ALU.mult)
                    tbb(t2[:], bni[:], fib, ALU.mult)
                    tbb(t1[:], t1[:], t2[:], ALU.subtract)
                    for (lo, c0) in [(0, 0), (64, 16)]:
                        S.op('dve', lambda e: e.tensor_copy(out=X4r[lo:lo + 64, :, c0:c0 + 16], in_=t1[lo:lo + 64, :, :]), reads=BK, writes=BK)
                    tbb(t1[:], bni[:], frb, ALU.mult)
                    tbb(t2[:], bnr[:], fib, ALU.mult)
                    tbb(t1[:], t1[:], t2[:], ALU.add)
                    for (lo, c0) in [(0, 0), (64, 16)]:
                        S.op('dve', lambda e: e.tensor_copy(out=X4i[lo:lo + 64, :, c0:c0 + 16], in_=t1[lo:lo + 64, :, :]), reads=BK, writes=BK)
                    rowm = sb(pp, "rowm", [128, 4], F32)
                    for q in range(4):
                        S.op('dve', lambda e: e.tensor_reduce(out=rowm[:, q:q + 1], in_=identf[:, 32 * q:32 * q + 32], axis=AX.X, op=ALU.add),
                             reads=["identf"], writes=["rowm"])
                    for (X4, LTB, lk) in [(X4r, LTBr, "LTBr"), (X4i, LTBi, "LTBi")]:
                        for k in range(8):
                            S.op('pe', lambda e: e.transpose(out=pt[0][:, k * 128:(k + 1) * 128],
                                                             in_=X4[:, 4 * k:4 * k + 4, :].rearrange("p a b -> p (a b)"), identity=ident[:]),
                                 reads=BK + ["ident"], writes=[PT[0]])
                        for k in range(8):
                            for q in range(4):
                                S.op('dve', lambda e: e.tensor_scalar(out=LTB[:, 4 * k + q, :], in0=pt[0][:, k * 128:(k + 1) * 128], scalar1=rowm[:, q:q + 1], scalar2=None, op0=ALU.mult),
                                     reads=[PT[0], "rowm"], writes=[lk])
                    Xc = sb(pp, "Xc", [32, 32, 128], F32)
                    for (c_d, CL, ck, sgn) in [(c_re_d, CLr, "CLr", 1.0), (c_im_d, CLi, "CLi", -1.0)]:
                        S.op('pool', lambda e: e.memset(Xc[:], 0.0), reads=["Xc"], writes=["Xc"])
                        S.op('pool', lambda e: e.memset(CL[:], 0.0), writes=[ck])
                        cv = c_d.rearrange("(j two) c n -> two c j n", two=2)
                        S.dma('sp', Xc[0:16, :, 0:64], cv[0], reads=[], writes=["Xc"])
                        S.dma('sp', Xc[16:32, :, 64:128], cv[1], reads=[], writes=["Xc"])
                        for hb in range(2):
                            for jj in range(16):
                                j = hb * 16 + jj
                                S.op('pe', lambda e: e.transpose(out=pb[hb][:, jj * 32:(jj + 1) * 32], in_=Xc[:, j, :], identity=identf[0:32, 0:32]),
                                     reads=["Xc", "identf"], writes=[PB[hb]])
                            for q in range(4):
                                src = pb[hb][:, 0:512].rearrange("p (k q c) -> p k q c", q=4, c=32)[:, :, q, :]
                                dst = CL[:, hb * 16:(hb + 1) * 16, :].rearrange("p (k q) c -> p k q c", q=4)[:, :, q, 32 * q:32 * q + 32]
                                S.op('act', lambda e: e.activation(out=dst, in_=src, func=AF.Copy, scale=sgn), reads=[PB[hb]], writes=[ck])
                    S.barrier()
                    chk("C1")

                with ExitStack() as pm:
                    Rm = [sb(pm, "Rm%d" % i, [128, NT], F32) for i in range(1)]
                    bre = sb(pm, "bre", [128, NT], F32)
                    bim = sb(pm, "bim", [128, NT], F32)
                    ta = sb(pm, "ta", [128, NT], F32)
                    tb_ = sb(pm, "tb_", [128, NT], F32)
                    gre = sb(pm, "gre", [128, NT], F32)
                    gim = sb(pm, "gim", [128, NT], F32)
                    hre = [sb(pm, "hre%d" % i, [128, NT], BF16) for i in range(1)]
                    him = [sb(pm, "him%d" % i, [128, NT], BF16) for i in range(1)]

                    def bc(tab, j, n):
                        return tab[:, j:j + 1, :].to_broadcast([128, n, 128])

                    def v3(ap_, n):
                        return ap_.rearrange("p (a b) -> p a b", b=128)

                    for k in range(8):
                        for q in range(4):
                            j = 4 * k + q
                            jb = 0
                            rk = "Rm%d" % jb
                            S.op('pool', lambda e: e.tensor_copy(out=Rm[jb][:], in_=rr[:, j:j + 1].to_broadcast([128, NT])), reads=["rr"], writes=[rk])
                            S.op('pool', lambda e: e.memset(v3(Rm[jb][:], 8)[:, :, 0:1], 0.0), writes=[rk])
                            for half in range(2):
                                sl = slice(half * 512, (half + 1) * 512)
                                S.op('pe', lambda e: e.matmul(pb[0][:, :], lhsT=LTBr[:, j, :], rhs=uT[:, k, sl], start=True, stop=True),
                                     reads=["LTBr", "uT"], writes=[PB[0]])
                                S.op('pe', lambda e: e.matmul(pb[1][:, :], lhsT=LTBi[:, j, :], rhs=uT[:, k, sl], start=True, stop=True),
                                     reads=["LTBi", "uT"], writes=[PB[1]])
                                cB = bc(Ct, j, 4)
                                sB = bc(St, j, 4)
                                S.op('dve', lambda e: e.tensor_tensor(out=v3(bre[:, sl], 4), in0=v3(pb[0][:, :], 4), in1=cB, op=ALU.mult), reads=[PB[0], "Ct"], writes=["bre"])
                                S.op('dve', lambda e: e.tensor_tensor(out=v3(ta[:, sl], 4), in0=v3(pb[1][:, :], 4), in1=sB, op=ALU.mult), reads=[PB[1], "St"], writes=["ta"])
                                S.op('dve', lambda e: e.tensor_tensor(out=v3(bim[:, sl], 4), in0=v3(pb[1][:, :], 4), in1=cB, op=ALU.mult), reads=[PB[1], "Ct"], writes=["bim"])
                                S.op('dve', lambda e: e.tensor_tensor(out=v3(tb_[:, sl], 4), in0=v3(pb[0][:, :], 4), in1=sB, op=ALU.mult), reads=[PB[0], "St"], writes=["tb_"])
                            S.op('pool', lambda e: e.tensor_tensor(out=bre[:], in0=bre[:], in1=ta[:], op=ALU.add), reads=["bre", "ta"], writes=["bre"])
                            S.op('pool', lambda e: e.tensor_tensor(out=bim[:], in0=bim[:], in1=tb_[:], op=ALU.subtract), reads=["bim", "tb_"], writes=["bim"])
                            rmb = Rm[jb][:, :]
                            S.op('dve', lambda e: e.tensor_tensor_scan(out=gre[:], data0=rmb, data1=bre[:], initial=0.0, op0=ALU.mult, op1=ALU.add),
                                 reads=[rk, "bre"], writes=["gre"])
                            S.op('dve', lambda e: e.tensor_tensor_scan(out=gim[:], data0=rmb, data1=bim[:], initial=0.0, op0=ALU.mult, op1=ALU.add),
                                 reads=[rk, "bim"], writes=["gim"])
                            c8 = bc(Ct, j, TPC)
                            s8 = bc(St, j, TPC)
                            S.op('pool', lambda e: e.tensor_tensor(out=v3(ta[:], 8), in0=v3(gre[:], 8), in1=c8, op=ALU.mult), reads=["gre", "Ct"], writes=["ta"])
                            S.op('dve', lambda e: e.tensor_tensor(out=v3(tb_[:], 8), in0=v3(gim[:], 8), in1=s8, op=ALU.mult), reads=["gim", "St"], writes=["tb_"])
                            S.op('pool', lambda e: e.tensor_tensor(out=ta[:], in0=ta[:], in1=tb_[:], op=ALU.subtract), reads=["ta", "tb_"], writes=["ta"])
                            S.op('act', lambda e: e.activation(out=hre[jb][:], in_=ta[:], func=AF.Copy), reads=["ta"], writes=["hre%d" % jb])
                            S.op('act', lambda e: e.activation(out=Eloc[:, :, j, 0], in_=v3(ta[:], 8)[:, :, 127], func=AF.Copy), reads=["ta"], writes=["Eloc"])
                            S.op('pool', lambda e: e.tensor_tensor(out=v3(bre[:], 8), in0=v3(gim[:], 8), in1=c8, op=ALU.mult), reads=["gim", "Ct"], writes=["bre"])
                            S.op('dve', lambda e: e.tensor_tensor(out=v3(bim[:], 8), in0=v3(gre[:], 8), in1=s8, op=ALU.mult), reads=["gre", "St"], writes=["bim"])
                            S.op('pool', lambda e: e.tensor_tensor(out=bre[:], in0=bre[:], in1=bim[:], op=ALU.add), reads=["bre", "bim"], writes=["bre"])
                            S.op('act', lambda e: e.activation(out=him[jb][:], in_=bre[:], func=AF.Copy), reads=["bre"], writes=["him%d" % jb])
                            S.op('act', lambda e: e.activation(out=Eloc[:, :, j, 1], in_=v3(bre[:], 8)[:, :, 127], func=AF.Copy), reads=["bre"], writes=["Eloc"])
                            for half in range(2):
                                sl = slice(half * 512, (half + 1) * 512)
                                S.op('pe', lambda e: e.matmul(pb[2 + half][:, :], lhsT=CLr[:, j, :], rhs=hre[jb][:, sl], start=(q == 0), stop=False),
                                     reads=["CLr", "hre%d" % jb], writes=[PB[2 + half]])
                                S.op('pe', lambda e: e.matmul(pb[2 + half][:, :], lhsT=CLi[:, j, :], rhs=him[jb][:, sl], start=False, stop=(q == 3)),
                                     reads=["CLi", "him%d" % jb], writes=[PB[2 + half]])
                        for half in range(2):
                            sl = slice(half * 512, (half + 1) * 512)
                            copy_evac(ymain[:, k, sl], pb[2 + half][:, :], [PB[2 + half]], ["ymain"])
                    S.dma('sp', e_loc, Eloc[:].rearrange("p m j c -> p (m j c)"), reads=["Eloc"], writes=["e_loc"])
                    S.wait_keys('pool', ["e_loc"])
                    nc.gpsimd.collective_compute("AllGather", ALU.bypass, replica_groups=[list(range(NCORE))],
                                                 ins=[e_loc], outs=[e_gat]).then_inc(cc_sem, 1)
                    ncc[0] += 1
                    if "ymain" in dbg:
                        S.dma('sp', dbg_tensor("ymain", [128, 8 * NT], BF16), ymain[:].rearrange("p a b -> p (a b)"), reads=["ymain"], writes=["dbg_ymain"])
                    S.barrier()
                    chk("C2")

                ptab.close()
                with ExitStack() as pc:
                    nc.sync.wait_ge(cc_sem, ncc[0])
                    nc.gpsimd.wait_ge(cc_sem, ncc[0])
                    cre = sb(pc, "cre", [128, TPC, 32], F32)
                    cim = sb(pc, "cim", [128, TPC, 32], F32)
                    pcc = pc.enter_context(ExitStack())
                    Eall = sb(pcc, "Eall", [128, NCORE, TPC, 32, 2], F32)
                    S.dma('sp', Eall[:].rearrange("p r m j c -> p r (m j c)"), e_gat.rearrange("(r p) f -> p r f", p=128), writes=["Eall"])
                    sre = sb(pcc, "sre", [128, 32], F32)
                    sim = sb(pcc, "sim", [128, 32], F32)
                    u1 = sb(pcc, "u1", [128, 32], F32)
                    u2 = sb(pcc, "u2", [128, 32], F32)
                    CK = ["sre", "sim", "cre", "cim", "u1", "u2"]

                    def cop(fn, extra=()):
                        S.op('dve', fn, reads=CK + list(extra), writes=CK)

                    cop(lambda e: e.memset(sre[:], 0.0))
                    cop(lambda e: e.memset(sim[:], 0.0))
                    cop(lambda e: e.memset(cre[:], 0.0))
                    cop(lambda e: e.memset(cim[:], 0.0))
                    for G in range(NCORE * TPC):
                        r, m = G % NCORE, G // NCORE
                        if G > 0:
                            cop(lambda e: e.scalar_tensor_tensor(out=cre[:, m, :], in0=sre[:], scalar=oh[:, r:r + 1], in1=cre[:, m, :], op0=ALU.mult, op1=ALU.add), ["oh"])
                            cop(lambda e: e.scalar_tensor_tensor(out=cim[:, m, :], in0=sim[:], scalar=oh[:, r:r + 1], in1=cim[:, m, :], op0=ALU.mult, op1=ALU.add), ["oh"])
                        if G == NCORE * TPC - 1:
                            break
                        cop(lambda e: e.tensor_tensor(out=u1[:], in0=sre[:], in1=l128r[:], op=ALU.mult), ["l128"])
                        cop(lambda e: e.tensor_tensor(out=u2[:], in0=sim[:], in1=l128i[:], op=ALU.mult), ["l128"])
                        cop(lambda e: e.tensor_tensor(out=u1[:], in0=u1[:], in1=u2[:], op=ALU.subtract))
                        cop(lambda e: e.tensor_tensor(out=u2[:], in0=sre[:], in1=l128i[:], op=ALU.mult), ["l128"])
                        cop(lambda e: e.tensor_tensor(out=sim[:], in0=sim[:], in1=l128r[:], op=ALU.mult), ["l128"])
                        cop(lambda e: e.tensor_tensor(out=sim[:], in0=sim[:], in1=u2[:], op=ALU.add))
                        cop(lambda e: e.tensor_tensor(out=sim[:], in0=sim[:], in1=Eall[:, r, m, :, 1], op=ALU.add), ["Eall"])
                        cop(lambda e: e.tensor_tensor(out=sre[:], in0=u1[:], in1=Eall[:, r, m, :, 0], op=ALU.add), ["Eall"])
                    S.barrier()
                    pcc.close()
                    Am = [sb(pc, "Am%d" % i, [128, 32, 128], BF16) for i in range(1)]
                    Bm = [sb(pc, "Bm%d" % i, [128, 32, 128], BF16) for i in range(1)]
                    t3 = sb(pc, "t3", [128, 8, 32], F32)
                    t4 = sb(pc, "t4", [128, 8, 32], F32)
                    for i in range(1):
                        S.op('pool', lambda e: e.memset(Am[i][:], 0.0), writes=["Am%d" % i])
                        S.op('pool', lambda e: e.memset(Bm[i][:], 0.0), writes=["Bm%d" % i])
                    yg = sb(pc, "yg", [128, 8, NT], F32)
                    for m in range(TPC):
                        mb = 0
                        ak, bk = "Am%d" % mb, "Bm%d" % mb
                        for q in range(4):
                            def blk(T):
                                return T[:].rearrange("p (k q) c -> p k q c", q=4)[:, :, q, 32 * q:32 * q + 32]

                            def cb_(cvec):
                                return cvec[:, m, :].rearrange("p (k q) -> p k q", q=4)[:, :, q:q + 1].to_broadcast([128, 8, 32])
                            KK = ["t3", "t4", ak, bk]
                            S.op('dve', lambda e: e.tensor_tensor(out=t3[:], in0=blk(CLr), in1=cb_(cre), op=ALU.mult), reads=CK + KK + ["CLr"], writes=KK)
                            S.op('dve', lambda e: e.tensor_tensor(out=t4[:], in0=blk(CLi), in1=cb_(cim), op=ALU.mult), reads=CK + KK + ["CLi"], writes=KK)
                            S.op('dve', lambda e: e.tensor_tensor(out=blk(Am[mb]), in0=t3[:], in1=t4[:], op=ALU.add), reads=KK, writes=KK)
                            S.op('dve', lambda e: e.tensor_tensor(out=t3[:], in0=blk(CLi), in1=cb_(cre), op=ALU.mult), reads=CK + KK + ["CLi"], writes=KK)
                            S.op('dve', lambda e: e.tensor_tensor(out=t4[:], in0=blk(CLr), in1=cb_(cim), op=ALU.mult), reads=CK + KK + ["CLr"], writes=KK)
                            S.op('dve', lambda e: e.tensor_tensor(out=blk(Bm[mb]), in0=t3[:], in1=t4[:], op=ALU.subtract), reads=KK, writes=KK)
                        for k in range(8):
                            pi = k % 4
                            for q in range(4):
                                j = 4 * k + q
                                S.op('pe', lambda e: e.matmul(pb[pi][:, 0:128], lhsT=Am[mb][:, j, :], rhs=W2r[:, j, :], start=(q == 0), stop=False),
                                     reads=[ak, "W2r"], writes=[PB[pi]])
                                S.op('pe', lambda e: e.matmul(pb[pi][:, 0:128], lhsT=Bm[mb][:, j, :], rhs=W2i[:, j, :], start=False, stop=(q == 3)),
                                     reads=[bk, "W2i"], writes=[PB[pi]])
                            tsl = slice(m * 128, (m + 1) * 128)
                            S.op('dve', lambda e: e.tensor_tensor(out=yg[:, k, tsl], in0=pb[pi][:, 0:128], in1=ymain[:, k, tsl], op=ALU.add),
                                 reads=[PB[pi], "ymain"], writes=["yg"])
                    for k in range(8):
                        S.op('dve', lambda e: e.scalar_tensor_tensor(out=yg[:, k, :], in0=uT[:, k, :], scalar=dsk[:, k:k + 1], in1=yg[:, k, :], op0=ALU.mult, op1=ALU.add),
                             reads=["uT", "dsk", "yg"], writes=["yg"])
                    if "ypre" in dbg:
                        S.dma('sp', dbg_tensor("ypre", [128, 8 * NT], F32), yg[:].rearrange("p a b -> p (a b)"), reads=["yg"], writes=["dbg_ypre"])
                    gt = sb(pc, "gt", [128, NT], F32)
                    ygb = ymain
                    for k in range(8):
                        S.op('act', lambda e: e.activation(out=gt[:], in_=yg[:, k, :], func=AF.Square), reads=["yg"], writes=["gt"])
                        S.op('dve', lambda e: e.tensor_scalar(out=gt[:], in0=gt[:], scalar1=0.044715, scalar2=1.0, op0=ALU.mult, op1=ALU.add), reads=["gt"], writes=["gt"])
                        S.op('dve', lambda e: e.tensor_tensor(out=gt[:], in0=gt[:], in1=yg[:, k, :], op=ALU.mult), reads=["gt", "yg"], writes=["gt"])
                        S.op('act', lambda e: e.activation(out=gt[:], in_=gt[:], func=AF.Sigmoid, scale=1.5957691216057308), reads=["gt"], writes=["gt"])
                        S.op('dve', lambda e: e.tensor_tensor(out=ygb[:, k, :], in0=gt[:], in1=yg[:, k, :], op=ALU.mult), reads=["gt", "yg"], writes=["ymain"])
                    S.barrier()
                    chk("C3")

                with ExitStack() as pg:
                    wa = [sb(pg, "wa%d" % i, [128, 8, 128], BF16) for i in range(2)]
                    wb = [sb(pg, "wb%d" % i, [128, 8, 128], BF16) for i in range(2)]
                    sg = sb(pg, "sg", [128, 512], F32)
                    for fc in range(8):
                        b = fc % 2
                        load_w_bf16(wa[b][:], w_glu_d[:, fc * 128:(fc + 1) * 128].rearrange("(kc p) c -> p kc c", p=128), "wa%d" % b)
                        load_w_bf16(wb[b][:], w_glu_d[:, 1024 + fc * 128:1024 + (fc + 1) * 128].rearrange("(kc p) c -> p kc c", p=128), "wb%d" % b)
                        for half in range(2):
                            sl = slice(half * 512, (half + 1) * 512)
                            pa_, pb_ = 2 * half, 2 * half + 1
                            for kc in range(8):
                                S.op('pe', lambda e: e.matmul(pb[pa_][:, :], lhsT=wa[b][:, kc, :], rhs=ymain[:, kc, sl], start=(kc == 0), stop=(kc == 7)),
                                     reads=["wa%d" % b, "ymain"], writes=[PB[pa_]])
                            for kc in range(8):
                                S.op('pe', lambda e: e.matmul(pb[pb_][:, :], lhsT=wb[b][:, kc, :], rhs=ymain[:, kc, sl], start=(kc == 0), stop=(kc == 7)),
                                     reads=["wb%d" % b, "ymain"], writes=[PB[pb_]])
                            S.op('act', lambda e: e.activation(out=sg[:], in_=pb[pb_][:, :], func=AF.Sigmoid), reads=[PB[pb_]], writes=["sg"])
                            S.op('dve', lambda e: e.tensor_tensor(out=sg[:], in0=pb[pa_][:, :], in1=sg[:], op=ALU.mult), reads=[PB[pa_], "sg"], writes=["sg"])
                            S.op('dve', lambda e: e.tensor_tensor(out=szs[:, fc, sl], in0=sg[:], in1=szs[:, fc, sl], op=ALU.mult), reads=["sg", "szs"], writes=["szs"])
                    if "yssm" in dbg:
                        S.dma('sp', dbg_tensor("yssm", [128, 8 * NT], BF16), szs[:].rearrange("p a b -> p (a b)"), reads=["szs"], writes=["dbg_yssm"])
                    S.barrier()
                    chk("C4")

            ust.close()
            with ExitStack() as ph:
                NK = NCORE * NT
                wuq = sb(ph, "wuq", [128, 4, 1024], BF16)
                wuqi = sb(ph, "wuqi", [128, 4, 1024], BF16)
                load_w_bf16(wuq[:], w_uq_d.rearrange("(kc p) c -> p kc c", p=128), "wuq")
                load_w_bf16(wuqi[:], w_uqi_d.rearrange("(kc p) c -> p kc c", p=128), "wuqi")
                kiA = sb(ph, "kiA", [128, NK], BF16)
                KTg = sb(ph, "KTg", [128, NK], BF16)
                Vg = sb(ph, "Vg", [128, 64, 130], BF16)
                Sc = sb(ph, "Sc", [128, NK], F32)
                Mb = sb(ph, "Mb", [128, NK], BF16)
                MT = sb(ph, "MT", [128, 64, 128], BF16)
                qTt = sb(ph, "qTt", [128, 8, 128], BF16)
                qiT = sb(ph, "qiT", [128, 8, 128], BF16)
                Dg = sb(ph, "Dg", [128, 16, 128], BF16)
                cbias = sb(ph, "cbias", [128, 1024], F32)
                rl = [sb(ph, "rl%d" % i, [128, 512], BF16) for i in range(3)]
                Eb = [sb(ph, "Eb%d" % i, [128, 512], BF16) for i in range(2)]
                Pb = [sb(ph, "Pb%d" % i, [128, 512], BF16) for i in range(2)]
                bs = sb(ph, "bs", [128, 16], F32)
                bcnt = sb(ph, "bcnt", [128, 1], F32)
                bcn2 = sb(ph, "bcn2", [128, 1], F32)
                bnm = sb(ph, "bnm", [128, 1], F32)
                lomin = sb(ph, "lomin", [128, 16], F32)
                osb = sb(ph, "osb", [128, 128], BF16)
                rec = sb(ph, "rec", [128, 1], F32)
                nc.sync.wait_ge(cc_sem, ncc[0])
                for half in range(2):
                    for r in range(NCORE):
                        S.dma('sp', kiA[half * 64:(half + 1) * 64, :].rearrange("p (m r t) -> p m r t", r=NCORE, t=128)[:, :, r, :],
                              ki_gat[r * 64:(r + 1) * 64, :].rearrange("p (m t) -> p m t", t=128), writes=["kiA"])
                S.op('pool', lambda e: e.memset(Vg[:, :, 128:130], 1.0), writes=["Vg"])

                for m in DTILES:
                    nkb = 8 * (m + 1)
                    nk = nkb * 128
                    tsl = slice(m * 128, (m + 1) * 128)
                    chk("D0")
                    for (wmat, wk_, dst, dk) in [(wuq, "wuq", qTt, "qTt"), (wuqi, "wuqi", qiT, "qiT")]:
                        for hb in range(2):
                            for c in range(4):
                                fc = hb * 4 + c
                                for kc in range(4):
                                    S.op('pe', lambda e: e.matmul(pb[hb][:, c * 128:(c + 1) * 128], lhsT=wmat[:, kc, fc * 128:(fc + 1) * 128], rhs=cqn[:, kc, tsl],
                                                                  start=(kc == 0), stop=(kc == 3)), reads=[wk_, "cqn"], writes=[PB[hb]])
                            copy_evac(dst[:, hb * 4:(hb + 1) * 4, :].rearrange("p a b -> p (a b)"), pb[hb][:, :], [PB[hb]], [dk])
                    for h in range(16):
                        S.op('dve', lambda e: e.tensor_scalar(out=Dg[:, h, :], in0=ident[:], scalar1=wq[:, m, h:h + 1], scalar2=None, op0=ALU.mult),
                             reads=["ident", "wq"], writes=["Dg"])
                    S.op('pool', lambda e: e.iota(cbias[:], pattern=[[1, 1024]], base=(nkb - 8) * 128, channel_multiplier=0, allow_small_or_imprecise_dtypes=True),
                         writes=["cbias"])
                    S.op('dve', lambda e: e.tensor_scalar(out=cbias[:], in0=cbias[:], scalar1=qpos[:, m:m + 1], scalar2=None, op0=ALU.is_gt),
                         reads=["cbias", "qpos"], writes=["cbias"])
                    S.op('dve', lambda e: e.tensor_scalar(out=cbias[:], in0=cbias[:], scalar1=-1e30, scalar2=None, op0=ALU.mult), reads=["cbias"], writes=["cbias"])
                    nch = nk // 512
                    for ch in range(nch):
                        ksl = slice(ch * 512, (ch + 1) * 512)
                        sp = 2 + ch % 2
                        for h in range(16):
                            rp = h % 2
                            po = (h % 2) * 64
                            rb_ = h % 3
                            S.op('pe', lambda e: e.matmul(pb[rp][:, :], lhsT=qiT[po:po + 64, h // 2, :], rhs=kiA[po:po + 64, ksl], start=True, stop=True),
                                 reads=["qiT", "kiA"], writes=[PB[rp]])
                            S.op('act', lambda e: e.activation(out=rl[rb_][:], in_=pb[rp][:, :], func=AF.Relu), reads=[PB[rp]], writes=["rl%d" % rb_])
                            S.op('pe', lambda e: e.matmul(pb[sp][:, :], lhsT=Dg[:, h, :], rhs=rl[rb_][:], start=(h == 0), stop=(h == 15)),
                                 reads=["Dg", "rl%d" % rb_], writes=[PB[sp]])
                        S.op('dve', lambda e: e.tensor_reduce(out=lomin[:, ch:ch + 1], in_=pb[sp][:, :], axis=AX.X, op=ALU.min), reads=[PB[sp]], writes=["lomin"])
                        if ch >= nch - 2:
                            off = (ch - (nch - 2)) * 512
                            S.op('dve', lambda e: e.tensor_tensor(out=Sc[:, ksl], in0=pb[sp][:, :], in1=cbias[:, off:off + 512], op=ALU.add),
                                 reads=[PB[sp], "cbias"], writes=["Sc"])
                        else:
                            S.op('act', lambda e: e.activation(out=Sc[:, ksl], in_=pb[sp][:, :], func=AF.Copy), reads=[PB[sp]], writes=["Sc"])
                    chk("D2")
                    BS = "bs"
                    LO, HI, MID, PR, T1, T2 = [bs[:, i:i + 1] for i in range(6)]
                    CNT = bcnt[:, 0:1]
                    CN2 = bcn2[:, 0:1]
                    NM = bnm[:, 0:1]

                    def bop(fn, extra=(), wr=()):
                        S.op('dve', fn, reads=[BS] + list(extra), writes=[BS] + list(wr))

                    bop(lambda e: e.tensor_reduce(out=LO, in_=lomin[:, 0:nch], axis=AX.X, op=ALU.min), ["lomin"])
                    bop(lambda e: e.tensor_reduce(out=HI, in_=Sc[:, 0:nk], axis=AX.X, op=ALU.max), ["Sc"])
                    ndve = (nk * 7 // 16) // 128 * 128
                    for it in range(NBIS):
                        bop(lambda e: e.tensor_tensor(out=MID, in0=LO, in1=HI, op=ALU.add))
                        bop(lambda e: e.tensor_scalar(out=MID, in0=MID, scalar1=0.5, scalar2=None, op0=ALU.mult))
                        bop(lambda e: e.tensor_scalar(out=NM, in0=MID, scalar1=-1.0, scalar2=None, op0=ALU.mult), ["bnm"], ["bnm"])
                        S.op('act', lambda e: e.activation(out=Mb[:, ndve:nk], in_=Sc[:, ndve:nk], func=AF.Sign, bias=NM, scale=1.0, accum_out=CN2),
                             reads=["Sc", "bnm"], writes=["Mb_hi", "bcn2"])
                        S.op('dve', lambda e: e.tensor_scalar(out=Mb[:, 0:ndve], in0=Sc[:, 0:ndve], scalar1=MID, scalar2=0.0, op0=ALU.is_ge, op1=ALU.add, accum_out=CNT),
                             reads=["Sc", BS], writes=["Mb_lo", "bcnt"])
                        bop(lambda e: e.tensor_scalar(out=T1, in0=CN2, scalar1=float(nk - ndve), scalar2=0.5, op0=ALU.add, op1=ALU.mult), ["bcn2"])
                        bop(lambda e: e.tensor_tensor(out=T1, in0=T1, in1=CNT, op=ALU.add), ["bcnt"])
                        bop(lambda e: e.tensor_scalar(out=PR, in0=T1, scalar1=float(TOPK) - 0.25, scalar2=None, op0=ALU.is_ge))
                        bop(lambda e: e.tensor_tensor(out=T1, in0=MID, in1=LO, op=ALU.subtract))
                        bop(lambda e: e.tensor_tensor(out=T2, in0=HI, in1=MID, op=ALU.subtract))
                        bop(lambda e: e.scalar_tensor_tensor(out=LO, in0=T1, scalar=PR, in1=LO, op0=ALU.mult, op1=ALU.add))
                        bop(lambda e: e.scalar_tensor_tensor(out=HI, in0=T2, scalar=PR, in1=MID, op0=ALU.mult, op1=ALU.add))
                    S.op('dve', lambda e: e.tensor_scalar(out=Mb[:, 0:nk], in0=Sc[:, 0:nk], scalar1=LO, scalar2=None, op0=ALU.is_ge),
                         reads=["Sc", BS], writes=["Mb_lo", "Mb_hi"])
                    if "mask" in dbg and m == dbg_m:
                        S.dma('sp', dbg_tensor("mask", [128, NK], BF16)[:, 0:nk], Mb[:, 0:nk], reads=["Mb_lo", "Mb_hi"], writes=["dbg_mask"])
                        S.dma('sp', dbg_tensor("score", [128, NK], F32)[:, 0:nk], Sc[:, 0:nk], reads=["Sc"], writes=["dbg_score"])
                    chk("D3")
                    for g8 in range(nkb // 8):
                        pti = g8 % 2
                        for c in range(8):
                            kb = g8 * 8 + c
                            S.op('pe', lambda e: e.transpose(out=pt[pti][:, c * 128:(c + 1) * 128], in_=Mb[:, kb * 128:(kb + 1) * 128], identity=ident[:]),
                                 reads=["Mb_lo", "Mb_hi", "ident"], writes=[PT[pti]])
                        copy_evac(MT[:, g8 * 8:(g8 + 1) * 8, :].rearrange("p a b -> p (a b)"), pt[pti][:, :], [PT[pti]], ["MT"])
                    chk("D4")
                    for g in range(2):
                        for r in range(NCORE):
                            S.dma('sp', KTg[:, 0:nk].rearrange("p (m r t) -> p m r t", r=NCORE, t=128)[:, :, r, :],
                                  kT_gat[r * 256 + g * 128:r * 256 + (g + 1) * 128, 0:(m + 1) * 128].rearrange("p (m t) -> p m t", t=128),
                                  writes=["KTg"])
                            S.dma('sp', Vg[:, 0:nkb, 0:128].rearrange("p (m r) d -> p m r d", r=NCORE)[:, :, r, :],
                                  v_gat[r * NT:r * NT + (m + 1) * 128, g * 128:(g + 1) * 128].rearrange("(m p) d -> p m d", p=128),
                                  writes=["Vg"])
                        for kb in range(nkb):
                            lp = 4 + kb % 2
                            eb = kb % 2
                            S.op('pe', lambda e: e.matmul(pb[lp][:, :], lhsT=KTg[:, kb * 128:(kb + 1) * 128], rhs=qTt[:, 4 * g:4 * g + 4, :].rearrange("p a b -> p (a b)"),
                                                          start=True, stop=True), reads=["KTg", "qTt"], writes=[PB[lp]])
                            S.op('act', lambda e: e.activation(out=Eb[eb][:], in_=pb[lp][:, :], func=AF.Exp, scale=128.0 ** -0.5), reads=[PB[lp]], writes=["Eb%d" % eb])
                            me = 'dve' if kb % 2 == 0 else 'pool'
                            S.op(me, lambda e: e.tensor_tensor(out=Pb[eb][:].rearrange("p (a b) -> p a b", b=128), in0=Eb[eb][:].rearrange("p (a b) -> p a b", b=128),
                                                               in1=MT[:, kb:kb + 1, :].to_broadcast([128, 4, 128]), op=ALU.mult),
                                 reads=["Eb%d" % eb, "MT"], writes=["Pb%d" % eb])
                            for hh in range(4):
                                S.op('pe', lambda e: e.matmul(pb[hh][:, 0:129], lhsT=Pb[eb][:, hh * 128:(hh + 1) * 128], rhs=Vg[:, kb, 0:129],
                                                              start=(kb == 0), stop=(kb == nkb - 1)), reads=["Pb%d" % eb, "Vg"], writes=[PB[hh]])
                        for hh in range(4):
                            hd = 4 * g + hh
                            S.op('dve', lambda e: e.reciprocal(out=rec[:], in_=pb[hh][:, 128:129]), reads=[PB[hh]], writes=["rec"])
                            S.op('dve', lambda e: e.tensor_scalar(out=osb[:], in0=pb[hh][:, 0:128], scalar1=rec[:, 0:1], scalar2=None, op0=ALU.mult),
                                 reads=[PB[hh], "rec"], writes=["osb"])
                            S.op('pe', lambda e: e.transpose(out=pt[0][:, 0:128], in_=osb[:], identity=ident[:]), reads=["osb", "ident"], writes=[PT[0]])
                            S.op('dve', lambda e: e.tensor_tensor(out=sza[:, hd, tsl], in0=pt[0][:, 0:128], in1=sza[:, hd, tsl], op=ALU.mult),
                                 reads=[PT[0], "sza"], writes=["sza"])
                if "yattn" in dbg:
                    S.dma('sp', dbg_tensor("yattn", [128, 8 * NT], BF16), sza[:].rearrange("p a b -> p (a b)"), reads=["sza"], writes=["dbg_yattn"])
                S.barrier()
                chk("D")

            with ExitStack() as ph:
                mts = ph.enter_context(ExitStack())
                mT = sb(mts, "mT", [128, 16, NT], BF16)
                with ExitStack() as p1:
                    hT = sb(p1, "hT2", [128, 16, NT], BF16)
                    with ExitStack() as pa:
                        norm_transpose(pa, "A2", x_tile_loader(pa, "A2", x_d, D), gm, hT, "hT2", "gm")
                        S.barrier()
                    wso = [sb(p1, "wso%d" % i, [128, 8, 128], BF16) for i in range(2)]
                    wao = [sb(p1, "wao%d" % i, [128, 8, 128], BF16) for i in range(2)]
                    wgs = [sb(p1, "wgs%d" % i, [128, 16, 128], BF16) for i in range(2)]
                    wga = [sb(p1, "wga%d" % i, [128, 16, 128], BF16) for i in range(2)]
                    s1 = sb(p1, "s1", [128, 512], F32)
                    s2 = sb(p1, "s2", [128, 512], F32)
                    for fc in range(16):
                        b = fc % 2
                        cs = slice(fc * 128, (fc + 1) * 128)
                        load_w_bf16(wso[b][:], w_so_d[:, cs].rearrange("(kc p) c -> p kc c", p=128), "wso%d" % b)
                        load_w_bf16(wao[b][:], w_ao_d[:, cs].rearrange("(kc p) c -> p kc c", p=128), "wao%d" % b)
                        load_w_bf16(wgs[b][:], w_in_d[:, 4176 + fc * 128:4176 + (fc + 1) * 128].rearrange("(kc p) c -> p kc c", p=128), "wgs%d" % b)
                        load_w_bf16(wga[b][:], w_in_d[:, 6224 + fc * 128:6224 + (fc + 1) * 128].rearrange("(kc p) c -> p kc c", p=128), "wga%d" % b)
                        for half in range(2):
                            sl = slice(half * 512, (half + 1) * 512)
                            for kc in range(8):
                                S.op('pe', lambda e: e.matmul(pb[0][:, :], lhsT=wso[b][:, kc, :], rhs=szs[:, kc, sl], start=(kc == 0), stop=(kc == 7)),
                                     reads=["wso%d" % b, "szs"], writes=[PB[0]])
                            for kc in range(8):
                                S.op('pe', lambda e: e.matmul(pb[1][:, :], lhsT=wao[b][:, kc, :], rhs=sza[:, kc, sl], start=(kc == 0), stop=(kc == 7)),
                                     reads=["wao%d" % b, "sza"], writes=[PB[1]])
                            for kc in range(16):
                                S.op('pe', lambda e: e.matmul(pb[2][:, :], lhsT=wgs[b][:, kc, :], rhs=hT[:, kc, sl], start=(kc == 0), stop=(kc == 15)),
                                     reads=["wgs%d" % b, "hT2"], writes=[PB[2]])
                            for kc in range(16):
                                S.op('pe', lambda e: e.matmul(pb[3][:, :], lhsT=wga[b][:, kc, :], rhs=hT[:, kc, sl], start=(kc == 0), stop=(kc == 15)),
                                     reads=["wga%d" % b, "hT2"], writes=[PB[3]])
                            S.op('act', lambda e: e.activation(out=s1[:], in_=pb[2][:, :], func=AF.Sigmoid), reads=[PB[2]], writes=["s1"])
                            S.op('act', lambda e: e.activation(out=s2[:], in_=pb[3][:, :], func=AF.Sigmoid), reads=[PB[3]], writes=["s2"])
                            S.op('dve', lambda e: e.tensor_tensor(out=s1[:], in0=pb[0][:, :], in1=s1[:], op=ALU.mult), reads=[PB[0], "s1"], writes=["s1"])
                            S.op('dve', lambda e: e.tensor_tensor(out=s2[:], in0=pb[1][:, :], in1=s2[:], op=ALU.mult), reads=[PB[1], "s2"], writes=["s2"])
                            S.op('pool', lambda e: e.tensor_tensor(out=mT[:, fc, sl], in0=s1[:], in1=s2[:], op=ALU.add), reads=["s1", "s2"], writes=["mT"])
                    if "merged" in dbg:
                        S.dma('sp', dbg_tensor("merged", [128, 16 * NT], BF16), mT[:].rearrange("p a b -> p (a b)"), reads=["mT"], writes=["dbg_merged"])
                    S.barrier()
                pers.close()

                x2 = sb(ph, "x2", [128, TPC, D], F32)
                with ExitStack() as p2:
                    wos = [sb(p2, "wos%d" % i, [128, 16, 512], BF16) for i in range(2)]
                    xq = [sb(p2, "xq%d" % i, [128, 512], F32) for i in range(4)]
                    for fb in range(4):
                        b = fb % 2
                        fs = slice(fb * 512, (fb + 1) * 512)
                        load_w_bf16(wos[b][:], w_o_d[:, fs].rearrange("(kc p) c -> p kc c", p=128), "wos%d" % b)
                        for m in range(TPC):
                            pi = m % 4
                            for kc in range(16):
                                S.op('pe', lambda e: e.matmul(pb[pi][:, :], lhsT=mT[:, kc, m * 128:(m + 1) * 128], rhs=wos[b][:, kc, :], start=(kc == 0), stop=(kc == 15)),
                                     reads=["mT", "wos%d" % b], writes=[PB[pi]])
                            S.dma('sp', xq[pi][:], x_d[m * 128:(m + 1) * 128, fs], writes=["xq%d" % pi])
                            S.op('dve', lambda e: e.tensor_tensor(out=x2[:, m, fs], in0=pb[pi][:, :], in1=xq[pi][:], op=ALU.add),
                                 reads=[PB[pi], "xq%d" % pi], writes=["x2"])
                    if "x2" in dbg:
                        S.dma('sp', dbg_tensor("x2", [128, TPC * D], F32), x2[:].rearrange("p a b -> p (a b)"), reads=["x2"], writes=["dbg_x2"])
                    S.barrier()

                mts.close()
                gate = sb(ph, "gate", [128, TPC, D], BF16)
                with ExitStack() as p3:
                    xnT = sb(p3, "xnT", [128, 16, NT], BF16)
                    with ExitStack() as pa:
                        def get_x2(m):
                            return x2[:, m, :], "x2"
                        norm_transpose(pa, "A3", get_x2, gpl, xnT, "xnT", "gpl")
                        S.barrier()
                    wps = [sb(p3, "wps%d" % i, [128, 16, 512], BF16) for i in range(2)]
                    for fb in range(4):
                        b = fb % 2
                        fs = slice(fb * 512, (fb + 1) * 512)
                        load_w_bf16(wps[b][:], w_pg_d[:, fs].rearrange("(kc p) c -> p kc c", p=128), "wps%d" % b)
                        for m in range(TPC):
                            pi = m % 4
                            for kc in range(16):
                                S.op('pe', lambda e: e.matmul(pb[pi][:, :], lhsT=xnT[:, kc, m * 128:(m + 1) * 128], rhs=wps[b][:, kc, :], start=(kc == 0), stop=(kc == 15)),
                                     reads=["xnT", "wps%d" % b], writes=[PB[pi]])
                            S.op('act', lambda e: e.activation(out=gate[:, m, fs], in_=pb[pi][:, :], func=AF.Sigmoid), reads=[PB[pi]], writes=["gate"])
                    S.barrier()

                with ExitStack() as p4:
                    wpl = sb(p4, "wpl", [128, 2, D], BF16)
                    load_w_bf16(wpl[:], w_ple_d.rearrange("(kc p) c -> p kc c", p=128), "wpl")
                    gpp = sb(p4, "gpp", [128, D], F32)
                    gfin = sb(p4, "gfin", [128, D], F32)
                    grow = sb(p4, "grow", [1, 2 * D], F32)
                    onesf = sb(p4, "onesf", [1, 128], F32)
                    S.op('dve', lambda e: e.memset(onesf[:], 1.0), writes=["onesf"])
                    S.dma('sp', grow[0:1, 0:D], g_pp_d, writes=["grow"])
                    S.dma('sp', grow[0:1, D:2 * D], g_fin_d, writes=["grow"])
                    for gi, (gdst, gk) in enumerate([(gpp, "gpp"), (gfin, "gfin")]):
                        for fb in range(4):
                            S.op('pe', lambda e: e.matmul(pb[fb][:, :], lhsT=onesf[0:1, :], rhs=grow[0:1, gi * D + fb * 512:gi * D + (fb + 1) * 512], start=True, stop=True),
                                 reads=["onesf", "grow"], writes=[PB[fb]])
                            S.op('dve', lambda e: e.tensor_copy(out=gdst[:, fb * 512:(fb + 1) * 512], in_=pb[fb][:, :]), reads=[PB[fb]], writes=[gk])
                    pf = [sb(p4, "pf%d" % i, [128, 256], F32) for i in range(2)]
                    pbf = sb(p4, "pbf", [128, 256], BF16)
                    pT = sb(p4, "pT", [128, 2, 128], BF16)
                    er = sb(p4, "er", [128, D], F32)
                    ej = sb(p4, "ej", [128, D], BF16)
                    ob = [sb(p4, "ob%d" % i, [128, D], F32) for i in range(2)]
                    st4 = sb(p4, "st4", [128, 2 * TPC], F32)
                    for m in range(TPC):
                        b = m % 2
                        S.dma('sp', pf[b][:], p_d[m * 128:(m + 1) * 128, :], writes=["pf%d" % b])
                        S.op('dve', lambda e: e.tensor_copy(out=pbf[:], in_=pf[b][:]), reads=["pf%d" % b], writes=["pbf"])
                        for c in range(2):
                            S.op('pe', lambda e: e.transpose(out=pt[0][:, c * 128:(c + 1) * 128], in_=pbf[:, c * 128:(c + 1) * 128], identity=ident[:]),
                                 reads=["pbf", "ident"], writes=[PT[0]])
                        S.op('act', lambda e: e.activation(out=pT[:].rearrange("p a b -> p (a b)"), in_=pt[0][:, 0:256], func=AF.Copy), reads=[PT[0]], writes=["pT"])
                        for fb in range(4):
                            for kc in range(2):
                                S.op('pe', lambda e: e.matmul(pb[fb][:, :], lhsT=pT[:, kc, :], rhs=wpl[:, kc, fb * 512:(fb + 1) * 512], start=(kc == 0), stop=(kc == 1)),
                                     reads=["pT", "wpl"], writes=[PB[fb]])
                        for fb in range(4):
                            copy_evac(er[:, fb * 512:(fb + 1) * 512], pb[fb][:, :], [PB[fb]], ["er"])
                        S.op('act', lambda e: e.activation(out=ej[:], in_=er[:], func=AF.Square, accum_out=st4[:, 2 * m:2 * m + 1]), reads=["er"], writes=["ej", "st4"])
                        rstd_cols(st4[:, 2 * m:2 * m + 1], "st4", 1.0 / D)
                        S.op('dve', lambda e: e.scalar_tensor_tensor(out=er[:], in0=er[:], scalar=st4[:, 2 * m:2 * m + 1], in1=gpp[:], op0=ALU.mult, op1=ALU.mult),
                             reads=["er", "st4", "gpp"], writes=["er"])
                        S.op('pool', lambda e: e.tensor_tensor(out=er[:], in0=er[:], in1=gate[:, m, :], op=ALU.mult), reads=["er", "gate"], writes=["er"])
                        S.op('dve', lambda e: e.tensor_tensor(out=er[:], in0=er[:], in1=x2[:, m, :], op=ALU.add), reads=["er", "x2"], writes=["er"])
                        S.op('act', lambda e: e.activation(out=ej[:], in_=er[:], func=AF.Square, accum_out=st4[:, 2 * m + 1:2 * m + 2]), reads=["er"], writes=["ej", "st4"])
                        rstd_cols(st4[:, 2 * m + 1:2 * m + 2], "st4", 1.0 / D)
                        S.op('dve', lambda e: e.scalar_tensor_tensor(out=ob[b][:], in0=er[:], scalar=st4[:, 2 * m + 1:2 * m + 2], in1=gfin[:], op0=ALU.mult, op1=ALU.mult),
                             reads=["er", "st4", "gfin"], writes=["ob%d" % b])
                        S.dma('sp', y_d[m * 128:(m + 1) * 128, :], ob[b][:], reads=["ob%d" % b], writes=["y"])
                    S.barrier()

        except _Stop:
            pass
        S.barrier()
    return nc, dbg_out, S


dbg_m = 7
import os as _os
DTILES = [int(v) for v in _os.environ.get("DTILES", "0,1,2,3,4,5,6,7").split(",") if v != ""]
_CACHE = {}


def _shard_inputs(inputs):
    f = lambda a: np.ascontiguousarray(np.asarray(a, dtype=np.float32))
    x = f(inputs["x"])[0].reshape(NCORE * TPC, 128, D)
    p = f(inputs["p"])[0, 0].reshape(NCORE * TPC, 128, 256)
    common = {
        "g_mix": f(inputs["g_mix"]).reshape(1, D), "w_in": f(inputs["w_in"])[0],
        "g_q": f(inputs["g_q"]).reshape(1, 512), "w_uq": f(inputs["w_uq"])[0], "w_uq_idx": f(inputs["w_uq_idx"])[0],
        "g_kidx": f(inputs["g_kidx"]).reshape(1, 64),
        "a_re": f(inputs["a_re"])[0].reshape(32, 128), "a_im": f(inputs["a_im"])[0].reshape(32, 128),
        "log_dt": f(inputs["log_dt"])[0].reshape(32, 2),
        "b_re": f(inputs["b_re"])[0].reshape(4096, 16), "b_im": f(inputs["b_im"])[0].reshape(4096, 16),
        "c_re": f(inputs["c_re"])[0], "c_im": f(inputs["c_im"])[0],
        "d_skip": f(inputs["d_skip"]).reshape(1, 1024),
        "w_glu": f(inputs["w_glu"])[0], "w_ssm_out": f(inputs["w_ssm_out"])[0], "w_attn_out": f(inputs["w_attn_out"])[0],
        "w_o": f(inputs["w_o"])[0], "g_ple": f(inputs["g_ple"]).reshape(1, D), "w_ple_gate": f(inputs["w_ple_gate"])[0],
        "w_ple": f(inputs["w_ple"])[0], "g_ple_post": f(inputs["g_ple_post"]).reshape(1, D), "g_final": f(inputs["g_final"]).reshape(1, D),
    }
    maps = []
    for i in range(NCORE):
        mp = dict(common)
        mp["x"] = np.ascontiguousarray(x[i::NCORE].reshape(NT, D))
        mp["p"] = np.ascontiguousarray(p[i::NCORE].reshape(NT, 256))
        qp = np.zeros((128, TPC), np.float32)
        for m in range(TPC):
            qp[:, m] = (NCORE * m + i) * 128 + np.arange(128)
        mp["qpos"] = qp
        ohm = np.zeros((128, NCORE), np.float32)
        ohm[:, i] = 1.0
        mp["onehot"] = ohm
        maps.append(mp)
    return maps


def _assemble(res, name="y", width=D):
    out = np.zeros((NCORE * TPC, 128, width), np.float32)
    for i in range(NCORE):
        out[i::NCORE] = np.asarray(res.results[i][name]).reshape(TPC, 128, width)
    return out.reshape(1, NCORE * NT, width)


def kernel(**inputs):
    if "nc" not in _CACHE:
        _CACHE["nc"] = build_nc()[0]
    nc = _CACHE["nc"]
    maps = _shard_inputs(inputs)
    res = run_bass_kernel_spmd(nc, maps, core_ids=list(range(NCORE)))
    return _assemble(res)
```

```python
import math
import numpy as np
from contextlib import ExitStack
import concourse.bass as bass
import concourse.mybir as mybir
from concourse.bass_utils import run_bass_kernel_spmd

F32 = mybir.dt.float32
BF16 = mybir.dt.bfloat16
AF = mybir.ActivationFunctionType
ALU = mybir.AluOpType
AX = mybir.AxisListType

NCORE = 8
TPC = 8
NT = 1024
D = 2048
INW = 8272
EPS = 1e-6
TOPK = 256
NBIS = 18
PI = math.pi


class Sched:
    def __init__(self, nc, es, ndma=28, needed=None):
        self.nc = nc
        self.needed = needed
        self.rec = set()
        self.sigcnt = {}
        self.sigval = {}
        self.engs = {'pe': nc.tensor, 'act': nc.scalar, 'dve': nc.vector, 'pool': nc.gpsimd, 'sp': nc.sync}
        self.semh = []
        self.esem = {}
        for e in ['pe', 'act', 'dve', 'pool']:
            self.esem[e] = len(self.semh)
            self.semh.append(es.enter_context(nc.semaphore("s_" + e)))
        self.cnt = {e: 0 for e in self.engs}
        self.seen = {e: {} for e in self.engs}
        self.last_w = {}
        self.readers = {}
        self.dsem = []
        for i in range(ndma):
            self.dsem.append(len(self.semh))
            self.semh.append(es.enter_context(nc.semaphore("s_dma%d" % i)))
        self.duse = [0] * ndma
        self.di = 0

    def _deps(self, reads, writes):
        deps = []
        for k in reads:
            lw = self.last_w.get(k)
            if lw:
                deps.append(lw)
            if k[:2] in ("pb", "pt") and len(k) == 3:
                deps.extend(self.readers.get(k, {}).items())
        for k in writes:
            lw = self.last_w.get(k)
            if lw:
                deps.append(lw)
            deps.extend(self.readers.get(k, {}).items())
        return deps

    def _wait(self, e, deps):
        best = {}
        for s, v in deps:
            if v > best.get(s, 0):
                best[s] = v
        for s, v in best.items():
            if e == 'pe' and s == self.esem['pe']:
                continue
            if self.seen[e].get(s, 0) >= v:
                continue
            if s in self.esem.values():
                self.rec.add((s, v))
                self.engs[e].wait_ge(self.semh[s], self.sigval[(s, v)])
            else:
                self.engs[e].wait_ge(self.semh[s], v)
            self.seen[e][s] = v

    def _upd(self, tok, reads, writes):
        for k in reads:
            d = self.readers.setdefault(k, {})
            if tok[1] > d.get(tok[0], 0):
                d[tok[0]] = tok[1]
        for k in writes:
            self.last_w[k] = tok
            self.readers[k] = {}

    def op(self, e, fn, reads=(), writes=()):
        self._wait(e, self._deps(reads, writes))
        inst = fn(self.engs[e])
        self.cnt[e] += 1
        tok = (self.esem[e], self.cnt[e])
        if self.needed is None or tok in self.needed:
            self.sigcnt[e] = self.sigcnt.get(e, 0) + 1
            inst.then_inc(self.semh[self.esem[e]], 1)
            self.sigval[tok] = self.sigcnt[e]
        self._upd(tok, reads, writes)
        return tok

    def dma(self, q, out, in_, reads=(), writes=(), **kw):
        i = self.di
        self.di = (i + 1) % len(self.dsem)
        deps = self._deps(reads, writes)
        if self.duse[i]:
            deps.append((self.dsem[i], 16 * self.duse[i]))
        self._wait(q, deps)
        inst = self.engs[q].dma_start(out=out, in_=in_, **kw)
        self.duse[i] += 1
        inst.then_inc(self.semh[self.dsem[i]], 16)
        tok = (self.dsem[i], 16 * self.duse[i])
        self._upd(tok, reads, writes)
        return tok

    def wait_keys(self, e, keys):
        deps = []
        for k in keys:
            lw = self.last_w.get(k)
            if lw:
                deps.append(lw)
            deps.extend(self.readers.get(k, {}).items())
        self._wait(e, deps)

    def barrier(self):
        deps = [(self.esem[e], self.cnt[e]) for e in ['pe', 'act', 'dve', 'pool'] if self.cnt[e]]
        deps += [(self.dsem[i], 16 * self.duse[i]) for i in range(len(self.dsem)) if self.duse[i]]
        for e in self.engs:
            self._wait(e, deps)


class Arena:
    def __init__(self, nbytes):
        self.free = [(0, nbytes)]

    def alloc(self, n):
        n = (n + 63) // 64 * 64
        for i, (o, sz) in enumerate(self.free):
            if sz >= n:
                if sz == n:
                    self.free.pop(i)
                else:
                    self.free[i] = (o + n, sz - n)
                return o, n
        raise MemoryError("arena out of memory: need %d, free list %s" % (n, self.free))

    def release(self, o, n):
        self.free.append((o, n))
        self.free.sort()
        out = []
        for (a, b) in self.free:
            if out and out[-1][0] + out[-1][1] == a:
                out[-1] = (out[-1][0], out[-1][1] + b)
            else:
                out.append((a, b))
        self.free = out


ARENA_KB = 204


class _Stop(Exception):
    pass


def build_nc(dbg=None, stop=None):
    _, _, S1 = _build_nc(dbg, stop, None)
    nc, dbg_out, S2 = _build_nc(dbg, stop, S1.rec)
    return nc, dbg_out


def _build_nc(dbg=None, stop=None, needed=None):
    dbg = dbg or ()

    def chk(tag):
        if stop == tag:
            raise _Stop()
    nc = bass.Bass("TRN2", target_bir_lowering=False)

    def din(name, shape):
        return nc.dram_tensor(name, list(shape), F32, kind="ExternalInput").ap()

    x_d = din("x", [NT, D]); p_d = din("p", [NT, 256])
    g_mix_d = din("g_mix", [1, D]); w_in_d = din("w_in", [D, INW])
    g_q_d = din("g_q", [1, 512]); w_uq_d = din("w_uq", [512, 1024]); w_uqi_d = din("w_uq_idx", [512, 1024])
    g_kidx_d = din("g_kidx", [1, 64])
    a_re_d = din("a_re", [32, 128]); a_im_d = din("a_im", [32, 128]); log_dt_d = din("log_dt", [32, 2])
    b_re_d = din("b_re", [4096, 16]); b_im_d = din("b_im", [4096, 16])
    c_re_d = din("c_re", [64, 16, 64]); c_im_d = din("c_im", [64, 16, 64])
    d_skip_d = din("d_skip", [1, 1024])
    w_glu_d = din("w_glu", [1024, 2048]); w_so_d = din("w_ssm_out", [1024, 2048]); w_ao_d = din("w_attn_out", [1024, 2048])
    w_o_d = din("w_o", [D, D]); g_ple_d = din("g_ple", [1, D]); w_pg_d = din("w_ple_gate", [D, D])
    w_ple_d = din("w_ple", [256, D]); g_pp_d = din("g_ple_post", [1, D]); g_fin_d = din("g_final", [1, D])
    qpos_d = din("qpos", [128, TPC]); oh_d = din("onehot", [128, NCORE])
    y_d = nc.dram_tensor("y", [NT, D], F32, kind="ExternalOutput").ap()

    def dint(name, shape, dt):
        return nc.dram_tensor(name, list(shape), dt, kind="Internal").ap()

    kT_loc = dint("kT_loc", [256, NT], BF16); kT_gat = dint("kT_gat", [NCORE * 256, NT], BF16)
    v_loc = dint("v_loc", [NT, 256], BF16); v_gat = dint("v_gat", [NCORE * NT, 256], BF16)
    ki_loc = dint("ki_loc", [64, NT], BF16); ki_gat = dint("ki_gat", [NCORE * 64, NT], BF16)
    e_loc = dint("e_loc", [128, TPC * 64], F32); e_gat = dint("e_gat", [NCORE * 128, TPC * 64], F32)

    dbg_out = {}

    def dbg_tensor(name, shape, dt=F32):
        dbg_out[name] = nc.dram_tensor("dbg_" + name, list(shape), dt, kind="ExternalOutput").ap()
        return dbg_out[name]

    with ExitStack() as es:
        S = Sched(nc, es, needed=needed)
        cc_sem = es.enter_context(nc.semaphore("cc_sem"))

        arena_t = es.enter_context(nc.sbuf_tensor("arena", [128, ARENA_KB * 256], F32))
        arena = Arena(ARENA_KB * 1024)

        def sb(st, name, shape, dt):
            esz = 4 if dt == F32 else 2
            nel = 1
            for d_ in shape[1:]:
                nel *= d_
            off, n = arena.alloc(nel * esz)
            st.callback(arena.release, off, n)
            v = arena_t[:, off // 4:(off + n) // 4]
            if dt != F32:
                v = v.bitcast(dt)
            v = v[0:shape[0], 0:nel]
            if len(shape) == 3:
                v = v.rearrange("p (a b) -> p a b", a=shape[1])
            elif len(shape) == 4:
                v = v.rearrange("p (a b c) -> p a b c", a=shape[1], b=shape[2])
            elif len(shape) == 5:
                v = v.rearrange("p (a b c d) -> p a b c d", a=shape[1], b=shape[2], c=shape[3])
            return v

        pb = [es.enter_context(nc.psum_tensor("pb%d" % i, [128, 512], F32)) for i in range(6)]
        pt = [es.enter_context(nc.psum_tensor("pt%d" % i, [128, 1024], BF16)) for i in range(2)]
        PB = ["pb%d" % i for i in range(6)]
        PT = ["pt0", "pt1"]

        identf = sb(es, "identf", [128, 128], F32)
        ident = sb(es, "ident", [128, 128], BF16)
        ones_b = sb(es, "ones_b", [128, 128], BF16)
        qpos = sb(es, "qpos_s", [128, TPC], F32)
        oh = sb(es, "oh_s", [128, NCORE], F32)
        S.op('pool', lambda e: e.memset(identf[:], 0.0), writes=['identf'])
        S.op('pool', lambda e: e.affine_select(out=identf[:], in_=identf[:], pattern=[[-1, 128]],
                                                compare_op=ALU.not_equal, fill=1.0, base=0, channel_multiplier=1),
             reads=['identf'], writes=['identf'])
        S.op('dve', lambda e: e.tensor_copy(out=ident[:], in_=identf[:]), reads=['identf'], writes=['ident'])
        S.op('dve', lambda e: e.memset(ones_b[:], 1.0), writes=['ones_b'])
        S.dma('sp', qpos[:], qpos_d, writes=['qpos'])
        S.dma('sp', oh[:], oh_d, writes=['oh'])

        def load_pp(st, name, vec_d, n):
            t = sb(st, name, [128, n], F32)
            S.dma('sp', t[:], vec_d.rearrange("o (c p) -> p (o c)", p=128), writes=[name],
                  allow_slow_non_contiguous=True)
            return t

        gm = load_pp(es, "gm", g_mix_d, 16)
        gpl = load_pp(es, "gpl", g_ple_d, 16)
        gq = load_pp(es, "gq", g_q_d, 4)
        dsk = load_pp(es, "dsk", d_skip_d, 8)

        pers = es.enter_context(ExitStack())
        ust = es.enter_context(ExitStack())
        szs = sb(pers, "szs", [128, 8, NT], BF16)
        sza = sb(pers, "sza", [128, 8, NT], BF16)
        cqn = sb(pers, "cqn", [128, 4, NT], BF16)
        wq = sb(pers, "wq", [128, TPC, 16], F32)

        cast_i = [0]

        stg = [sb(es, "stg%d" % i, [128, 1024], F32) for i in range(2)]

        def load_w_bf16(dst_ap, src_ap, key):
            P, A, B = dst_ap.shape
            bs_ = min(B, 1024)
            step = max(1, 1024 // bs_)
            for b0 in range(0, B, bs_):
                for a0 in range(0, A, step):
                    a1 = min(A, a0 + step)
                    i = cast_i[0] % 2
                    cast_i[0] += 1
                    sv = stg[i][0:P, 0:(a1 - a0) * bs_].rearrange("p (a b) -> p a b", b=bs_)
                    S.dma('sp', sv, src_ap[:, a0:a1, b0:b0 + bs_], writes=["stg%d" % i])
                    if cast_i[0] % 3 == 0:
                        S.op('act', lambda e: e.activation(out=dst_ap[:, a0:a1, b0:b0 + bs_], in_=sv, func=AF.Copy), reads=["stg%d" % i], writes=[key])
                    else:
                        S.op('pool', lambda e: e.tensor_copy(out=dst_ap[:, a0:a1, b0:b0 + bs_], in_=sv), reads=["stg%d" % i], writes=[key])

        evac_i = [0]

        def copy_evac(out_ap, in_ap, reads, writes, scale=None):
            evac_i[0] += 1
            if evac_i[0] % 2 == 0 or scale is not None:
                if scale is None:
                    S.op('act', lambda e: e.activation(out=out_ap, in_=in_ap, func=AF.Copy), reads=reads, writes=writes)
                else:
                    S.op('act', lambda e: e.activation(out=out_ap, in_=in_ap, func=AF.Copy, scale=scale), reads=reads, writes=writes)
            else:
                S.op('dve', lambda e: e.tensor_copy(out=out_ap, in_=in_ap), reads=reads, writes=writes)

        def rstd_cols(col_ap, key, inv_n):
            S.op('dve', lambda e: e.tensor_scalar(out=col_ap, in0=col_ap, scalar1=inv_n, scalar2=EPS, op0=ALU.mult, op1=ALU.add),
                 reads=[key], writes=[key])
            S.op('act', lambda e: e.activation(out=col_ap, in_=col_ap, func=AF.Sqrt), reads=[key], writes=[key])
            S.op('dve', lambda e: e.reciprocal(out=col_ap, in_=col_ap), reads=[key], writes=[key])

        def norm_transpose(st, tag, get_tile, gain, dstT, dkey, gkey):
            xn = sb(st, tag + "_xn", [128, D], BF16)
            junk = sb(st, tag + "_junk", [128, D], BF16)
            ssq = sb(st, tag + "_ssq", [128, TPC], F32)
            for m in range(TPC):
                xs, xkey = get_tile(m)
                S.op('act', lambda e: e.activation(out=junk[:], in_=xs, func=AF.Square, accum_out=ssq[:, m:m + 1]),
                     reads=[xkey], writes=[tag + "_junk", tag + "_ssq"])
                rstd_cols(ssq[:, m:m + 1], tag + "_ssq", 1.0 / D)
                S.op('dve', lambda e: e.tensor_scalar(out=xn[:], in0=xs, scalar1=ssq[:, m:m + 1], scalar2=None, op0=ALU.mult),
                     reads=[xkey, tag + "_ssq"], writes=[tag + "_xn"])
                for hb in range(2):
                    for c in range(8):
                        dc = hb * 8 + c
                        S.op('pe', lambda e: e.transpose(out=pt[hb][:, c * 128:(c + 1) * 128], in_=xn[:, dc * 128:(dc + 1) * 128], identity=ident[:]),
                             reads=[tag + "_xn", 'ident'], writes=[PT[hb]])
                    for c in range(8):
                        dc = hb * 8 + c
                        S.op('act', lambda e: e.activation(out=dstT[:, dc, m * 128:(m + 1) * 128], in_=pt[hb][:, c * 128:(c + 1) * 128],
                                                           func=AF.Copy, scale=gain[:, dc:dc + 1]),
                             reads=[PT[hb], gkey], writes=[dkey])

        def x_tile_loader(st, tag, src_d, width):
            bufs = [sb(st, "%s_x%d" % (tag, i), [128, width], F32) for i in range(2)]

            def get(m):
                b = m % 2
                key = "%s_x%d" % (tag, b)
                S.dma('sp', bufs[b][:], src_d[m * 128:(m + 1) * 128, :], writes=[key])
                return bufs[b][:], key
            return get

        def proj_fm(st, tag, w_d, col0, ncols_list, KC, actT, akey, evac):
            wst = [sb(st, "%s_w%d" % (tag, i), [128, KC, 128], BF16) for i in range(2)]
            c0 = col0
            for i, ncol in enumerate(ncols_list):
                b = i % 2
                wkey = "%s_w%d" % (tag, b)
                load_w_bf16(wst[b][:, :, 0:ncol], w_d[:, c0:c0 + ncol].rearrange("(kc p) c -> p kc c", p=128), wkey)
                for half in range(2):
                    pi = (2 * i + half) % 4
                    for kc in range(KC):
                        S.op('pe', lambda e: e.matmul(pb[pi][0:ncol, :], lhsT=wst[b][:, kc, 0:ncol], rhs=actT[:, kc, half * 512:(half + 1) * 512],
                                                      start=(kc == 0), stop=(kc == KC - 1)),
                             reads=[wkey, akey], writes=[PB[pi]])
                    evac(i, half, pb[pi], PB[pi], ncol)
                c0 += ncol

        try:
            with ExitStack() as ph:
                hT = sb(ph, "hT", [128, 16, NT], BF16)
                uT = sb(ust, "uT", [128, 8, NT], BF16)
                kTl = sb(ph, "kTl", [128, 2, NT], BF16)
                cqf = sb(ph, "cqf", [128, 4, NT], F32)
                with ExitStack() as pa:
                    norm_transpose(pa, "A", x_tile_loader(pa, "A", x_d, D), gm, hT, "hT", "gm")
                    S.barrier()
                chk("A")

                def evac_main(i, half, ps, pkey, ncol):
                    sl = slice(half * 512, (half + 1) * 512)
                    if i < 8:
                        copy_evac(uT[:, i, sl], ps[:, :], [pkey], ["uT"])
                    elif i < 16:
                        S.op('act', lambda e: e.activation(out=szs[:, i - 8, sl], in_=ps[:, :], func=AF.Silu), reads=[pkey], writes=["szs"])
                    elif i < 20:
                        copy_evac(cqf[:, i - 16, sl], ps[:, :], [pkey], ["cqf"])
                    else:
                        copy_evac(kTl[:, i - 20, sl], ps[:, :], [pkey], ["kTl"])

                proj_fm(ph, "B1", w_in_d, 0, [128] * 22, 16, hT, "hT", evac_main)

                def evac_za(i, half, ps, pkey, ncol):
                    sl = slice(half * 512, (half + 1) * 512)
                    S.op('act', lambda e: e.activation(out=sza[:, i, sl], in_=ps[:, :], func=AF.Silu), reads=[pkey], writes=["sza"])

                proj_fm(ph, "B2", w_in_d, 3072, [128] * 8, 16, hT, "hT", evac_za)
                chk("B2")
                S.dma('sp', kT_loc.rearrange("(c p) t -> p c t", p=128), kTl[:], reads=["kTl"], writes=["kT_loc"])

                wv = sb(ph, "wv", [128, 16, 256], BF16)
                wkw = sb(ph, "wkw", [128, 16, 80], BF16)
                load_w_bf16(wv[:], w_in_d[:, 2816:3072].rearrange("(kc p) c -> p kc c", p=128), "wv")
                load_w_bf16(wkw[:], w_in_d[:, 4096:4176].rearrange("(kc p) c -> p kc c", p=128), "wkw")
                vtok = sb(ph, "vtok", [128, TPC, 256], BF16)
                kif = sb(ph, "kif", [128, 64], F32)
                kib = sb(ph, "kib", [128, 64], BF16)
                kjunk = sb(ph, "kjunk", [128, 64], BF16)
                kss = sb(ph, "kss", [128, TPC], F32)
                kiT = sb(ph, "kiT", [64, NT], BF16)
                gki = sb(ph, "gki", [64, 1], F32)
                S.dma('sp', gki[:], g_kidx_d.rearrange("o (c p) -> p (o c)", p=64), writes=["gki"], allow_slow_non_contiguous=True)
                for m in range(TPC):
                    for kc in range(16):
                        S.op('pe', lambda e: e.matmul(pb[4][:, 0:256], lhsT=hT[:, kc, m * 128:(m + 1) * 128], rhs=wv[:, kc, :],
                                                      start=(kc == 0), stop=(kc == 15)), reads=["hT", "wv"], writes=[PB[4]])
                    for kc in range(16):
                        S.op('pe', lambda e: e.matmul(pb[5][:, 0:80], lhsT=hT[:, kc, m * 128:(m + 1) * 128], rhs=wkw[:, kc, :],
                                                      start=(kc == 0), stop=(kc == 15)), reads=["hT", "wkw"], writes=[PB[5]])
                    copy_evac(vtok[:, m, :], pb[4][:, 0:256], [PB[4]], ["vtok"])
                    S.op('dve', lambda e: e.tensor_copy(out=kif[:], in_=pb[5][:, 0:64]), reads=[PB[5]], writes=["kif"])
                    S.op('dve', lambda e: e.tensor_scalar(out=wq[:, m, :], in0=pb[5][:, 64:80], scalar1=1.0 / 32.0, scalar2=None, op0=ALU.mult),
                         reads=[PB[5]], writes=["wq"])
                    S.op('act', lambda e: e.activation(out=kjunk[:], in_=kif[:], func=AF.Square, accum_out=kss[:, m:m + 1]),
                         reads=["kif"], writes=["kjunk", "kss"])
                    rstd_cols(kss[:, m:m + 1], "kss", 1.0 / 64)
                    S.op('dve', lambda e: e.tensor_scalar(out=kib[:], in0=kif[:], scalar1=kss[:, m:m + 1], scalar2=None, op0=ALU.mult),
                         reads=["kif", "kss"], writes=["kib"])
                    S.op('pe', lambda e: e.transpose(out=pt[0][0:64, 0:128], in_=kib[:, :], identity=ident[:]),
                         reads=["kib", "ident"], writes=[PT[0]])
                    S.op('act', lambda e: e.activation(out=kiT[:, m * 128:(m + 1) * 128], in_=pt[0][0:64, 0:128], func=AF.Copy, scale=gki[:, 0:1]),
                         reads=[PT[0], "gki"], writes=["kiT"])
                S.dma('sp', v_loc.rearrange("(m p) d -> p m d", p=128), vtok[:], reads=["vtok"], writes=["v_loc"])
                S.dma('sp', ki_loc, kiT[:], reads=["kiT"], writes=["ki_loc"])
                chk("B3")

                S.wait_keys('pool', ["kT_loc", "v_loc", "ki_loc"])
                ncc = [0]
                for (a, b_) in [(kT_loc, kT_gat), (v_loc, v_gat), (ki_loc, ki_gat)]:
                    nc.gpsimd.collective_compute("AllGather", ALU.bypass, replica_groups=[list(range(NCORE))],
                                                 ins=[a], outs=[b_]).then_inc(cc_sem, 1)
                    ncc[0] += 1

                sq = sb(ph, "sq", [128, 4, NT], BF16)
                rsb = sb(ph, "rsb", [128, NT], F32)
                S.op('act', lambda e: e.activation(out=sq[:], in_=cqf[:], func=AF.Square), reads=["cqf"], writes=["sq"])
                for half in range(2):
                    sl = slice(half * 512, (half + 1) * 512)
                    for c in range(4):
                        S.op('pe', lambda e: e.matmul(pb[half][:, :], lhsT=ones_b[:], rhs=sq[:, c, sl], start=(c == 0), stop=(c == 3)),
                             reads=["ones_b", "sq"], writes=[PB[half]])
                    S.op('dve', lambda e: e.tensor_scalar(out=rsb[:, sl], in0=pb[half][:, :], scalar1=1.0 / 512, scalar2=EPS, op0=ALU.mult, op1=ALU.add),
                         reads=[PB[half]], writes=["rsb"])
                S.op('act', lambda e: e.activation(out=rsb[:], in_=rsb[:], func=AF.Sqrt), reads=["rsb"], writes=["rsb"])
                S.op('dve', lambda e: e.reciprocal(out=rsb[:], in_=rsb[:]), reads=["rsb"], writes=["rsb"])
                for c in range(4):
                    S.op('dve', lambda e: e.scalar_tensor_tensor(out=cqn[:, c, :], in0=cqf[:, c, :], scalar=gq[:, c:c + 1], in1=rsb[:],
                                                                op0=ALU.mult, op1=ALU.mult), reads=["cqf", "gq", "rsb"], writes=["cqn"])
                if "uT" in dbg:
                    S.dma('sp', dbg_tensor("uT", [128, 8 * NT], BF16), uT[:].rearrange("p a b -> p (a b)"), reads=["uT"], writes=["dbg_uT"])
                if "cqn" in dbg:
                    S.dma('sp', dbg_tensor("cqn", [128, 4 * NT], BF16), cqn[:].rearrange("p a b -> p (a b)"), reads=["cqn"], writes=["dbg_cqn"])
                S.barrier()
                chk("B5")

            with ExitStack() as ph:
                ptab = ph.enter_context(ExitStack())
                Ct = sb(ptab, "Ct", [128, 32, 128], F32)
                St = sb(ptab, "St", [128, 32, 128], F32)
                W2r = sb(ph, "W2r", [128, 32, 128], BF16)
                W2i = sb(ph, "W2i", [128, 32, 128], BF16)
                CLr = sb(ph, "CLr", [128, 32, 128], BF16)
                CLi = sb(ph, "CLi", [128, 32, 128], BF16)
                LTBr = sb(ph, "LTBr", [128, 32, 128], BF16)
                LTBi = sb(ph, "LTBi", [128, 32, 128], BF16)
                rr = sb(ph, "rr", [128, 32], F32)
                l128r = sb(ph, "l128r", [128, 32], F32)
                l128i = sb(ph, "l128i", [128, 32], F32)
                ymain = sb(ph, "ymain", [128, 8, NT], BF16)
                Eloc = sb(ph, "Eloc", [128, TPC, 32, 2], F32)

                with ExitStack() as pp:
                    nat = sb(pp, "nat", [32, 3, 128], F32)
                    S.dma('sp', nat[:, 0, :], a_re_d, writes=["nat"])
                    S.dma('sp', nat[:, 1, :], a_im_d, writes=["nat"])
                    ld2 = sb(pp, "ld2", [32, 2], F32)
                    S.dma('sp', ld2[:], log_dt_d, writes=["ld2"])
                    S.op('dve', lambda e: e.tensor_copy(out=nat[:, 2, :].rearrange("p (a b) -> p a b", a=2),
                                                        in_=ld2[:, :].unsqueeze(2).to_broadcast([32, 2, 64])),
                         reads=["ld2", "nat"], writes=["nat"])
                    prm = sb(pp, "prm", [128, 3, 32], F32)
                    for q in range(3):
                        S.op('pe', lambda e: e.transpose(out=pb[0][:, q * 32:(q + 1) * 32], in_=nat[:, q, :], identity=identf[0:32, 0:32]),
                             reads=["nat", "identf"], writes=[PB[0]])
                    S.op('dve', lambda e: e.tensor_copy(out=prm[:].rearrange("p a b -> p (a b)"), in_=pb[0][:, 0:96]), reads=[PB[0]], writes=["prm"])
                    ar = prm[:, 0, :]
                    ai = prm[:, 1, :]
                    w = sb(pp, "wk", [128, 16, 32], F32)
                    WK = "wk"

                    def V(i):
                        return w[:, i, :]

                    def tt(o, a, b_, op):
                        S.op('dve', lambda e: e.tensor_tensor(out=o, in0=a, in1=b_, op=op), reads=[WK, "prm"], writes=[WK])

                    def ts(o, a, s1, op0, s2=None, op1=None):
                        if op1 is None:
                            S.op('dve', lambda e: e.tensor_scalar(out=o, in0=a, scalar1=s1, scalar2=None, op0=op0), reads=[WK, "prm"], writes=[WK])
                        else:
                            S.op('dve', lambda e: e.tensor_scalar(out=o, in0=a, scalar1=s1, scalar2=s2, op0=op0, op1=op1), reads=[WK, "prm"], writes=[WK])

                    def act(o, a, f, scale=1.0):
                        S.op('act', lambda e: e.activation(out=o, in_=a, func=f, scale=scale), reads=[WK, "prm"], writes=[WK])

                    def reduce_pi(xv, tmp):
                        for _ in range(5):
                            ts(tmp, xv, PI, ALU.is_gt, -2.0 * PI, ALU.mult)
                            tt(xv, xv, tmp, ALU.add)
                        for _ in range(2):
                            ts(tmp, xv, -PI, ALU.is_lt, 2.0 * PI, ALU.mult)
                            tt(xv, xv, tmp, ALU.add)
                        ts(xv, xv, PI, ALU.min, -PI, ALU.max)

                    dt_ = V(0)
                    act(dt_, prm[:, 2, :], AF.Exp)
                    tt(V(1), dt_, ar, ALU.mult)
                    act(V(1), V(1), AF.Exp)
                    tt(V(2), dt_, ai, ALU.mult)
                    ts(V(3), V(2), PI / 2, ALU.add)
                    reduce_pi(V(2), V(4))
                    reduce_pi(V(3), V(4))
                    act(V(2), V(2), AF.Sin)
                    act(V(3), V(3), AF.Sin)
                    S.op('dve', lambda e: e.tensor_copy(out=rr[:], in_=V(1)), reads=[WK], writes=["rr"])
                    tt(V(5), V(1), V(3), ALU.mult)
                    tt(V(6), V(1), V(2), ALU.mult)
                    tt(V(7), ar, ar, ALU.mult)
                    tt(V(8), ai, ai, ALU.mult)
                    tt(V(7), V(7), V(8), ALU.add)
                    S.op('dve', lambda e: e.reciprocal(out=V(7), in_=V(7)), reads=[WK], writes=[WK])
                    ts(V(8), V(5), -1.0, ALU.add)
                    tt(V(9), V(8), ar, ALU.mult)
                    tt(V(10), V(6), ai, ALU.mult)
                    tt(V(9), V(9), V(10), ALU.add)
                    tt(V(9), V(9), V(7), ALU.mult)
                    tt(V(10), V(6), ar, ALU.mult)
                    tt(V(11), V(8), ai, ALU.mult)
                    tt(V(10), V(10), V(11), ALU.subtract)
                    tt(V(10), V(10), V(7), ALU.mult)
                    ptb = pp.enter_context(ExitStack())
                    Rp = sb(ptb, "Rp", [128, 32, 128], F32)
                    tmpa = sb(ptb, "tmpa", [128, 32, 64], F32)
                    tmpb = sb(ptb, "tmpb", [128, 32, 64], F32)
                    TK = ["Ct", "St", "Rp", "tmpa", "tmpb", WK]

                    def tb(o, a, b_, op):
                        S.op('dve', lambda e: e.tensor_tensor(out=o, in0=a, in1=b_, op=op), reads=TK, writes=TK)

                    S.op('dve', lambda e: e.tensor_copy(out=Ct[:, :, 0], in_=V(3)), reads=[WK], writes=["Ct"])
                    S.op('dve', lambda e: e.tensor_copy(out=St[:, :, 0], in_=V(2)), reads=[WK], writes=["St"])
                    S.op('dve', lambda e: e.tensor_copy(out=Rp[:, :, 0], in_=V(1)), reads=[WK], writes=["Rp"])
                    L = 1
                    while L < 128:
                        cb = Ct[:, :, L - 1:L].to_broadcast([128, 32, L])
                        sbc = St[:, :, L - 1:L].to_broadcast([128, 32, L])
                        rb = Rp[:, :, L - 1:L].to_broadcast([128, 32, L])
                        tb(tmpa[:, :, 0:L], Ct[:, :, 0:L], cb, ALU.mult)
                        tb(tmpb[:, :, 0:L], St[:, :, 0:L], sbc, ALU.mult)
                        tb(Ct[:, :, L:2 * L], tmpa[:, :, 0:L], tmpb[:, :, 0:L], ALU.subtract)
                        tb(tmpa[:, :, 0:L], Ct[:, :, 0:L], sbc, ALU.mult)
                        tb(tmpb[:, :, 0:L], St[:, :, 0:L], cb, ALU.mult)
                        tb(St[:, :, L:2 * L], tmpa[:, :, 0:L], tmpb[:, :, 0:L], ALU.add)
                        tb(Rp[:, :, L:2 * L], Rp[:, :, 0:L], rb, ALU.mult)
                        L *= 2
                    tb(W2r[:], Rp[:], Ct[:], ALU.mult)
                    tb(W2i[:], Rp[:], St[:], ALU.mult)
                    S.op('dve', lambda e: e.tensor_tensor(out=l128r[:], in0=Rp[:, :, 127], in1=Ct[:, :, 127], op=ALU.mult), reads=TK, writes=["l128"])
                    S.op('dve', lambda e: e.tensor_tensor(out=l128i[:], in0=Rp[:, :, 127], in1=St[:, :, 127], op=ALU.mult), reads=TK, writes=["l128"])
                    S.barrier()
                    ptb.close()
                    bnr = sb(pp, "bnr", [128, 32, 16], F32)
                    bni = sb(pp, "bni", [128, 32, 16], F32)
                    S.dma('sp', bnr[:], b_re_d.rearrange("(j p) c -> p j c", p=128), writes=["bnr"])
                    S.dma('sp', bni[:], b_im_d.rearrange("(j p) c -> p j c", p=128), writes=["bni"])
                    t1 = sb(pp, "t1", [128, 32, 16], F32)
                    t2 = sb(pp, "t2", [128, 32, 16], F32)
                    X4r = sb(pp, "X4r", [128, 32, 32], BF16)
                    X4i = sb(pp, "X4i", [128, 32, 32], BF16)
                    BK = ["bnr", "bni", "t1", "t2", "X4r", "X4i", WK]

                    def tbb(o, a, b_, op):
                        S.op('dve', lambda e: e.tensor_tensor(out=o, in0=a, in1=b_, op=op), reads=BK, writes=BK)

                    frb = V(9).unsqueeze(2).to_broadcast([128, 32, 16])
                    fib = V(10).unsqueeze(2).to_broadcast([128, 32, 16])
                    S.op('pool', lambda e: e.memset(X4r[:], 0.0), writes=["X4r"])
                    S.op('pool', lambda e: e.memset(X4i[:], 0.0), writes=["X4i"])
                    tbb(t1[:], bnr[:], frb, ALU.mult)
                    tbb(t2[:], bni[:], fib, ALU.mult)
                    tbb(t1[:], t1[:], t2[:], ALU.subtract)
                    for (lo, c0) in [(0, 0), (64, 16)]:
                        S.op('dve', lambda e: e.tensor_copy(out=X4r[lo:lo + 64, :, c0:c0 + 16], in_=t1[lo:lo + 64, :, :]), reads=BK, writes=BK)
                    tbb(t1[:], bni[:], frb, ALU.mult)
                    tbb(t2[:], bnr[:], fib, ALU.mult)
                    tbb(t1[:], t1[:], t2[:], ALU.add)
                    for (lo, c0) in [(0, 0), (64, 16)]:
                        S.op('dve', lambda e: e.tensor_copy(out=X4i[lo:lo + 64, :, c0:c0 + 16], in_=t1[lo:lo + 64, :, :]), reads=BK, writes=BK)
                    rowm = sb(pp, "rowm", [128, 4], F32)
                    for q in range(4):
                        S.op('dve', lambda e: e.tensor_reduce(out=rowm[:, q:q + 1], in_=identf[:, 32 * q:32 * q + 32], axis=AX.X, op=ALU.add),
                             reads=["identf"], writes=["rowm"])
                    for (X4, LTB, lk) in [(X4r, LTBr, "LTBr"), (X4i, LTBi, "LTBi")]:
                        for k in range(8):
                            S.op('pe', lambda e: e.transpose(out=pt[0][:, k * 128:(k + 1) * 128],
                                                             in_=X4[:, 4 * k:4 * k + 4, :].rearrange("p a b -> p (a b)"), identity=ident[:]),
                                 reads=BK + ["ident"], writes=[PT[0]])
                        for k in range(8):
                            for q in range(4):
                                S.op('dve', lambda e: e.tensor_scalar(out=LTB[:, 4 * k + q, :], in0=pt[0][:, k * 128:(k + 1) * 128], scalar1=rowm[:, q:q + 1], scalar2=None, op0=ALU.mult),
                                     reads=[PT[0], "rowm"], writes=[lk])
                    Xc = sb(pp, "Xc", [32, 32, 128], F32)
                    for (c_d, CL, ck, sgn) in [(c_re_d, CLr, "CLr", 1.0), (c_im_d, CLi, "CLi", -1.0)]:
                        S.op('pool', lambda e: e.memset(Xc[:], 0.0), reads=["Xc"], writes=["Xc"])
                        S.op('pool', lambda e: e.memset(CL[:], 0.0), writes=[ck])
                        cv = c_d.rearrange("(j two) c n -> two c j n", two=2)
                        S.dma('sp', Xc[0:16, :, 0:64], cv[0], reads=[], writes=["Xc"])
                        S.dma('sp', Xc[16:32, :, 64:128], cv[1], reads=[], writes=["Xc"])
                        for hb in range(2):
                            for jj in range(16):
                                j = hb * 16 + jj
                                S.op('pe', lambda e: e.transpose(out=pb[hb][:, jj * 32:(jj + 1) * 32], in_=Xc[:, j, :], identity=identf[0:32, 0:32]),
                                     reads=["Xc", "identf"], writes=[PB[hb]])
                            for q in range(4):
                                src = pb[hb][:, 0:512].rearrange("p (k q c) -> p k q c", q=4, c=32)[:, :, q, :]
                                dst = CL[:, hb * 16:(hb + 1) * 16, :].rearrange("p (k q) c -> p k q c", q=4)[:, :, q, 32 * q:32 * q + 32]
                                S.op('act', lambda e: e.activation(out=dst, in_=src, func=AF.Copy, scale=sgn), reads=[PB[hb]], writes=[ck])
                    S.barrier()
                    chk("C1")

                with ExitStack() as pm:
                    Rm = [sb(pm, "Rm%d" % i, [128, NT], F32) for i in range(1)]
                    bre = sb(pm, "bre", [128, NT], F32)
                    bim = sb(pm, "bim", [128, NT], F32)
                    ta = sb(pm, "ta", [128, NT], F32)
                    tb_ = sb(pm, "tb_", [128, NT], F32)
                    gre = sb(pm, "gre", [128, NT], F32)
                    gim = sb(pm, "gim", [128, NT], F32)
                    hre = [sb(pm, "hre%d" % i, [128, NT], BF16) for i in range(1)]
                    him = [sb(pm, "him%d" % i, [128, NT], BF16) for i in range(1)]

                    def bc(tab, j, n):
                        return tab[:, j:j + 1, :].to_broadcast([128, n, 128])

                    def v3(ap_, n):
                        return ap_.rearrange("p (a b) -> p a b", b=128)

                    for k in range(8):
                        for q in range(4):
                            j = 4 * k + q
                            jb = 0
                            rk = "Rm%d" % jb
                            S.op('pool', lambda e: e.tensor_copy(out=Rm[jb][:], in_=rr[:, j:j + 1].to_broadcast([128, NT])), reads=["rr"], writes=[rk])
                            S.op('pool', lambda e: e.memset(v3(Rm[jb][:], 8)[:, :, 0:1], 0.0), writes=[rk])
                            for half in range(2):
                                sl = slice(half * 512, (half + 1) * 512)
                                S.op('pe', lambda e: e.matmul(pb[0][:, :], lhsT=LTBr[:, j, :], rhs=uT[:, k, sl], start=True, stop=True),
                                     reads=["LTBr", "uT"], writes=[PB[0]])
                                S.op('pe', lambda e: e.matmul(pb[1][:, :], lhsT=LTBi[:, j, :], rhs=uT[:, k, sl], start=True, stop=True),
                                     reads=["LTBi", "uT"], writes=[PB[1]])
                                cB = bc(Ct, j, 4)
                                sB = bc(St, j, 4)
                                S.op('dve', lambda e: e.tensor_tensor(out=v3(bre[:, sl], 4), in0=v3(pb[0][:, :], 4), in1=cB, op=ALU.mult), reads=[PB[0], "Ct"], writes=["bre"])
                                S.op('dve', lambda e: e.tensor_tensor(out=v3(ta[:, sl], 4), in0=v3(pb[1][:, :], 4), in1=sB, op=ALU.mult), reads=[PB[1], "St"], writes=["ta"])
                                S.op('dve', lambda e: e.tensor_tensor(out=v3(bim[:, sl], 4), in0=v3(pb[1][:, :], 4), in1=cB, op=ALU.mult), reads=[PB[1], "Ct"], writes=["bim"])
                                S.op('dve', lambda e: e.tensor_tensor(out=v3(tb_[:, sl], 4), in0=v3(pb[0][:, :], 4), in1=sB, op=ALU.mult), reads=[PB[0], "St"], writes=["tb_"])
                            S.op('pool', lambda e: e.tensor_tensor(out=bre[:], in0=bre[:], in1=ta[:], op=ALU.add), reads=["bre", "ta"], writes=["bre"])
                            S.op('pool', lambda e: e.tensor_tensor(out=bim[:], in0=bim[:], in1=tb_[:], op=ALU.subtract), reads=["bim", "tb_"], writes=["bim"])
                            rmb = Rm[jb][:, :]
                            S.op('dve', lambda e: e.tensor_tensor_scan(out=gre[:], data0=rmb, data1=bre[:], initial=0.0, op0=ALU.mult, op1=ALU.add),
                                 reads=[rk, "bre"], writes=["gre"])
                            S.op('dve', lambda e: e.tensor_tensor_scan(out=gim[:], data0=rmb, data1=bim[:], initial=0.0, op0=ALU.mult, op1=ALU.add),
                                 reads=[rk, "bim"], writes=["gim"])
                            c8 = bc(Ct, j, TPC)
                            s8 = bc(St, j, TPC)
                            S.op('pool', lambda e: e.tensor_tensor(out=v3(ta[:], 8), in0=v3(gre[:], 8), in1=c8, op=ALU.mult), reads=["gre", "Ct"], writes=["ta"])
                            S.op('dve', lambda e: e.tensor_tensor(out=v3(tb_[:], 8), in0=v3(gim[:], 8), in1=s8, op=ALU.mult), reads=["gim", "St"], writes=["tb_"])
                            S.op('pool', lambda e: e.tensor_tensor(out=ta[:], in0=ta[:], in1=tb_[:], op=ALU.subtract), reads=["ta", "tb_"], writes=["ta"])
                            S.op('act', lambda e: e.activation(out=hre[jb][:], in_=ta[:], func=AF.Copy), reads=["ta"], writes=["hre%d" % jb])
                            S.op('act', lambda e: e.activation(out=Eloc[:, :, j, 0], in_=v3(ta[:], 8)[:, :, 127], func=AF.Copy), reads=["ta"], writes=["Eloc"])
                            S.op('pool', lambda e: e.tensor_tensor(out=v3(bre[:], 8), in0=v3(gim[:], 8), in1=c8, op=ALU.mult), reads=["gim", "Ct"], writes=["bre"])
                            S.op('dve', lambda e: e.tensor_tensor(out=v3(bim[:], 8), in0=v3(gre[:], 8), in1=s8, op=ALU.mult), reads=["gre", "St"], writes=["bim"])
                            S.op('pool', lambda e: e.tensor_tensor(out=bre[:], in0=bre[:], in1=bim[:], op=ALU.add), reads=["bre", "bim"], writes=["bre"])
                            S.op('act', lambda e: e.activation(out=him[jb][:], in_=bre[:], func=AF.Copy), reads=["bre"], writes=["him%d" % jb])
                            S.op('act', lambda e: e.activation(out=Eloc[:, :, j, 1], in_=v3(bre[:], 8)[:, :, 127], func=AF.Copy), reads=["bre"], writes=["Eloc"])
                            for half in range(2):
                                sl = slice(half * 512, (half + 1) * 512)
                                S.op('pe', lambda e: e.matmul(pb[2 + half][:, :], lhsT=CLr[:, j, :], rhs=hre[jb][:, sl], start=(q == 0), stop=False),
                                     reads=["CLr", "hre%d" % jb], writes=[PB[2 + half]])
                                S.op('pe', lambda e: e.matmul(pb[2 + half][:, :], lhsT=CLi[:, j, :], rhs=him[jb][:, sl], start=False, stop=(q == 3)),
                                     reads=["CLi", "him%d" % jb], writes=[PB[2 + half]])
                        for half in range(2):
                            sl = slice(half * 512, (half + 1) * 512)
                            copy_evac(ymain[:, k, sl], pb[2 + half][:, :], [PB[2 + half]], ["ymain"])
                    S.dma('sp', e_loc, Eloc[:].rearrange("p m j c -> p (m j c)"), reads=["Eloc"], writes=["e_loc"])
                    S.wait_keys('pool', ["e_loc"])
                    nc.gpsimd.collective_compute("AllGather", ALU.bypass, replica_groups=[list(range(NCORE))],
                                                 ins=[e_loc], outs=[e_gat]).then_inc(cc_sem, 1)
                    ncc[0] += 1
                    if "ymain" in dbg:
                        S.dma('sp', dbg_tensor("ymain", [128, 8 * NT], BF16), ymain[:].rearrange("p a b -> p (a b)"), reads=["ymain"], writes=["dbg_ymain"])
                    S.barrier()
                    chk("C2")

                ptab.close()
                with ExitStack() as pc:
                    nc.sync.wait_ge(cc_sem, ncc[0])
                    nc.gpsimd.wait_ge(cc_sem, ncc[0])
                    cre = sb(pc, "cre", [128, TPC, 32], F32)
                    cim = sb(pc, "cim", [128, TPC, 32], F32)
                    pcc = pc.enter_context(ExitStack())
                    Eall = sb(pcc, "Eall", [128, NCORE, TPC, 32, 2], F32)
                    S.dma('sp', Eall[:].rearrange("p r m j c -> p r (m j c)"), e_gat.rearrange("(r p) f -> p r f", p=128), writes=["Eall"])
                    sre = sb(pcc, "sre", [128, 32], F32)
                    sim = sb(pcc, "sim", [128, 32], F32)
                    u1 = sb(pcc, "u1", [128, 32], F32)
                    u2 = sb(pcc, "u2", [128, 32], F32)
                    CK = ["sre", "sim", "cre", "cim", "u1", "u2"]

                    def cop(fn, extra=()):
                        S.op('dve', fn, reads=CK + list(extra), writes=CK)

                    cop(lambda e: e.memset(sre[:], 0.0))
                    cop(lambda e: e.memset(sim[:], 0.0))
                    cop(lambda e: e.memset(cre[:], 0.0))
                    cop(lambda e: e.memset(cim[:], 0.0))
                    for G in range(NCORE * TPC):
                        r, m = G % NCORE, G // NCORE
                        if G > 0:
                            cop(lambda e: e.scalar_tensor_tensor(out=cre[:, m, :], in0=sre[:], scalar=oh[:, r:r + 1], in1=cre[:, m, :], op0=ALU.mult, op1=ALU.add), ["oh"])
                            cop(lambda e: e.scalar_tensor_tensor(out=cim[:, m, :], in0=sim[:], scalar=oh[:, r:r + 1], in1=cim[:, m, :], op0=ALU.mult, op1=ALU.add), ["oh"])
                        if G == NCORE * TPC - 1:
                            break
                        cop(lambda e: e.tensor_tensor(out=u1[:], in0=sre[:], in1=l128r[:], op=ALU.mult), ["l128"])
                        cop(lambda e: e.tensor_tensor(out=u2[:], in0=sim[:], in1=l128i[:], op=ALU.mult), ["l128"])
                        cop(lambda e: e.tensor_tensor(out=u1[:], in0=u1[:], in1=u2[:], op=ALU.subtract))
                        cop(lambda e: e.tensor_tensor(out=u2[:], in0=sre[:], in1=l128i[:], op=ALU.mult), ["l128"])
                        cop(lambda e: e.tensor_tensor(out=sim[:], in0=sim[:], in1=l128r[:], op=ALU.mult), ["l128"])
                        cop(lambda e: e.tensor_tensor(out=sim[:], in0=sim[:], in1=u2[:], op=ALU.add))
                        cop(lambda e: e.tensor_tensor(out=sim[:], in0=sim[:], in1=Eall[:, r, m, :, 1], op=ALU.add), ["Eall"])
                        cop(lambda e: e.tensor_tensor(out=sre[:], in0=u1[:], in1=Eall[:, r, m, :, 0], op=ALU.add), ["Eall"])
                    S.barrier()
                    pcc.close()
                    Am = [sb(pc, "Am%d" % i, [128, 32, 128], BF16) for i in range(1)]
                    Bm = [sb(pc, "Bm%d" % i, [128, 32, 128], BF16) for i in range(1)]
                    t3 = sb(pc, "t3", [128, 8, 32], F32)
                    t4 = sb(pc, "t4", [128, 8, 32], F32)
                    for i in range(1):
                        S.op('pool', lambda e: e.memset(Am[i][:], 0.0), writes=["Am%d" % i])
                        S.op('pool', lambda e: e.memset(Bm[i][:], 0.0), writes=["Bm%d" % i])
                    yg = sb(pc, "yg", [128, 8, NT], F32)
                    for m in range(TPC):
                        mb = 0
                        ak, bk = "Am%d" % mb, "Bm%d" % mb
                        for q in range(4):
                            def blk(T):
                                return T[:].rearrange("p (k q) c -> p k q c", q=4)[:, :, q, 32 * q:32 * q + 32]

                            def cb_(cvec):
                                return cvec[:, m, :].rearrange("p (k q) -> p k q", q=4)[:, :, q:q + 1].to_broadcast([128, 8, 32])
                            KK = ["t3", "t4", ak, bk]
                            S.op('dve', lambda e: e.tensor_tensor(out=t3[:], in0=blk(CLr), in1=cb_(cre), op=ALU.mult), reads=CK + KK + ["CLr"], writes=KK)
                            S.op('dve', lambda e: e.tensor_tensor(out=t4[:], in0=blk(CLi), in1=cb_(cim), op=ALU.mult), reads=CK + KK + ["CLi"], writes=KK)
                            S.op('dve', lambda e: e.tensor_tensor(out=blk(Am[mb]), in0=t3[:], in1=t4[:], op=ALU.add), reads=KK, writes=KK)
                            S.op('dve', lambda e: e.tensor_tensor(out=t3[:], in0=blk(CLi), in1=cb_(cre), op=ALU.mult), reads=CK + KK + ["CLi"], writes=KK)
                            S.op('dve', lambda e: e.tensor_tensor(out=t4[:], in0=blk(CLr), in1=cb_(cim), op=ALU.mult), reads=CK + KK + ["CLr"], writes=KK)
                            S.op('dve', lambda e: e.tensor_tensor(out=blk(Bm[mb]), in0=t3[:], in1=t4[:], op=ALU.subtract), reads=KK, writes=KK)
                        for k in range(8):
                            pi = k % 4
                            for q in range(4):
                                j = 4 * k + q
                                S.op('pe', lambda e: e.matmul(pb[pi][:, 0:128], lhsT=Am[mb][:, j, :], rhs=W2r[:, j, :], start=(q == 0), stop=False),
                                     reads=[ak, "W2r"], writes=[PB[pi]])
                                S.op('pe', lambda e: e.matmul(pb[pi][:, 0:128], lhsT=Bm[mb][:, j, :], rhs=W2i[:, j, :], start=False, stop=(q == 3)),
                                     reads=[bk, "W2i"], writes=[PB[pi]])
                            tsl = slice(m * 128, (m + 1) * 128)
                            S.op('dve', lambda e: e.tensor_tensor(out=yg[:, k, tsl], in0=pb[pi][:, 0:128], in1=ymain[:, k, tsl], op=ALU.add),
                                 reads=[PB[pi], "ymain"], writes=["yg"])
                    for k in range(8):
                        S.op('dve', lambda e: e.scalar_tensor_tensor(out=yg[:, k, :], in0=uT[:, k, :], scalar=dsk[:, k:k + 1], in1=yg[:, k, :], op0=ALU.mult, op1=ALU.add),
                             reads=["uT", "dsk", "yg"], writes=["yg"])
                    if "ypre" in dbg:
                        S.dma('sp', dbg_tensor("ypre", [128, 8 * NT], F32), yg[:].rearrange("p a b -> p (a b)"), reads=["yg"], writes=["dbg_ypre"])
                    gt = sb(pc, "gt", [128, NT], F32)
                    ygb = ymain
                    for k in range(8):
                        S.op('act', lambda e: e.activation(out=gt[:], in_=yg[:, k, :], func=AF.Square), reads=["yg"], writes=["gt"])
                        S.op('dve', lambda e: e.tensor_scalar(out=gt[:], in0=gt[:], scalar1=0.044715, scalar2=1.0, op0=ALU.mult, op1=ALU.add), reads=["gt"], writes=["gt"])
                        S.op('dve', lambda e: e.tensor_tensor(out=gt[:], in0=gt[:], in1=yg[:, k, :], op=ALU.mult), reads=["gt", "yg"], writes=["gt"])
                        S.op('act', lambda e: e.activation(out=gt[:], in_=gt[:], func=AF.Sigmoid, scale=1.5957691216057308), reads=["gt"], writes=["gt"])
                        S.op('dve', lambda e: e.tensor_tensor(out=ygb[:, k, :], in0=gt[:], in1=yg[:, k, :], op=ALU.mult), reads=["gt", "yg"], writes=["ymain"])
                    S.barrier()
                    chk("C3")

                with ExitStack() as pg:
                    wa = [sb(pg, "wa%d" % i, [128, 8, 128], BF16) for i in range(2)]
                    wb = [sb(pg, "wb%d" % i, [128, 8, 128], BF16) for i in range(2)]
                    sg = sb(pg, "sg", [128, 512], F32)
                    for fc in range(8):
                        b = fc % 2
                        load_w_bf16(wa[b][:], w_glu_d[:, fc * 128:(fc + 1) * 128].rearrange("(kc p) c -> p kc c", p=128), "wa%d" % b)
                        load_w_bf16(wb[b][:], w_glu_d[:, 1024 + fc * 128:1024 + (fc + 1) * 128].rearrange("(kc p) c -> p kc c", p=128), "wb%d" % b)
                        for half in range(2):
                            sl = slice(half * 512, (half + 1) * 512)
                            pa_, pb_ = 2 * half, 2 * half + 1
                            for kc in range(8):
                                S.op('pe', lambda e: e.matmul(pb[pa_][:, :], lhsT=wa[b][:, kc, :], rhs=ymain[:, kc, sl], start=(kc == 0), stop=(kc == 7)),
                                     reads=["wa%d" % b, "ymain"], writes=[PB[pa_]])
                            for kc in range(8):
                                S.op('pe', lambda e: e.matmul(pb[pb_][:, :], lhsT=wb[b][:, kc, :], rhs=ymain[:, kc, sl], start=(kc == 0), stop=(kc == 7)),
                                     reads=["wb%d" % b, "ymain"], writes=[PB[pb_]])
                            S.op('act', lambda e: e.activation(out=sg[:], in_=pb[pb_][:, :], func=AF.Sigmoid), reads=[PB[pb_]], writes=["sg"])
                            S.op('dve', lambda e: e.tensor_tensor(out=sg[:], in0=pb[pa_][:, :], in1=sg[:], op=ALU.mult), reads=[PB[pa_], "sg"], writes=["sg"])
                            S.op('dve', lambda e: e.tensor_tensor(out=szs[:, fc, sl], in0=sg[:], in1=szs[:, fc, sl], op=ALU.mult), reads=["sg", "szs"], writes=["szs"])
                    if "yssm" in dbg:
                        S.dma('sp', dbg_tensor("yssm", [128, 8 * NT], BF16), szs[:].rearrange("p a b -> p (a b)"), reads=["szs"], writes=["dbg_yssm"])
                    S.barrier()
                    chk("C4")

            ust.close()
            with ExitStack() as ph:
                NK = NCORE * NT
                wuq = sb(ph, "wuq", [128, 4, 1024], BF16)
                wuqi = sb(ph, "wuqi", [128, 4, 1024], BF16)
                load_w_bf16(wuq[:], w_uq_d.rearrange("(kc p) c -> p kc c", p=128), "wuq")
                load_w_bf16(wuqi[:], w_uqi_d.rearrange("(kc p) c -> p kc c", p=128), "wuqi")
                kiA = sb(ph, "kiA", [128, NK], BF16)
                KTg = sb(ph, "KTg", [128, NK], BF16)
                Vg = sb(ph, "Vg", [128, 64, 130], BF16)
                Sc = sb(ph, "Sc", [128, NK], F32)
                Mb = sb(ph, "Mb", [128, NK], BF16)
                MT = sb(ph, "MT", [128, 64, 128], BF16)
                qTt = sb(ph, "qTt", [128, 8, 128], BF16)
                qiT = sb(ph, "qiT", [128, 8, 128], BF16)
                Dg = sb(ph, "Dg", [128, 16, 128], BF16)
                cbias = sb(ph, "cbias", [128, 1024], F32)
                rl = [sb(ph, "rl%d" % i, [128, 512], BF16) for i in range(3)]
                Eb = [sb(ph, "Eb%d" % i, [128, 512], BF16) for i in range(2)]
                Pb = [sb(ph, "Pb%d" % i, [128, 512], BF16) for i in range(2)]
                bs = sb(ph, "bs", [128, 16], F32)
                bcnt = sb(ph, "bcnt", [128, 1], F32)
                bcn2 = sb(ph, "bcn2", [128, 1], F32)
                bnm = sb(ph, "bnm", [128, 1], F32)
                lomin = sb(ph, "lomin", [128, 16], F32)
                osb = sb(ph, "osb", [128, 128], BF16)
                rec = sb(ph, "rec", [128, 1], F32)
                nc.sync.wait_ge(cc_sem, ncc[0])
                for half in range(2):
                    for r in range(NCORE):
                        S.dma('sp', kiA[half * 64:(half + 1) * 64, :].rearrange("p (m r t) -> p m r t", r=NCORE, t=128)[:, :, r, :],
                              ki_gat[r * 64:(r + 1) * 64, :].rearrange("p (m t) -> p m t", t=128), writes=["kiA"])
                S.op('pool', lambda e: e.memset(Vg[:, :, 128:130], 1.0), writes=["Vg"])

                for m in DTILES:
                    nkb = 8 * (m + 1)
                    nk = nkb * 128
                    tsl = slice(m * 128, (m + 1) * 128)
                    chk("D0")
                    for (wmat, wk_, dst, dk) in [(wuq, "wuq", qTt, "qTt"), (wuqi, "wuqi", qiT, "qiT")]:
                        for hb in range(2):
                            for c in range(4):
                                fc = hb * 4 + c
                                for kc in range(4):
                                    S.op('pe', lambda e: e.matmul(pb[hb][:, c * 128:(c + 1) * 128], lhsT=wmat[:, kc, fc * 128:(fc + 1) * 128], rhs=cqn[:, kc, tsl],
                                                                  start=(kc == 0), stop=(kc == 3)), reads=[wk_, "cqn"], writes=[PB[hb]])
                            copy_evac(dst[:, hb * 4:(hb + 1) * 4, :].rearrange("p a b -> p (a b)"), pb[hb][:, :], [PB[hb]], [dk])
                    for h in range(16):
                        S.op('dve', lambda e: e.tensor_scalar(out=Dg[:, h, :], in0=ident[:], scalar1=wq[:, m, h:h + 1], scalar2=None, op0=ALU.mult),
                             reads=["ident", "wq"], writes=["Dg"])
                    S.op('pool', lambda e: e.iota(cbias[:], pattern=[[1, 1024]], base=(nkb - 8) * 128, channel_multiplier=0, allow_small_or_imprecise_dtypes=True),
                         writes=["cbias"])
                    S.op('dve', lambda e: e.tensor_scalar(out=cbias[:], in0=cbias[:], scalar1=qpos[:, m:m + 1], scalar2=None, op0=ALU.is_gt),
                         reads=["cbias", "qpos"], writes=["cbias"])
                    S.op('dve', lambda e: e.tensor_scalar(out=cbias[:], in0=cbias[:], scalar1=-1e30, scalar2=None, op0=ALU.mult), reads=["cbias"], writes=["cbias"])
                    nch = nk // 512
                    for ch in range(nch):
                        ksl = slice(ch * 512, (ch + 1) * 512)
                        sp = 2 + ch % 2
                        for h in range(16):
                            rp = h % 2
                            po = (h % 2) * 64
                            rb_ = h % 3
                            S.op('pe', lambda e: e.matmul(pb[rp][:, :], lhsT=qiT[po:po + 64, h // 2, :], rhs=kiA[po:po + 64, ksl], start=True, stop=True),
                                 reads=["qiT", "kiA"], writes=[PB[rp]])
                            S.op('act', lambda e: e.activation(out=rl[rb_][:], in_=pb[rp][:, :], func=AF.Relu), reads=[PB[rp]], writes=["rl%d" % rb_])
                            S.op('pe', lambda e: e.matmul(pb[sp][:, :], lhsT=Dg[:, h, :], rhs=rl[rb_][:], start=(h == 0), stop=(h == 15)),
                                 reads=["Dg", "rl%d" % rb_], writes=[PB[sp]])
                        S.op('dve', lambda e: e.tensor_reduce(out=lomin[:, ch:ch + 1], in_=pb[sp][:, :], axis=AX.X, op=ALU.min), reads=[PB[sp]], writes=["lomin"])
                        if ch >= nch - 2:
                            off = (ch - (nch - 2)) * 512
                            S.op('dve', lambda e: e.tensor_tensor(out=Sc[:, ksl], in0=pb[sp][:, :], in1=cbias[:, off:off + 512], op=ALU.add),
                                 reads=[PB[sp], "cbias"], writes=["Sc"])
                        else:
                            S.op('act', lambda e: e.activation(out=Sc[:, ksl], in_=pb[sp][:, :], func=AF.Copy), reads=[PB[sp]], writes=["Sc"])
                    chk("D2")
                    BS = "bs"
                    LO, HI, MID, PR, T1, T2 = [bs[:, i:i + 1] for i in range(6)]
                    CNT = bcnt[:, 0:1]
                    CN2 = bcn2[:, 0:1]
                    NM = bnm[:, 0:1]

                    def bop(fn, extra=(), wr=()):
                        S.op('dve', fn, reads=[BS] + list(extra), writes=[BS] + list(wr))

                    bop(lambda e: e.tensor_reduce(out=LO, in_=lomin[:, 0:nch], axis=AX.X, op=ALU.min), ["lomin"])
                    bop(lambda e: e.tensor_reduce(out=HI, in_=Sc[:, 0:nk], axis=AX.X, op=ALU.max), ["Sc"])
                    ndve = (nk * 7 // 16) // 128 * 128
                    for it in range(NBIS):
                        bop(lambda e: e.tensor_tensor(out=MID, in0=LO, in1=HI, op=ALU.add))
                        bop(lambda e: e.tensor_scalar(out=MID, in0=MID, scalar1=0.5, scalar2=None, op0=ALU.mult))
                        bop(lambda e: e.tensor_scalar(out=NM, in0=MID, scalar1=-1.0, scalar2=None, op0=ALU.mult), ["bnm"], ["bnm"])
                        S.op('act', lambda e: e.activation(out=Mb[:, ndve:nk], in_=Sc[:, ndve:nk], func=AF.Sign, bias=NM, scale=1.0, accum_out=CN2),
                             reads=["Sc", "bnm"], writes=["Mb_hi", "bcn2"])
                        S.op('dve', lambda e: e.tensor_scalar(out=Mb[:, 0:ndve], in0=Sc[:, 0:ndve], scalar1=MID, scalar2=0.0, op0=ALU.is_ge, op1=ALU.add, accum_out=CNT),
                             reads=["Sc", BS], writes=["Mb_lo", "bcnt"])
                        bop(lambda e: e.tensor_scalar(out=T1, in0=CN2, scalar1=float(nk - ndve), scalar2=0.5, op0=ALU.add, op1=ALU.mult), ["bcn2"])
                        bop(lambda e: e.tensor_tensor(out=T1, in0=T1, in1=CNT, op=ALU.add), ["bcnt"])
                        bop(lambda e: e.tensor_scalar(out=PR, in0=T1, scalar1=float(TOPK) - 0.25, scalar2=None, op0=ALU.is_ge))
                        bop(lambda e: e.tensor_tensor(out=T1, in0=MID, in1=LO, op=ALU.subtract))
                        bop(lambda e: e.tensor_tensor(out=T2, in0=HI, in1=MID, op=ALU.subtract))
                        bop(lambda e: e.scalar_tensor_tensor(out=LO, in0=T1, scalar=PR, in1=LO, op0=ALU.mult, op1=ALU.add))
                        bop(lambda e: e.scalar_tensor_tensor(out=HI, in0=T2, scalar=PR, in1=MID, op0=ALU.mult, op1=ALU.add))
                    S.op('dve', lambda e: e.tensor_scalar(out=Mb[:, 0:nk], in0=Sc[:, 0:nk], scalar1=LO, scalar2=None, op0=ALU.is_ge),
                         reads=["Sc", BS], writes=["Mb_lo", "Mb_hi"])
                    if "mask" in dbg and m == dbg_m:
                        S.dma('sp', dbg_tensor("mask", [128, NK], BF16)[:, 0:nk], Mb[:, 0:nk], reads=["Mb_lo", "Mb_hi"], writes=["dbg_mask"])
                        S.dma('sp', dbg_tensor("score", [128, NK], F32)[:, 0:nk], Sc[:, 0:nk], reads=["Sc"], writes=["dbg_score"])
                    chk("D3")
                    for g8 in range(nkb // 8):
                        pti = g8 % 2
                        for c in range(8):
                            kb = g8 * 8 + c
                            S.op('pe', lambda e: e.transpose(out=pt[pti][:, c * 128:(c + 1) * 128], in_=Mb[:, kb * 128:(kb + 1) * 128], identity=ident[:]),
                                 reads=["Mb_lo", "Mb_hi", "ident"], writes=[PT[pti]])
                        copy_evac(MT[:, g8 * 8:(g8 + 1) * 8, :].rearrange("p a b -> p (a b)"), pt[pti][:, :], [PT[pti]], ["MT"])
                    chk("D4")
                    S.op('dve', lambda e: e.tensor_scalar(out=MT[:, 0:nkb, :], in0=MT[:, 0:nkb, :], scalar1=-1.0, scalar2=30000.0, op0=ALU.add, op1=ALU.mult),
                         reads=["MT"], writes=["MT"])
                    for g in range(2):
                        for r in range(NCORE):
                            S.dma('sp', KTg[:, 0:nk].rearrange("p (m r t) -> p m r t", r=NCORE, t=128)[:, :, r, :],
                                  kT_gat[r * 256 + g * 128:r * 256 + (g + 1) * 128, 0:(m + 1) * 128].rearrange("p (m t) -> p m t", t=128),
                                  writes=["KTg"])
                            S.dma('sp', Vg[:, 0:nkb, 0:128].rearrange("p (m r) d -> p m r d", r=NCORE)[:, :, r, :],
                                  v_gat[r * NT:r * NT + (m + 1) * 128, g * 128:(g + 1) * 128].rearrange("(m p) d -> p m d", p=128),
                                  writes=["Vg"])
                        for kb in range(nkb):
                            lp = 4 + kb % 2
                            eb = kb % 2
                            S.op('pe', lambda e: e.matmul(pb[lp][:, :], lhsT=KTg[:, kb * 128:(kb + 1) * 128], rhs=qTt[:, 4 * g:4 * g + 4, :].rearrange("p a b -> p (a b)"),
                                                          start=True, stop=False), reads=["KTg", "qTt"], writes=[PB[lp]])
                            S.op('pe', lambda e: e.matmul(pb[lp][:, :].rearrange("p (a b) -> p a b", b=128), lhsT=ident[:], rhs=MT[:, kb:kb + 1, :].to_broadcast([128, 4, 128]),
                                                          start=False, stop=True), reads=["ident", "MT"], writes=[PB[lp]])
                            S.op('act', lambda e: e.activation(out=Pb[eb][:], in_=pb[lp][:, :], func=AF.Exp, scale=128.0 ** -0.5), reads=[PB[lp]], writes=["Pb%d" % eb])
                            for hh in range(4):
                                S.op('pe', lambda e: e.matmul(pb[hh][:, 0:129], lhsT=Pb[eb][:, hh * 128:(hh + 1) * 128], rhs=Vg[:, kb, 0:129],
                                                              start=(kb == 0), stop=(kb == nkb - 1)), reads=["Pb%d" % eb, "Vg"], writes=[PB[hh]])
                        for hh in range(4):
                            hd = 4 * g + hh
                            S.op('dve', lambda e: e.reciprocal(out=rec[:], in_=pb[hh][:, 128:129]), reads=[PB[hh]], writes=["rec"])
                            S.op('dve', lambda e: e.tensor_scalar(out=osb[:], in0=pb[hh][:, 0:128], scalar1=rec[:, 0:1], scalar2=None, op0=ALU.mult),
                                 reads=[PB[hh], "rec"], writes=["osb"])
                            S.op('pe', lambda e: e.transpose(out=pt[0][:, 0:128], in_=osb[:], identity=ident[:]), reads=["osb", "ident"], writes=[PT[0]])
                            S.op('dve', lambda e: e.tensor_tensor(out=sza[:, hd, tsl], in0=pt[0][:, 0:128], in1=sza[:, hd, tsl], op=ALU.mult),
                                 reads=[PT[0], "sza"], writes=["sza"])
                if "yattn" in dbg:
                    S.dma('sp', dbg_tensor("yattn", [128, 8 * NT], BF16), sza[:].rearrange("p a b -> p (a b)"), reads=["sza"], writes=["dbg_yattn"])
                S.barrier()
                chk("D")

            with ExitStack() as ph:
                mts = ph.enter_context(ExitStack())
                mT = sb(mts, "mT", [128, 16, NT], BF16)
                with ExitStack() as p1:
                    hT = sb(p1, "hT2", [128, 16, NT], BF16)
                    with ExitStack() as pa:
                        norm_transpose(pa, "A2", x_tile_loader(pa, "A2", x_d, D), gm, hT, "hT2", "gm")
                        S.barrier()
                    wso = [sb(p1, "wso%d" % i, [128, 8, 128], BF16) for i in range(2)]
                    wao = [sb(p1, "wao%d" % i, [128, 8, 128], BF16) for i in range(2)]
                    wgs = [sb(p1, "wgs%d" % i, [128, 16, 128], BF16) for i in range(2)]
                    wga = [sb(p1, "wga%d" % i, [128, 16, 128], BF16) for i in range(2)]
                    s1 = sb(p1, "s1", [128, 512], F32)
                    s2 = sb(p1, "s2", [128, 512], F32)
                    for fc in range(16):
                        b = fc % 2
                        cs = slice(fc * 128, (fc + 1) * 128)
                        load_w_bf16(wso[b][:], w_so_d[:, cs].rearrange("(kc p) c -> p kc c", p=128), "wso%d" % b)
                        load_w_bf16(wao[b][:], w_ao_d[:, cs].rearrange("(kc p) c -> p kc c", p=128), "wao%d" % b)
                        load_w_bf16(wgs[b][:], w_in_d[:, 4176 + fc * 128:4176 + (fc + 1) * 128].rearrange("(kc p) c -> p kc c", p=128), "wgs%d" % b)
                        load_w_bf16(wga[b][:], w_in_d[:, 6224 + fc * 128:6224 + (fc + 1) * 128].rearrange("(kc p) c -> p kc c", p=128), "wga%d" % b)
                        for half in range(2):
                            sl = slice(half * 512, (half + 1) * 512)
                            for kc in range(8):
                                S.op('pe', lambda e: e.matmul(pb[0][:, :], lhsT=wso[b][:, kc, :], rhs=szs[:, kc, sl], start=(kc == 0), stop=(kc == 7)),
                                     reads=["wso%d" % b, "szs"], writes=[PB[0]])
                            for kc in range(8):
                                S.op('pe', lambda e: e.matmul(pb[1][:, :], lhsT=wao[b][:, kc, :], rhs=sza[:, kc, sl], start=(kc == 0), stop=(kc == 7)),
                                     reads=["wao%d" % b, "sza"], writes=[PB[1]])
                            for kc in range(16):
                                S.op('pe', lambda e: e.matmul(pb[2][:, :], lhsT=wgs[b][:, kc, :], rhs=hT[:, kc, sl], start=(kc == 0), stop=(kc == 15)),
                                     reads=["wgs%d" % b, "hT2"], writes=[PB[2]])
                            for kc in range(16):
                                S.op('pe', lambda e: e.matmul(pb[3][:, :], lhsT=wga[b][:, kc, :], rhs=hT[:, kc, sl], start=(kc == 0), stop=(kc == 15)),
                                     reads=["wga%d" % b, "hT2"], writes=[PB[3]])
                            S.op('act', lambda e: e.activation(out=s1[:], in_=pb[2][:, :], func=AF.Sigmoid), reads=[PB[2]], writes=["s1"])
                            S.op('act', lambda e: e.activation(out=s2[:], in_=pb[3][:, :], func=AF.Sigmoid), reads=[PB[3]], writes=["s2"])
                            S.op('dve', lambda e: e.tensor_tensor(out=s1[:], in0=pb[0][:, :], in1=s1[:], op=ALU.mult), reads=[PB[0], "s1"], writes=["s1"])
                            S.op('dve', lambda e: e.tensor_tensor(out=s2[:], in0=pb[1][:, :], in1=s2[:], op=ALU.mult), reads=[PB[1], "s2"], writes=["s2"])
                            S.op('pool', lambda e: e.tensor_tensor(out=mT[:, fc, sl], in0=s1[:], in1=s2[:], op=ALU.add), reads=["s1", "s2"], writes=["mT"])
                    if "merged" in dbg:
                        S.dma('sp', dbg_tensor("merged", [128, 16 * NT], BF16), mT[:].rearrange("p a b -> p (a b)"), reads=["mT"], writes=["dbg_merged"])
                    S.barrier()
                pers.close()

                x2 = sb(ph, "x2", [128, TPC, D], F32)
                with ExitStack() as p2:
                    wos = [sb(p2, "wos%d" % i, [128, 16, 512], BF16) for i in range(2)]
                    xq = [sb(p2, "xq%d" % i, [128, 512], F32) for i in range(4)]
                    for fb in range(4):
                        b = fb % 2
                        fs = slice(fb * 512, (fb + 1) * 512)
                        load_w_bf16(wos[b][:], w_o_d[:, fs].rearrange("(kc p) c -> p kc c", p=128), "wos%d" % b)
                        for m in range(TPC):
                            pi = m % 4
                            for kc in range(16):
                                S.op('pe', lambda e: e.matmul(pb[pi][:, :], lhsT=mT[:, kc, m * 128:(m + 1) * 128], rhs=wos[b][:, kc, :], start=(kc == 0), stop=(kc == 15)),
                                     reads=["mT", "wos%d" % b], writes=[PB[pi]])
                            S.dma('sp', xq[pi][:], x_d[m * 128:(m + 1) * 128, fs], writes=["xq%d" % pi])
                            S.op('dve', lambda e: e.tensor_tensor(out=x2[:, m, fs], in0=pb[pi][:, :], in1=xq[pi][:], op=ALU.add),
                                 reads=[PB[pi], "xq%d" % pi], writes=["x2"])
                    if "x2" in dbg:
                        S.dma('sp', dbg_tensor("x2", [128, TPC * D], F32), x2[:].rearrange("p a b -> p (a b)"), reads=["x2"], writes=["dbg_x2"])
                    S.barrier()

                mts.close()
                gate = sb(ph, "gate", [128, TPC, D], BF16)
                with ExitStack() as p3:
                    xnT = sb(p3, "xnT", [128, 16, NT], BF16)
                    with ExitStack() as pa:
                        def get_x2(m):
                            return x2[:, m, :], "x2"
                        norm_transpose(pa, "A3", get_x2, gpl, xnT, "xnT", "gpl")
                        S.barrier()
                    wps = [sb(p3, "wps%d" % i, [128, 16, 512], BF16) for i in range(2)]
                    for fb in range(4):
                        b = fb % 2
                        fs = slice(fb * 512, (fb + 1) * 512)
                        load_w_bf16(wps[b][:], w_pg_d[:, fs].rearrange("(kc p) c -> p kc c", p=128), "wps%d" % b)
                        for m in range(TPC):
                            pi = m % 4
                            for kc in range(16):
                                S.op('pe', lambda e: e.matmul(pb[pi][:, :], lhsT=xnT[:, kc, m * 128:(m + 1) * 128], rhs=wps[b][:, kc, :], start=(kc == 0), stop=(kc == 15)),
                                     reads=["xnT", "wps%d" % b], writes=[PB[pi]])
                            S.op('act', lambda e: e.activation(out=gate[:, m, fs], in_=pb[pi][:, :], func=AF.Sigmoid), reads=[PB[pi]], writes=["gate"])
                    S.barrier()

                with ExitStack() as p4:
                    wpl = sb(p4, "wpl", [128, 2, D], BF16)
                    load_w_bf16(wpl[:], w_ple_d.rearrange("(kc p) c -> p kc c", p=128), "wpl")
                    gpp = sb(p4, "gpp", [128, D], F32)
                    gfin = sb(p4, "gfin", [128, D], F32)
                    grow = sb(p4, "grow", [1, 2 * D], F32)
                    onesf = sb(p4, "onesf", [1, 128], F32)
                    S.op('dve', lambda e: e.memset(onesf[:], 1.0), writes=["onesf"])
                    S.dma('sp', grow[0:1, 0:D], g_pp_d, writes=["grow"])
                    S.dma('sp', grow[0:1, D:2 * D], g_fin_d, writes=["grow"])
                    for gi, (gdst, gk) in enumerate([(gpp, "gpp"), (gfin, "gfin")]):
                        for fb in range(4):
                            S.op('pe', lambda e: e.matmul(pb[fb][:, :], lhsT=onesf[0:1, :], rhs=grow[0:1, gi * D + fb * 512:gi * D + (fb + 1) * 512], start=True, stop=True),
                                 reads=["onesf", "grow"], writes=[PB[fb]])
                            S.op('dve', lambda e: e.tensor_copy(out=gdst[:, fb * 512:(fb + 1) * 512], in_=pb[fb][:, :]), reads=[PB[fb]], writes=[gk])
                    pf = [sb(p4, "pf%d" % i, [128, 256], F32) for i in range(2)]
                    pbf = sb(p4, "pbf", [128, 256], BF16)
                    pT = sb(p4, "pT", [128, 2, 128], BF16)
                    er = sb(p4, "er", [128, D], F32)
                    ej = sb(p4, "ej", [128, D], BF16)
                    ob = [sb(p4, "ob%d" % i, [128, D], F32) for i in range(2)]
                    st4 = sb(p4, "st4", [128, 2 * TPC], F32)
                    for m in range(TPC):
                        b = m % 2
                        S.dma('sp', pf[b][:], p_d[m * 128:(m + 1) * 128, :], writes=["pf%d" % b])
                        S.op('dve', lambda e: e.tensor_copy(out=pbf[:], in_=pf[b][:]), reads=["pf%d" % b], writes=["pbf"])
                        for c in range(2):
                            S.op('pe', lambda e: e.transpose(out=pt[0][:, c * 128:(c + 1) * 128], in_=pbf[:, c * 128:(c + 1) * 128], identity=ident[:]),
                                 reads=["pbf", "ident"], writes=[PT[0]])
                        S.op('act', lambda e: e.activation(out=pT[:].rearrange("p a b -> p (a b)"), in_=pt[0][:, 0:256], func=AF.Copy), reads=[PT[0]], writes=["pT"])
                        for fb in range(4):
                            for kc in range(2):
                                S.op('pe', lambda e: e.matmul(pb[fb][:, :], lhsT=pT[:, kc, :], rhs=wpl[:, kc, fb * 512:(fb + 1) * 512], start=(kc == 0), stop=(kc == 1)),
                                     reads=["pT", "wpl"], writes=[PB[fb]])
                        for fb in range(4):
                            copy_evac(er[:, fb * 512:(fb + 1) * 512], pb[fb][:, :], [PB[fb]], ["er"])
                        S.op('act', lambda e: e.activation(out=ej[:], in_=er[:], func=AF.Square, accum_out=st4[:, 2 * m:2 * m + 1]), reads=["er"], writes=["ej", "st4"])
                        rstd_cols(st4[:, 2 * m:2 * m + 1], "st4", 1.0 / D)
                        S.op('dve', lambda e: e.scalar_tensor_tensor(out=er[:], in0=er[:], scalar=st4[:, 2 * m:2 * m + 1], in1=gpp[:], op0=ALU.mult, op1=ALU.mult),
                             reads=["er", "st4", "gpp"], writes=["er"])
                        S.op('pool', lambda e: e.tensor_tensor(out=er[:], in0=er[:], in1=gate[:, m, :], op=ALU.mult), reads=["er", "gate"], writes=["er"])
                        S.op('dve', lambda e: e.tensor_tensor(out=er[:], in0=er[:], in1=x2[:, m, :], op=ALU.add), reads=["er", "x2"], writes=["er"])
                        S.op('act', lambda e: e.activation(out=ej[:], in_=er[:], func=AF.Square, accum_out=st4[:, 2 * m + 1:2 * m + 2]), reads=["er"], writes=["ej", "st4"])
                        rstd_cols(st4[:, 2 * m + 1:2 * m + 2], "st4", 1.0 / D)
                        S.op('dve', lambda e: e.scalar_tensor_tensor(out=ob[b][:], in0=er[:], scalar=st4[:, 2 * m + 1:2 * m + 2], in1=gfin[:], op0=ALU.mult, op1=ALU.mult),
                             reads=["er", "st4", "gfin"], writes=["ob%d" % b])
                        S.dma('sp', y_d[m * 128:(m + 1) * 128, :], ob[b][:], reads=["ob%d" % b], writes=["y"])
                    S.barrier()

        except _Stop:
            pass
        S.barrier()
    return nc, dbg_out, S


dbg_m = 7
import os as _os
DTILES = [int(v) for v in _os.environ.get("DTILES", "0,1,2,3,4,5,6,7").split(",") if v != ""]
_CACHE = {}


def _shard_inputs(inputs):
    f = lambda a: np.ascontiguousarray(np.asarray(a, dtype=np.float32))
    x = f(inputs["x"])[0].reshape(NCORE * TPC, 128, D)
    p = f(inputs["p"])[0, 0].reshape(NCORE * TPC, 128, 256)
    common = {
        "g_mix": f(inputs["g_mix"]).reshape(1, D), "w_in": f(inputs["w_in"])[0],
        "g_q": f(inputs["g_q"]).reshape(1, 512), "w_uq": f(inputs["w_uq"])[0], "w_uq_idx": f(inputs["w_uq_idx"])[0],
        "g_kidx": f(inputs["g_kidx"]).reshape(1, 64),
        "a_re": f(inputs["a_re"])[0].reshape(32, 128), "a_im": f(inputs["a_im"])[0].reshape(32, 128),
        "log_dt": f(inputs["log_dt"])[0].reshape(32, 2),
        "b_re": f(inputs["b_re"])[0].reshape(4096, 16), "b_im": f(inputs["b_im"])[0].reshape(4096, 16),
        "c_re": f(inputs["c_re"])[0], "c_im": f(inputs["c_im"])[0],
        "d_skip": f(inputs["d_skip"]).reshape(1, 1024),
        "w_glu": f(inputs["w_glu"])[0], "w_ssm_out": f(inputs["w_ssm_out"])[0], "w_attn_out": f(inputs["w_attn_out"])[0],
        "w_o": f(inputs["w_o"])[0], "g_ple": f(inputs["g_ple"]).reshape(1, D), "w_ple_gate": f(inputs["w_ple_gate"])[0],
        "w_ple": f(inputs["w_ple"])[0], "g_ple_post": f(inputs["g_ple_post"]).reshape(1, D), "g_final": f(inputs["g_final"]).reshape(1, D),
    }
    maps = []
    for i in range(NCORE):
        mp = dict(common)
        mp["x"] = np.ascontiguousarray(x[i::NCORE].reshape(NT, D))
        mp["p"] = np.ascontiguousarray(p[i::NCORE].reshape(NT, 256))
        qp = np.zeros((128, TPC), np.float32)
        for m in range(TPC):
            qp[:, m] = (NCORE * m + i) * 128 + np.arange(128)
        mp["qpos"] = qp
        ohm = np.zeros((128, NCORE), np.float32)
        ohm[:, i] = 1.0
        mp["onehot"] = ohm
        maps.append(mp)
    return maps


def _assemble(res, name="y", width=D):
    out = np.zeros((NCORE * TPC, 128, width), np.float32)
    for i in range(NCORE):
        out[i::NCORE] = np.asarray(res.results[i][name]).reshape(TPC, 128, width)
    return out.reshape(1, NCORE * NT, width)


def kernel(**inputs):
    if "nc" not in _CACHE:
        _CACHE["nc"] = build_nc()[0]
    nc = _CACHE["nc"]
    maps = _shard_inputs(inputs)
    res = run_bass_kernel_spmd(nc, maps, core_ids=list(range(NCORE)))
    return _assemble(res)
```
